# Optimizing a Trainium2 kernel written in Bass

```python
import math
import jax
import jax.numpy as jnp
from jax import lax
import numpy as np

D_MODEL = 2048
BATCH = 1
SEQ = 16384
DEPTH = 2

PLE_DIM = 256
D_FF = 5632
D_SSM = 1024
SSM_GROUP = 16
N_GROUPS = D_SSM // SSM_GROUP
STATE = 64
N_HEADS = 8
HEAD_DIM = 64
QK_WIDTH = N_HEADS * 2 * HEAD_DIM
V_WIDTH = N_HEADS * 2 * HEAD_DIM
IN_WIDTH = D_SSM + 2 * QK_WIDTH + V_WIDTH + 2 * D_MODEL
ROT_DIM = HEAD_DIM // 4
ROPE_THETA = 500000.0
Q_BLOCK = 128
LN_EPS = 1e-5
RMS_EPS = 1e-5
NEG_INF = -1e30
ALPHA = (2 * DEPTH) ** 0.25
BETA = (8 * DEPTH) ** -0.25
DT_MIN = 0.001
DT_MAX = 0.1

kernel_name = "hybrid_s5_diffattn_macaron_deepnorm"


def _layer_norm(x, g, b):
    xf = x.astype(jnp.float32)
    mu = jnp.mean(xf, axis=-1, keepdims=True)
    xc = xf - mu
    var = jnp.mean(xc * xc, axis=-1, keepdims=True)
    y = xc * lax.rsqrt(var + LN_EPS) * g.astype(jnp.float32) + b.astype(jnp.float32)
    return y.astype(x.dtype)


def _swiglu(x, w_gate, w_up, w_down):
    return (jax.nn.silu(x @ w_gate) * (x @ w_up)) @ w_down


def _rope(t, positions):
    half = ROT_DIM // 2
    inv_freq = ROPE_THETA ** (-jnp.arange(0, ROT_DIM, 2, dtype=jnp.float32) / ROT_DIM)
    ang = positions.astype(jnp.float32)[:, :, None] * inv_freq
    cos = jnp.cos(ang)[:, :, None, None, :].astype(t.dtype)
    sin = jnp.sin(ang)[:, :, None, None, :].astype(t.dtype)
    t1 = t[..., :half]
    t2 = t[..., half:ROT_DIM]
    return jnp.concatenate([t1 * cos - t2 * sin, t2 * cos + t1 * sin, t[..., ROT_DIM:]], axis=-1)


def _s5_group(args):
    u, a_re, a_im, b_re, b_im, c_re, c_im, log_dt = args
    f32 = jnp.float32
    u = u.astype(f32)
    a_re, a_im = a_re.astype(f32), a_im.astype(f32)
    b_re, b_im = b_re.astype(f32), b_im.astype(f32)
    c_re, c_im = c_re.astype(f32), c_im.astype(f32)
    dt = jnp.exp(log_dt.astype(f32))
    mag = jnp.exp(a_re * dt)
    lb_re = mag * jnp.cos(a_im * dt)
    lb_im = mag * jnp.sin(a_im * dt)
    nr, ni = lb_re - 1.0, lb_im
    den = a_re * a_re + a_im * a_im
    coef_re = (nr * a_re + ni * a_im) / den
    coef_im = (ni * a_re - nr * a_im) / den
    bb_re = coef_re[:, None] * b_re - coef_im[:, None] * b_im
    bb_im = coef_re[:, None] * b_im + coef_im[:, None] * b_re
    bu_re = jnp.einsum('blh,ph->blp', u, bb_re)
    bu_im = jnp.einsum('blh,ph->blp', u, bb_im)
    ar = jnp.broadcast_to(lb_re, bu_re.shape)
    ai = jnp.broadcast_to(lb_im, bu_im.shape)

    def combine(e1, e2):
        ar1, ai1, br1, bi1 = e1
        ar2, ai2, br2, bi2 = e2
        return (ar2 * ar1 - ai2 * ai1,
                ar2 * ai1 + ai2 * ar1,
                ar2 * br1 - ai2 * bi1 + br2,
                ar2 * bi1 + ai2 * br1 + bi2)

    _, _, s_re, s_im = lax.associative_scan(combine, (ar, ai, bu_re, bu_im), axis=1)
    return jnp.einsum('blp,hp->blh', s_re, c_re) - jnp.einsum('blp,hp->blh', s_im, c_im)


def _s5_branch(u, a_re, a_im, b_re, b_im, c_re, c_im, log_dt, d_skip, w_glu):
    bsz, length, _ = u.shape
    ug = u.reshape(bsz, length, N_GROUPS, SSM_GROUP).transpose(2, 0, 1, 3)
    y = lax.map(_s5_group, (ug, a_re, a_im, b_re, b_im, c_re, c_im, log_dt))
    y = y.transpose(1, 2, 0, 3).reshape(bsz, length, D_SSM).astype(u.dtype)
    y = y + d_skip * u
    z = jax.nn.gelu(y, approximate=False)
    return z * jax.nn.sigmoid(z @ w_glu)


def _diff_attention(q, k, v, positions, lam, subln_g, lam_init):
    bsz, length = q.shape[0], q.shape[1]
    q = _rope(q, positions) * (HEAD_DIM ** -0.5)
    k = _rope(k, positions)
    q1 = q[:, :, :, 0].transpose(0, 2, 1, 3)
    q2 = q[:, :, :, 1].transpose(0, 2, 1, 3)
    k1 = k[:, :, :, 0].transpose(0, 2, 1, 3)
    k2 = k[:, :, :, 1].transpose(0, 2, 1, 3)
    vt = v.transpose(0, 2, 1, 3)
    n_blocks = length // Q_BLOCK
    key_pos = jnp.arange(length)

    def to_blocks(t):
        return t.reshape(bsz, N_HEADS, n_blocks, Q_BLOCK, HEAD_DIM).transpose(2, 0, 1, 3, 4)

    def block(args):
        bi, qa, qb = args
        q_pos = bi * Q_BLOCK + jnp.arange(Q_BLOCK)
        mask = key_pos[None, :] <= q_pos[:, None]
        s1 = jnp.einsum('bhqd,bhkd->bhqk', qa, k1).astype(jnp.float32)
        s2 = jnp.einsum('bhqd,bhkd->bhqk', qb, k2).astype(jnp.float32)
        p1 = jax.nn.softmax(jnp.where(mask, s1, NEG_INF), axis=-1)
        p2 = jax.nn.softmax(jnp.where(mask, s2, NEG_INF), axis=-1)
        w = (p1 - lam * p2).astype(vt.dtype)
        return jnp.einsum('bhqk,bhkd->bhqd', w, vt)

    out = lax.map(block, (jnp.arange(n_blocks), to_blocks(q1), to_blocks(q2)))
    out = out.transpose(1, 0, 3, 2, 4).reshape(bsz, length, N_HEADS, 2 * HEAD_DIM)
    of = out.astype(jnp.float32)
    of = of * lax.rsqrt(jnp.mean(of * of, axis=-1, keepdims=True) + RMS_EPS)
    of = of * subln_g.astype(jnp.float32) * (1.0 - lam_init)
    return of.reshape(bsz, length, N_HEADS * 2 * HEAD_DIM).astype(v.dtype)


def setup_inputs(seed: int = 0) -> dict:
    key = jax.random.key(seed)
    ks = iter(jax.random.split(key, 64))
    f32 = jnp.float32

    def nrm(shape, scale):
        return scale * jax.random.normal(next(ks), shape, f32)

    def gain(shape):
        return 1.0 + nrm(shape, 0.02)

    L = DEPTH
    d = D_MODEL
    inp = {}
    inp["x"] = nrm((BATCH, SEQ, d), 1.0)
    inp["p"] = nrm((DEPTH, BATCH, SEQ, PLE_DIM), 1.0)
    inp["positions"] = jnp.broadcast_to(jnp.arange(SEQ, dtype=jnp.int32), (BATCH, SEQ))
    inp["ffn1_w_gate"] = nrm((L, d, D_FF), d ** -0.5)
    inp["ffn1_w_up"] = nrm((L, d, D_FF), d ** -0.5)
    inp["ffn1_w_down"] = nrm((L, D_FF, d), BETA * D_FF ** -0.5)
    inp["ln1_g"] = gain((L, d))
    inp["ln1_b"] = nrm((L, d), 0.02)
    inp["w_in"] = nrm((L, d, IN_WIDTH), d ** -0.5)
    inp["ssm_a_re"] = -0.5 + nrm((L, N_GROUPS, STATE), 0.01)
    inp["ssm_a_im"] = math.pi * jnp.arange(STATE, dtype=f32)[None, None, :] + nrm((L, N_GROUPS, STATE), 0.01)
    inp["ssm_b_re"] = nrm((L, N_GROUPS, STATE, SSM_GROUP), (2.0 * SSM_GROUP) ** -0.5)
    inp["ssm_b_im"] = nrm((L, N_GROUPS, STATE, SSM_GROUP), (2.0 * SSM_GROUP) ** -0.5)
    inp["ssm_c_re"] = nrm((L, N_GROUPS, SSM_GROUP, STATE), (2.0 * STATE) ** -0.5)
    inp["ssm_c_im"] = nrm((L, N_GROUPS, SSM_GROUP, STATE), (2.0 * STATE) ** -0.5)
    inp["ssm_log_dt"] = jax.random.uniform(next(ks), (L, N_GROUPS), f32,
                                           minval=math.log(DT_MIN), maxval=math.log(DT_MAX))
    inp["ssm_d"] = nrm((L, D_SSM), 1.0)
    inp["ssm_w_glu"] = nrm((L, D_SSM, D_SSM), D_SSM ** -0.5)
    inp["w_branch_ssm"] = nrm((L, D_SSM, d), D_SSM ** -0.5)
    inp["lambda_q1"] = nrm((L, HEAD_DIM), 0.1)
    inp["lambda_k1"] = nrm((L, HEAD_DIM), 0.1)
    inp["lambda_q2"] = nrm((L, HEAD_DIM), 0.1)
    inp["lambda_k2"] = nrm((L, HEAD_DIM), 0.1)
    inp["attn_subln_g"] = gain((L, 2 * HEAD_DIM))
    inp["w_branch_attn"] = nrm((L, V_WIDTH, d), V_WIDTH ** -0.5)
    inp["w_out"] = nrm((L, d, d), BETA * d ** -0.5)
    inp["ln2_g"] = gain((L, d))
    inp["ln2_b"] = nrm((L, d), 0.02)
    inp["ffn2_w_gate"] = nrm((L, d, D_FF), d ** -0.5)
    inp["ffn2_w_up"] = nrm((L, d, D_FF), d ** -0.5)
    inp["ffn2_w_down"] = nrm((L, D_FF, d), BETA * D_FF ** -0.5)
    inp["ln3_g"] = gain((L, d))
    inp["ln3_b"] = nrm((L, d), 0.02)
    inp["ple_w_gate"] = nrm((L, d, d), d ** -0.5)
    inp["ple_w_proj"] = nrm((L, PLE_DIM, d), BETA * PLE_DIM ** -0.5)
    inp["ln4_g"] = gain((L, d))
    inp["ln4_b"] = nrm((L, d), 0.02)
    return inp


def reference(x, p, positions, ffn1_w_gate, ffn1_w_up, ffn1_w_down, ln1_g, ln1_b,
              w_in, ssm_a_re, ssm_a_im, ssm_b_re, ssm_b_im, ssm_c_re, ssm_c_im,
              ssm_log_dt, ssm_d, ssm_w_glu, w_branch_ssm,
              lambda_q1, lambda_k1, lambda_q2, lambda_k2, attn_subln_g, w_branch_attn,
              w_out, ln2_g, ln2_b, ffn2_w_gate, ffn2_w_up, ffn2_w_down, ln3_g, ln3_b,
              ple_w_gate, ple_w_proj, ln4_g, ln4_b):
    bsz, length, _ = x.shape
    splits = [D_SSM, D_SSM + QK_WIDTH, D_SSM + 2 * QK_WIDTH,
              D_SSM + 2 * QK_WIDTH + V_WIDTH, D_SSM + 2 * QK_WIDTH + V_WIDTH + D_MODEL]
    for i in range(DEPTH):
        lam_init = 0.8 - 0.6 * math.exp(-0.3 * i)
        x = _layer_norm(ALPHA * x + 0.5 * _swiglu(x, ffn1_w_gate[i], ffn1_w_up[i], ffn1_w_down[i]),
                        ln1_g[i], ln1_b[i])
        h = x @ w_in[i]
        u, q, k, v, g_a, g_b = jnp.split(h, splits, axis=-1)
        q = q.reshape(bsz, length, N_HEADS, 2, HEAD_DIM)
        k = k.reshape(bsz, length, N_HEADS, 2, HEAD_DIM)
        v = v.reshape(bsz, length, N_HEADS, 2 * HEAD_DIM)
        lam = (jnp.exp(jnp.sum(lambda_q1[i].astype(jnp.float32) * lambda_k1[i].astype(jnp.float32)))
               - jnp.exp(jnp.sum(lambda_q2[i].astype(jnp.float32) * lambda_k2[i].astype(jnp.float32)))
               + lam_init)
        y_a = _s5_branch(u, ssm_a_re[i], ssm_a_im[i], ssm_b_re[i], ssm_b_im[i],
                         ssm_c_re[i], ssm_c_im[i], ssm_log_dt[i], ssm_d[i], ssm_w_glu[i]) @ w_branch_ssm[i]
        y_b = _diff_attention(q, k, v, positions, lam, attn_subln_g[i], lam_init) @ w_branch_attn[i]
        mix = (jax.nn.sigmoid(g_a) * y_a + jax.nn.sigmoid(g_b) * y_b) @ w_out[i]
        x = _layer_norm(ALPHA * x + mix, ln2_g[i], ln2_b[i])
        x = _layer_norm(ALPHA * x + 0.5 * _swiglu(x, ffn2_w_gate[i], ffn2_w_up[i], ffn2_w_down[i]),
                        ln3_g[i], ln3_b[i])
        ple = jax.nn.sigmoid(x @ ple_w_gate[i]) * (p[i] @ ple_w_proj[i])
        x = _layer_norm(ALPHA * x + ple, ln4_g[i], ln4_b[i])
    return x
```

```python
import math
import numpy as np
import ml_dtypes
import concourse.bass as bass
import concourse.mybir as mybir
from concourse.bass_utils import run_bass_kernel_spmd

F32 = mybir.dt.float32
BF16 = mybir.dt.bfloat16
I32 = mybir.dt.int32
AF = mybir.ActivationFunctionType
ALU = mybir.AluOpType
NPBF = ml_dtypes.bfloat16

D_MODEL = 2048
SEQ = 16384
DEPTH = 2
PLE_DIM = 256
D_FF = 5632
D_SSM = 1024
N_HEADS = 8
HEAD_DIM = 64
IN_WIDTH = 8192
ROT_DIM = 16
ROPE_THETA = 500000.0
LN_EPS = 1e-5
RMS_EPS = 1e-5
ALPHA = (2 * DEPTH) ** 0.25
NCORES = 8
TT = 512
KC = D_MODEL // 128
FC = D_FF // 128
MAGIC = 12582912.0
TWO_PI = 2.0 * math.pi
C1 = 6.28125
C2 = TWO_PI - 6.28125


class Buf:
    __slots__ = ("name", "w", "r")

    def __init__(self, name):
        self.name = name
        self.w = None
        self.r = []


class Ctx:
    ENG = ("pe", "act", "dve", "pool", "sp")

    def __init__(self, nc):
        self.nc = nc
        self.streams = {e: [] for e in self.ENG}
        self.cnt = {}
        self.waited = {e: {} for e in self.ENG}
        self.semkeys = []
        self.out_waits = []
        self.bufs = {}

    def buf(self, name):
        b = self.bufs.get(name)
        if b is None:
            b = self.bufs[name] = Buf(name)
        return b

    def _semkey(self, k):
        if k not in self.cnt:
            self.cnt[k] = 0
            self.semkeys.append(k)
        return k

    def _deps(self, eng, reads, writes, own_key):
        deps = {}
        for b in reads:
            if b.w is not None:
                k, v = b.w
                deps[k] = max(deps.get(k, 0), v)
        for b in writes:
            if b.w is not None:
                k, v = b.w
                deps[k] = max(deps.get(k, 0), v)
            for k, v in b.r:
                deps[k] = max(deps.get(k, 0), v)
        out = []
        wt = self.waited[eng]
        for k, v in deps.items():
            if k == own_key and eng == "pe":
                continue
            if wt.get(k, 0) >= v:
                continue
            wt[k] = v
            out.append((k, v))
        return out

    def op(self, eng, fn, reads=(), writes=()):
        key = self._semkey("c_" + eng)
        waits = self._deps(eng, reads, writes, key)
        self.cnt[key] += 1
        val = self.cnt[key]
        self.streams[eng].append((waits, fn, key, 1))
        for b in writes:
            b.w = (key, val)
            b.r = []
        for b in reads:
            if b not in writes:
                b.r.append((key, val))
        return (key, val)

    def op_multi(self, eng, fns, reads=(), writes=()):
        key = self._semkey("c_" + eng)
        waits = self._deps(eng, reads, writes, key)
        self.cnt[key] += 1
        val = self.cnt[key]
        n = len(fns)
        for i, fn in enumerate(fns):
            self.streams[eng].append((waits if i == 0 else [], fn, key if i == n - 1 else None, 1))
        for b in writes:
            b.w = (key, val)
            b.r = []
        for b in reads:
            if b not in writes:
                b.r.append((key, val))
        return (key, val)

    def dma(self, eng, out, in_, reads=(), writes=(), semname=None, is_output=False):
        if semname is None:
            semname = (writes[0].name if writes else reads[0].name)
        key = self._semkey("d_" + semname)
        waits = self._deps(eng, reads, writes, key)
        self.cnt[key] += 16
        val = self.cnt[key]

        def fn(e, out=out, in_=in_):
            return e.dma_start(out=out, in_=in_)
        self.streams[eng].append((waits, fn, key, 16))
        for b in writes:
            b.w = (key, val)
            b.r = []
        for b in reads:
            if b not in writes:
                b.r.append((key, val))
        if is_output:
            self.out_waits.append((key, val))
        return (key, val)

    def fence(self, eng="sp"):
        waits = []
        wt = self.waited[eng]
        for k in self.semkeys:
            v = self.cnt[k]
            if v > 0 and wt.get(k, 0) < v:
                wt[k] = v
                waits.append((k, v))
        self.streams[eng].append((waits, None, None, 0))

    def emit(self):
        nc = self.nc
        fin = {}
        for k, v in self.out_waits:
            fin[k] = max(fin.get(k, 0), v)
        import contextlib
        with contextlib.ExitStack() as es:
            sems = {}
            for k in self.semkeys:
                sems[k] = es.enter_context(nc.semaphore(k))
            block = es.enter_context(nc.Block())
            streams = self.streams

            def run(e, name):
                for waits, fn, key, inc in streams[name]:
                    for k, v in waits:
                        e.wait_ge(sems[k], v)
                    if fn is None:
                        continue
                    ins = fn(e)
                    if key is not None:
                        ins.then_inc(sems[key], inc)
                if name == "sp":
                    for k, v in fin.items():
                        e.wait_ge(sems[k], v)

            @block.tensor
            def _(e):
                run(e, "pe")

            @block.scalar
            def _(e):
                run(e, "act")

            @block.vector
            def _(e):
                run(e, "dve")

            @block.gpsimd
            def _(e):
                run(e, "pool")

            @block.sync
            def _(e):
                run(e, "sp")


class Tile:
    def __init__(self, ctx, es, name, shape, dtype, psum=False, nsub=1):
        nc = ctx.nc
        if psum:
            self.t = es.enter_context(nc.psum_tensor("t_" + name, shape, dtype))
        else:
            self.t = es.enter_context(nc.sbuf_tensor("t_" + name, shape, dtype))
        self.b = [ctx.buf(f"{name}#{i}") for i in range(nsub)]
        self.name = name

    def __getitem__(self, idx):
        return self.t[idx]


def build_cast(ncols_total):
    import contextlib
    nc = bass.Bass("TRN2", target_bir_lowering=False)
    x = nc.dram_tensor("x", [128, ncols_total], F32, kind="ExternalInput").ap()
    y = nc.dram_tensor("y", [128, ncols_total], BF16, kind="ExternalOutput").ap()
    ctx = Ctx(nc)
    CB = 4096
    nblk = (ncols_total + CB - 1) // CB
    with contextlib.ExitStack() as es:
        NB = 3
        xin = [Tile(ctx, es, f"xin{i}", [128, CB], F32) for i in range(NB)]
        yo = [Tile(ctx, es, f"yo{i}", [128, CB], BF16) for i in range(NB)]
        engs = ["dve", "pool", "act"]
        def load(i):
            if i >= nblk:
                return
            c0 = i * CB
            cw = min(CB, ncols_total - c0)
            ctx.dma("sp", xin[i % NB][:, 0:cw], x[:, c0:c0 + cw], writes=[xin[i % NB].b[0]])
        for i in range(NB):
            load(i)
        for i in range(nblk):
            c0 = i * CB
            cw = min(CB, ncols_total - c0)
            xi, yi = xin[i % NB], yo[i % NB]
            eng = engs[i % 3]
            if eng == "act":
                ctx.op("act", lambda e, xi=xi, yi=yi, cw=cw: e.copy(out=yi[:, 0:cw], in_=xi[:, 0:cw]),
                       reads=[xi.b[0]], writes=[yi.b[0]])
            else:
                ctx.op(eng, lambda e, xi=xi, yi=yi, cw=cw: e.tensor_copy(out=yi[:, 0:cw], in_=xi[:, 0:cw]),
                       reads=[xi.b[0]], writes=[yi.b[0]])
            ctx.dma("sp", y[:, c0:c0 + cw], yi[:, 0:cw], reads=[yi.b[0]], is_output=True)
            load(i + NB)
        ctx.emit()
    return nc


def pretile(W):
    K, F = W.shape
    outs = []
    for fi in range(F // 128):
        for k0 in range(0, K, 2048):
            blk = W[k0:min(K, k0 + 2048), fi * 128:(fi + 1) * 128]
            n_c = blk.shape[0] // 128
            outs.append(blk.reshape(n_c, 128, 128).transpose(1, 0, 2).reshape(128, n_c * 128))
    return outs


def colvec(v):
    return np.ascontiguousarray(v.reshape(-1, 128).T)


class WStream:
    def __init__(self, ctx, es, wts_ap, nslab=8):
        self.ctx = ctx
        self.wts = wts_ap
        self.slabs = [Tile(ctx, es, f"slab{i}", [128, 2048], BF16) for i in range(nslab)]
        self.ns = nslab
        self.sched = []
        self.i = 0
        self.issued = 0

    def _issue(self, k):
        off, n = self.sched[k]
        sl = self.slabs[k % self.ns]
        self.ctx.dma("sp", sl[:, 0:n], self.wts[:, off:off + n], writes=[sl.b[0]])

    def next(self, ncols):
        while self.issued < min(len(self.sched), self.i + self.ns):
            self._issue(self.issued)
            self.issued += 1
        off, n = self.sched[self.i]
        assert n == ncols, (self.i, n, ncols)
        sl = self.slabs[self.i % self.ns]
        self.i += 1
        return sl


class TokPhase:
    def __init__(self, nc, ctx, es, wts_ap, vec_ap, nvec):
        self.nc, self.ctx = nc, ctx
        self.ws = WStream(ctx, es, wts_ap)
        self.x32 = Tile(ctx, es, "x32", [128, KC, TT], F32, nsub=KC)
        self.xb = Tile(ctx, es, "xb", [128, KC, TT], BF16, nsub=KC)
        self.hT = Tile(ctx, es, "hT", [128, FC, TT], BF16, nsub=FC)
        self.ps = [Tile(ctx, es, f"ps{i}", [128, TT], F32, psum=True) for i in range(8)]
        self.psi = 0
        self.tf = [Tile(ctx, es, f"tf{i}", [128, TT], F32) for i in range(10)]
        self.tfi = 0
        self.lnt = [Tile(ctx, es, f"lnt{i}", [128, TT], F32) for i in range(5)]
        self.tb = [Tile(ctx, es, f"tb{i}", [128, TT], BF16) for i in range(6)]
        self.tbi = 0
        self.vec = Tile(ctx, es, "vec", [128, nvec], F32)
        self.ones32 = Tile(ctx, es, "ones32", [128, 128], F32)
        ctx.dma("sp", self.vec[:, :], vec_ap, writes=[self.vec.b[0]])
        ctx.op("pool", lambda e: e.memset(self.ones32[:, :], 1.0), writes=[self.ones32.b[0]])

    def next_ps(self):
        p = self.ps[self.psi % 8]
        self.psi += 1
        return p

    def tmpf(self):
        t = self.tf[self.tfi % len(self.tf)]
        self.tfi += 1
        return t

    def tmpb(self):
        t = self.tb[self.tbi % len(self.tb)]
        self.tbi += 1
        return t

    def acc(self, rhs):
        ctx = self.ctx
        ps = self.next_ps()
        n = len(rhs)
        k = 0
        while k < n:
            nk = min(16, n - k)
            sl = self.ws.next(nk * 128)
            fns = []
            for c in range(nk):
                ap = rhs[k + c][0]
                fns.append(lambda e, ps=ps, sl=sl, c=c, ap=ap, st=(k + c == 0), sp=(k + c == n - 1):
                           e.matmul(ps[:, :], sl[:, c * 128:(c + 1) * 128], ap, start=st, stop=sp))
            ctx.op_multi("pe", fns, reads=[sl.b[0]] + [rhs[k + c][1] for c in range(nk)], writes=[ps.b[0]])
            k += nk
        return ps

    @staticmethod
    def units_for(base, K, F):
        out = []
        for fi in range(F // 128):
            for k0 in range(0, K, 2048):
                n = (min(K, k0 + 2048) - k0)
                out.append((base, n))
                base += n
        return out, base

    def chunks(self, tile, lo, hi):
        return [(tile[:, c, :], tile.b[c]) for c in range(lo, hi)]

    def load_x(self, xT_ap, t0):
        ctx = self.ctx
        ctx.dma("sp", self.x32[:, :, :], xT_ap[:, t0:t0 + TT].rearrange("(c p) t -> p c t", p=128),
                writes=list(self.x32.b), semname="x32ld")
        for c in range(KC):
            eng = ("pool", "dve")[c % 2]
            ctx.op(eng, lambda e, c=c: e.tensor_copy(out=self.xb[:, c, :], in_=self.x32[:, c, :]),
                   reads=[self.x32.b[c]], writes=[self.xb.b[c]])

    def store_x(self, out_ap, t0, is_output=True):
        for c in range(KC):
            self.ctx.dma("pool", out_ap[c * 128:(c + 1) * 128, t0:t0 + TT], self.x32[:, c, :], reads=[self.x32.b[c]],
                         semname="x32st", is_output=is_output)

    def ffn(self):
        ctx = self.ctx
        xr = self.chunks(self.xb, 0, KC)
        for j in range(FC):
            pg = self.acc(xr)
            pu = self.acc(xr)
            sg = self.tmpf()
            ctx.op("act", lambda e, sg=sg, pg=pg: e.activation(out=sg[:, :], in_=pg[:, :], func=AF.Silu),
                   reads=[pg.b[0]], writes=[sg.b[0]])
            ctx.op("dve", lambda e, sg=sg, pu=pu, j=j: e.scalar_tensor_tensor(
                out=self.hT[:, j, :], in0=sg[:, :], scalar=0.5, in1=pu[:, :], op0=ALU.mult, op1=ALU.mult),
                reads=[sg.b[0], pu.b[0]], writes=[self.hT.b[j]])
        hr = self.chunks(self.hT, 0, FC)
        for m in range(KC):
            pd = self.acc(hr)
            ctx.op("dve", lambda e, pd=pd, m=m: e.scalar_tensor_tensor(
                out=self.x32[:, m, :], in0=self.x32[:, m, :], scalar=ALPHA, in1=pd[:, :], op0=ALU.mult, op1=ALU.add),
                reads=[pd.b[0], self.x32.b[m]], writes=[self.x32.b[m]])

    @staticmethod
    def ffn_units(base):
        out = []
        for j in range(FC):
            out.append((base, 2048)); base += 2048
            out.append((base, 2048)); base += 2048
        for m in range(KC):
            for n in (2048, 2048, 1536):
                out.append((base, n)); base += n
        return out, base

    def layernorm(self, gcol, bcol):
        ctx = self.ctx
        ps_sum = self.next_ps()
        ps_sq = self.next_ps()
        for c in range(KC):
            sq = self.tmpf()
            ctx.op("act", lambda e, sq=sq, c=c: e.activation(out=sq[:, :], in_=self.x32[:, c, :], func=AF.Square),
                   reads=[self.x32.b[c]], writes=[sq.b[0]])
            ctx.op("pe", lambda e, c=c: e.matmul(ps_sum[:, :], self.ones32[:, :], self.x32[:, c, :],
                                                 start=(c == 0), stop=(c == KC - 1)),
                   reads=[self.x32.b[c], self.ones32.b[0]], writes=[ps_sum.b[0]])
            ctx.op("pe", lambda e, c=c, sq=sq: e.matmul(ps_sq[:, :], self.ones32[:, :], sq[:, :],
                                                        start=(c == 0), stop=(c == KC - 1)),
                   reads=[sq.b[0], self.ones32.b[0]], writes=[ps_sq.b[0]])
        mean, msq, var, rstd, nmr = self.lnt
        inv = 1.0 / D_MODEL
        ctx.op("act", lambda e: e.mul(out=mean[:, :], in_=ps_sum[:, :], mul=inv), reads=[ps_sum.b[0]], writes=[mean.b[0]])
        ctx.op("dve", lambda e: e.tensor_tensor(out=msq[:, :], in0=mean[:, :], in1=mean[:, :], op=ALU.mult),
               reads=[mean.b[0]], writes=[msq.b[0]])
        ctx.op("dve", lambda e: e.scalar_tensor_tensor(out=var[:, :], in0=ps_sq[:, :], scalar=inv, in1=msq[:, :],
                                                       op0=ALU.mult, op1=ALU.subtract),
               reads=[ps_sq.b[0], msq.b[0]], writes=[var.b[0]])
        ctx.op("dve", lambda e: e.tensor_scalar(out=var[:, :], in0=var[:, :], scalar1=LN_EPS, scalar2=None, op0=ALU.add),
               reads=[var.b[0]], writes=[var.b[0]])
        ctx.op("act", lambda e: e.activation(out=msq[:, :], in_=var[:, :], func=AF.Sqrt), reads=[var.b[0]], writes=[msq.b[0]])
        ctx.op("dve", lambda e: e.reciprocal(out=rstd[:, :], in_=msq[:, :]), reads=[msq.b[0]], writes=[rstd.b[0]])
        ctx.op("dve", lambda e: e.scalar_tensor_tensor(out=nmr[:, :], in0=mean[:, :], scalar=-1.0, in1=rstd[:, :],
                                                       op0=ALU.mult, op1=ALU.mult),
               reads=[mean.b[0], rstd.b[0]], writes=[nmr.b[0]])
        for c in range(KC):
            t1 = self.tmpf()
            t2 = self.tmpf()
            ctx.op("dve", lambda e, c=c, t1=t1: e.tensor_tensor(out=t1[:, :], in0=self.x32[:, c, :], in1=rstd[:, :], op=ALU.mult),
                   reads=[self.x32.b[c], rstd.b[0]], writes=[t1.b[0]])
            ctx.op("pool", lambda e, t1=t1, t2=t2: e.tensor_tensor(out=t2[:, :], in0=t1[:, :], in1=nmr[:, :], op=ALU.add),
                   reads=[t1.b[0], nmr.b[0]], writes=[t2.b[0]])
            ctx.op("act", lambda e, c=c, t2=t2: e.activation(out=self.x32[:, c, :], in_=t2[:, :], func=AF.Identity,
                                                             scale=self.vec[:, gcol + c:gcol + c + 1],
                                                             bias=self.vec[:, bcol + c:bcol + c + 1]),
                   reads=[t2.b[0], self.vec.b[0]], writes=[self.x32.b[c]])
            ctx.op("pool", lambda e, c=c: e.tensor_copy(out=self.xb[:, c, :], in_=self.x32[:, c, :]),
                   reads=[self.x32.b[c]], writes=[self.xb.b[c]])
        return mean, var, rstd, nmr

    def rope_tables(self, es_tiles, posi_ap, t0, invf_col):
        ctx = self.ctx
        pi_t, posf, ang, kk, r, cosk, sink, cosq, sinq = es_tiles
        PI_SAFE = 3.1415925
        ctx.dma("sp", pi_t[:, :], posi_ap[:, t0:t0 + TT], writes=[pi_t.b[0]])
        ctx.op("dve", lambda e: e.tensor_copy(out=posf[:, :], in_=pi_t[:, :]), reads=[pi_t.b[0]], writes=[posf.b[0]])
        ctx.op("dve", lambda e: e.tensor_scalar(out=ang[:, :], in0=posf[:, :], scalar1=self.vec[:, invf_col:invf_col + 1],
                                                scalar2=None, op0=ALU.mult),
               reads=[posf.b[0], self.vec.b[0]], writes=[ang.b[0]])
        self.range_reduce(ang, kk, r)
        ctx.op("act", lambda e: e.activation(out=sink[:, :], in_=r[:, :], func=AF.Sin), reads=[r.b[0]], writes=[sink.b[0]])
        ctx.op("dve", lambda e: e.tensor_scalar(out=ang[:, :], in0=r[:, :], scalar1=math.pi / 2, scalar2=None, op0=ALU.add),
               reads=[r.b[0]], writes=[ang.b[0]])
        ctx.op("dve", lambda e: e.tensor_scalar(out=kk[:, :], in0=ang[:, :], scalar1=math.pi, scalar2=TWO_PI,
                                                op0=ALU.is_gt, op1=ALU.mult),
               reads=[ang.b[0]], writes=[kk.b[0]])
        ctx.op("dve", lambda e: e.tensor_tensor(out=ang[:, :], in0=ang[:, :], in1=kk[:, :], op=ALU.subtract),
               reads=[ang.b[0], kk.b[0]], writes=[ang.b[0]])
        ctx.op("dve", lambda e: e.tensor_scalar(out=ang[:, :], in0=ang[:, :], scalar1=-PI_SAFE, scalar2=PI_SAFE,
                                                op0=ALU.max, op1=ALU.min),
               reads=[ang.b[0]], writes=[ang.b[0]])
        ctx.op("act", lambda e: e.activation(out=cosk[:, :], in_=ang[:, :], func=AF.Sin), reads=[ang.b[0]], writes=[cosk.b[0]])
        sc = HEAD_DIM ** -0.5
        ctx.op("pool", lambda e: e.tensor_scalar(out=cosq[:, :], in0=cosk[:, :], scalar1=sc, scalar2=None, op0=ALU.mult),
               reads=[cosk.b[0]], writes=[cosq.b[0]])
        ctx.op("pool", lambda e: e.tensor_scalar(out=sinq[:, :], in0=sink[:, :], scalar1=sc, scalar2=None, op0=ALU.mult),
               reads=[sink.b[0]], writes=[sinq.b[0]])

    def range_reduce(self, ang, kk, r, eng="dve"):
        ctx = self.ctx
        PI_SAFE = 3.1415925
        ctx.op(eng, lambda e: e.tensor_scalar(out=kk[:, :], in0=ang[:, :], scalar1=1.0 / TWO_PI, scalar2=MAGIC,
                                              op0=ALU.mult, op1=ALU.add), reads=[ang.b[0]], writes=[kk.b[0]])
        ctx.op(eng, lambda e: e.tensor_scalar(out=kk[:, :], in0=kk[:, :], scalar1=MAGIC, scalar2=None, op0=ALU.subtract),
               reads=[kk.b[0]], writes=[kk.b[0]])
        ctx.op(eng, lambda e: e.scalar_tensor_tensor(out=r[:, :], in0=kk[:, :], scalar=-C1, in1=ang[:, :],
                                                     op0=ALU.mult, op1=ALU.add),
               reads=[kk.b[0], ang.b[0]], writes=[r.b[0]])
        ctx.op(eng, lambda e: e.scalar_tensor_tensor(out=r[:, :], in0=kk[:, :], scalar=-C2, in1=r[:, :],
                                                     op0=ALU.mult, op1=ALU.add),
               reads=[kk.b[0], r.b[0]], writes=[r.b[0]])
        ctx.op(eng, lambda e: e.tensor_scalar(out=r[:, :], in0=r[:, :], scalar1=-PI_SAFE, scalar2=PI_SAFE,
                                              op0=ALU.max, op1=ALU.min), reads=[r.b[0]], writes=[r.b[0]])


def rope_perm():
    P = np.zeros((128, 128), np.float32)
    for blk in (0, 64):
        for i in range(8):
            P[blk + i + 8, blk + i] = -1.0
            P[blk + i, blk + i + 8] = 1.0
    return P


def rope_invf():
    v = np.zeros((128,), np.float32)
    inv = ROPE_THETA ** (-np.arange(0, ROT_DIM, 2, dtype=np.float32) / ROT_DIM)
    for blk in (0, 64):
        for j in range(16):
            v[blk + j] = inv[j % 8]
    return v.astype(np.float32)


def build_LA(T, stages=(1, 1, 1)):
    import contextlib
    nc = bass.Bass("TRN2", target_bir_lowering=False)
    n_w = (3 * D_MODEL * D_FF + D_MODEL * IN_WIDTH) // 128
    xT = nc.dram_tensor("xT", [D_MODEL, T], F32, kind="ExternalInput").ap()
    wts = nc.dram_tensor("wts", [128, n_w], BF16, kind="ExternalInput").ap()
    vec = nc.dram_tensor("vec", [128, 33], F32, kind="ExternalInput").ap()
    perm = nc.dram_tensor("perm", [128, 128], F32, kind="ExternalInput").ap()
    posi = nc.dram_tensor("posi", [128, T], I32, kind="ExternalInput").ap()
    x1T = nc.dram_tensor("x1T", [D_MODEL, T], F32, kind="ExternalOutput").ap()
    uT = nc.dram_tensor("uT", [D_SSM, T], F32, kind="ExternalOutput").ap()
    qT = nc.dram_tensor("qT", [1024, T], BF16, kind="ExternalOutput").ap()
    kT = nc.dram_tensor("kT", [1024, T], BF16, kind="ExternalOutput").ap()
    vT = nc.dram_tensor("vT", [1024, T], BF16, kind="ExternalOutput").ap()
    sgT = nc.dram_tensor("sgT", [4096, T], BF16, kind="ExternalOutput").ap()
    ctx = Ctx(nc)
    with contextlib.ExitStack() as es:
        ph = TokPhase(nc, ctx, es, wts, vec, 33)
        permt = Tile(ctx, es, "permt", [128, 128], F32)
        ctx.dma("sp", permt[:, :], perm, writes=[permt.b[0]])
        rt_i = Tile(ctx, es, "rt_pi", [128, TT], I32)
        rt = [rt_i] + [Tile(ctx, es, f"rt{i}", [128, TT], F32) for i in range(8)]
        cosk, sink, cosq, sinq = rt[5], rt[6], rt[7], rt[8]
        sched = []
        ntile = T // TT
        for _ in range(ntile):
            base = 0
            u, base = TokPhase.ffn_units(base)
            sched += u
            u, base = TokPhase.units_for(base, D_MODEL, IN_WIDTH)
            sched += u
        ph.ws.sched = sched
        for ti in range(ntile):
            t0 = ti * TT
            ph.load_x(xT, t0)
            ph.rope_tables(rt, posi, t0, 32)
            if stages[0]:
                ph.ffn()
            else:
                ph.ws.i += 2 * FC + 3 * KC
                ph.ws.issued = ph.ws.i
            if stages[1]:
                dbg = ph.layernorm(0, 16)
                if len(stages) > 3:
                    for i, tl in enumerate(dbg):
                        ctx.dma("pool", uT[i * 128:(i + 1) * 128, t0:t0 + TT], tl[:, :], reads=[tl.b[0]], is_output=True)
            ph.store_x(x1T, t0)
            xr = ph.chunks(ph.xb, 0, KC)
            for o in range(IN_WIDTH // 128 if stages[2] else 0):
                ps = ph.acc(xr)
                if o < 8:
                    st = ph.tmpf()
                    ctx.op("act", lambda e, st=st, ps=ps: e.copy(out=st[:, :], in_=ps[:, :]), reads=[ps.b[0]], writes=[st.b[0]])
                    ctx.dma("pool", uT[o * 128:(o + 1) * 128, t0:t0 + TT], st[:, :], reads=[st.b[0]], is_output=True)
                elif o < 24:
                    isq = o < 16
                    ct, sn = (cosq, sinq) if isq else (cosk, sink)
                    dst = qT if isq else kT
                    h = o - 8 if isq else o - 16
                    qf = ph.tmpf()
                    ctx.op("act", lambda e, qf=qf, ps=ps: e.copy(out=qf[:, :], in_=ps[:, :]), reads=[ps.b[0]], writes=[qf.b[0]])
                    ps2 = ph.next_ps()
                    ctx.op("pe", lambda e, ps2=ps2, qf=qf: e.matmul(ps2[:, :], permt[:, :], qf[:, :], start=True, stop=True),
                           reads=[qf.b[0], permt.b[0]], writes=[ps2.b[0]])
                    t1 = ph.tmpf()
                    t2 = ph.tmpf()
                    ctx.op("dve", lambda e, t1=t1, qf=qf, ct=ct: e.tensor_tensor(out=t1[:, :], in0=qf[:, :], in1=ct[:, :], op=ALU.mult),
                           reads=[qf.b[0], ct.b[0]], writes=[t1.b[0]])
                    ctx.op("dve", lambda e, t2=t2, ps2=ps2, sn=sn: e.tensor_tensor(out=t2[:, :], in0=ps2[:, :], in1=sn[:, :], op=ALU.mult),
                           reads=[ps2.b[0], sn.b[0]], writes=[t2.b[0]])
                    sb = ph.tmpb()
                    ctx.op("pool", lambda e, sb=sb, t1=t1, t2=t2: e.tensor_tensor(out=sb[:, :], in0=t1[:, :], in1=t2[:, :], op=ALU.add),
                           reads=[t1.b[0], t2.b[0]], writes=[sb.b[0]])
                    ctx.dma("pool", dst[h * 128:(h + 1) * 128, t0:t0 + TT], sb[:, :], reads=[sb.b[0]], is_output=True)
                elif o < 32:
                    h = o - 24
                    sb = ph.tmpb()
                    ctx.op("act", lambda e, sb=sb, ps=ps: e.copy(out=sb[:, :], in_=ps[:, :]), reads=[ps.b[0]], writes=[sb.b[0]])
                    ctx.dma("pool", vT[h * 128:(h + 1) * 128, t0:t0 + TT], sb[:, :], reads=[sb.b[0]], is_output=True)
                else:
                    g = o - 32
                    sb = ph.tmpb()
                    ctx.op("act", lambda e, sb=sb, ps=ps: e.activation(out=sb[:, :], in_=ps[:, :], func=AF.Sigmoid),
                           reads=[ps.b[0]], writes=[sb.b[0]])
                    ctx.dma("pool", sgT[g * 128:(g + 1) * 128, t0:t0 + TT], sb[:, :], reads=[sb.b[0]], is_output=True)
        ctx.emit()
    return nc


def la_weights(ffn_g, ffn_u, ffn_d, w_in):
    g = pretile(ffn_g)
    u = pretile(ffn_u)
    d = pretile(ffn_d)
    out = []
    for j in range(FC):
        out.append(g[j]); out.append(u[j])
    out += d
    out += pretile(w_in)
    return out


def ld_weights(w_glu, w_a, w_b, w_out, ffn_g, ffn_u, ffn_d, ple_g, ple_p):
    out = []
    out += pretile(w_glu)
    a = pretile(w_a)
    b = pretile(w_b)
    for o in range(KC):
        out.append(a[o]); out.append(b[o])
    out += pretile(w_out)
    g = pretile(ffn_g); u = pretile(ffn_u); d = pretile(ffn_d)
    for j in range(FC):
        out.append(g[j]); out.append(u[j])
    out += d
    pg = pretile(ple_g); pp = pretile(ple_p)
    for o in range(KC):
        out.append(pg[o]); out.append(pp[o])
    return out


def ld_sched():
    s = []
    base = 0
    for o in range(8):
        s.append((base, 1024)); base += 1024
    for o in range(KC):
        s.append((base, 1024)); base += 1024
        s.append((base, 1024)); base += 1024
    for o in range(KC):
        s.append((base, 2048)); base += 2048
    u, base = TokPhase.ffn_units(base)
    s += u
    for o in range(KC):
        s.append((base, 2048)); base += 2048
        s.append((base, 256)); base += 256
    return s, base


def build_LD(T):
    import contextlib
    nc = bass.Bass("TRN2", target_bir_lowering=False)
    sched1, n_w = ld_sched()
    x1T = nc.dram_tensor("x1T", [D_MODEL, T], F32, kind="ExternalInput").ap()
    zT = nc.dram_tensor("zT", [D_SSM, T], BF16, kind="ExternalInput").ap()
    aT = nc.dram_tensor("aT", [1024, T], BF16, kind="ExternalInput").ap()
    sgT = nc.dram_tensor("sgT", [4096, T], BF16, kind="ExternalInput").ap()
    pT = nc.dram_tensor("pT", [PLE_DIM, T], F32, kind="ExternalInput").ap()
    wts = nc.dram_tensor("wts", [128, n_w], BF16, kind="ExternalInput").ap()
    vec = nc.dram_tensor("vec", [128, 96], F32, kind="ExternalInput").ap()
    x4T = nc.dram_tensor("x4T", [D_MODEL, T], F32, kind="ExternalOutput").ap()
    ctx = Ctx(nc)
    with contextlib.ExitStack() as es:
        ph = TokPhase(nc, ctx, es, wts, vec, 96)
        zb = Tile(ctx, es, "zb", [128, 8, TT], BF16, nsub=8)
        ab = Tile(ctx, es, "ab", [128, 8, TT], BF16, nsub=8)
        zz = Tile(ctx, es, "zz", [128, 8, TT], BF16, nsub=8)
        p32 = Tile(ctx, es, "p32", [128, 2, TT], F32, nsub=2)
        pb = Tile(ctx, es, "pb", [128, 2, TT], BF16, nsub=2)
        ntile = T // TT
        ph.ws.sched = sched1 * ntile
        hT, x32, xb = ph.hT, ph.x32, ph.xb
        for ti in range(ntile):
            t0 = ti * TT
            ctx.dma("sp", x32[:, :, :], x1T[:, t0:t0 + TT].rearrange("(c p) t -> p c t", p=128), writes=list(x32.b), semname="x32ld")
            ctx.dma("sp", zb[:, :, :], zT[:, t0:t0 + TT].rearrange("(c p) t -> p c t", p=128), writes=list(zb.b), semname="zbld")
            ctx.dma("sp", ab[:, :, :], aT[:, t0:t0 + TT].rearrange("(c p) t -> p c t", p=128), writes=list(ab.b), semname="abld")
            ctx.dma("sp", hT[:, 0:32, :], sgT[:, t0:t0 + TT].rearrange("(c p) t -> p c t", p=128), writes=list(hT.b[0:32]), semname="sgld")
            ctx.dma("sp", p32[:, :, :], pT[:, t0:t0 + TT].rearrange("(c p) t -> p c t", p=128), writes=list(p32.b), semname="pld")
            for c in range(2):
                ctx.op("pool", lambda e, c=c: e.tensor_copy(out=pb[:, c, :], in_=p32[:, c, :]), reads=[p32.b[c]], writes=[pb.b[c]])
            zr = ph.chunks(zb, 0, 8)
            for o in range(8):
                ps = ph.acc(zr)
                sg = ph.tmpf()
                ctx.op("act", lambda e, sg=sg, ps=ps: e.activation(out=sg[:, :], in_=ps[:, :], func=AF.Sigmoid),
                       reads=[ps.b[0]], writes=[sg.b[0]])
                ctx.op("dve", lambda e, sg=sg, o=o: e.tensor_tensor(out=zz[:, o, :], in0=zb[:, o, :], in1=sg[:, :], op=ALU.mult),
                       reads=[sg.b[0], zb.b[o]], writes=[zz.b[o]])
            zzr = ph.chunks(zz, 0, 8)
            ar = ph.chunks(ab, 0, 8)
            for o in range(KC):
                pa = ph.acc(zzr)
                pbb = ph.acc(ar)
                t1 = ph.tmpf()
                t2 = ph.tmpf()
                ctx.op("dve", lambda e, t1=t1, pa=pa, o=o: e.tensor_tensor(out=t1[:, :], in0=pa[:, :], in1=hT[:, o, :], op=ALU.mult),
                       reads=[pa.b[0], hT.b[o]], writes=[t1.b[0]])
                ctx.op("dve", lambda e, t2=t2, pbb=pbb, o=o: e.tensor_tensor(out=t2[:, :], in0=pbb[:, :], in1=hT[:, 16 + o, :], op=ALU.mult),
                       reads=[pbb.b[0], hT.b[16 + o]], writes=[t2.b[0]])
                ctx.op("pool", lambda e, t1=t1, t2=t2, o=o: e.tensor_tensor(out=xb[:, o, :], in0=t1[:, :], in1=t2[:, :], op=ALU.add),
                       reads=[t1.b[0], t2.b[0]], writes=[xb.b[o]])
            mr = ph.chunks(xb, 0, KC)
            for o in range(KC):
                ps = ph.acc(mr)
                ctx.op("dve", lambda e, ps=ps, o=o: e.scalar_tensor_tensor(
                    out=x32[:, o, :], in0=x32[:, o, :], scalar=ALPHA, in1=ps[:, :], op0=ALU.mult, op1=ALU.add),
                    reads=[ps.b[0], x32.b[o]], writes=[x32.b[o]])
            ph.layernorm(0, 16)
            ph.ffn()
            ph.layernorm(32, 48)
            xr = ph.chunks(xb, 0, KC)
            pr = ph.chunks(pb, 0, 2)
            for o in range(KC):
                pg = ph.acc(xr)
                pp = ph.acc(pr)
                sg = ph.tmpf()
                t1 = ph.tmpf()
                ctx.op("act", lambda e, sg=sg, pg=pg: e.activation(out=sg[:, :], in_=pg[:, :], func=AF.Sigmoid),
                       reads=[pg.b[0]], writes=[sg.b[0]])
                ctx.op("dve", lambda e, sg=sg, pp=pp, t1=t1: e.tensor_tensor(out=t1[:, :], in0=pp[:, :], in1=sg[:, :], op=ALU.mult),
                       reads=[sg.b[0], pp.b[0]], writes=[t1.b[0]])
                ctx.op("dve", lambda e, t1=t1, o=o: e.scalar_tensor_tensor(
                    out=x32[:, o, :], in0=x32[:, o, :], scalar=ALPHA, in1=t1[:, :], op0=ALU.mult, op1=ALU.add),
                    reads=[t1.b[0], x32.b[o]], writes=[x32.b[o]])
            ph.layernorm(64, 80)
            ph.store_x(x4T, t0)
        ctx.emit()
    return nc


class EW:
    def __init__(self, ctx):
        self.ctx = ctx

    def tt(self, eng, out, a, b, op):
        self.ctx.op(eng, lambda e: e.tensor_tensor(out=out[0], in0=a[0], in1=b[0], op=op),
                    reads=a[1] + b[1], writes=out[1])

    def ts(self, eng, out, a, s1, op0, s2=None, op1=None, sreads=()):
        if op1 is None:
            self.ctx.op(eng, lambda e: e.tensor_scalar(out=out[0], in0=a[0], scalar1=s1, scalar2=None, op0=op0),
                        reads=a[1] + list(sreads), writes=out[1])
        else:
            self.ctx.op(eng, lambda e: e.tensor_scalar(out=out[0], in0=a[0], scalar1=s1, scalar2=s2, op0=op0, op1=op1),
                        reads=a[1] + list(sreads), writes=out[1])

    def stt(self, out, a, s, b, op0, op1, sreads=()):
        self.ctx.op("dve", lambda e: e.scalar_tensor_tensor(out=out[0], in0=a[0], scalar=s, in1=b[0], op0=op0, op1=op1),
                    reads=a[1] + b[1] + list(sreads), writes=out[1])

    def act(self, out, a, func, scale=None):
        if scale is None:
            self.ctx.op("act", lambda e: e.activation(out=out[0], in_=a[0], func=func), reads=a[1], writes=out[1])
        else:
            self.ctx.op("act", lambda e: e.activation(out=out[0], in_=a[0], func=func, scale=scale), reads=a[1], writes=out[1])

    def reduce(self, ang, kk, r, eng="dve"):
        PI_SAFE = 3.1415925
        self.ts(eng, kk, ang, 1.0 / TWO_PI, ALU.mult, MAGIC, ALU.add)
        self.ts(eng, kk, kk, MAGIC, ALU.subtract)
        self.stt(r, kk, -C1, ang, ALU.mult, ALU.add)
        self.stt(r, kk, -C2, r, ALU.mult, ALU.add)
        self.ts(eng, r, r, -PI_SAFE, ALU.max, PI_SAFE, ALU.min)

    def sincos(self, ang, kk, r, s_out, c_out):
        PI_SAFE = 3.1415925
        self.reduce(ang, kk, r)
        self.act(s_out, r, AF.Sin)
        self.ts("dve", ang, r, math.pi / 2, ALU.add)
        self.ts("dve", kk, ang, math.pi, ALU.is_gt, TWO_PI, ALU.mult)
        self.tt("dve", ang, ang, kk, ALU.subtract)
        self.ts("dve", ang, ang, -PI_SAFE, ALU.max, PI_SAFE, ALU.min)
        self.act(c_out, ang, AF.Sin)


def build_LC(L, do_attn=True, do_ssm=True):
    import contextlib
    nc = bass.Bass("TRN2", target_bir_lowering=False)
    CH = 512
    nchunk = L // CH
    dt = nc.dram_tensor
    qT = dt("qT", [128, L], BF16, kind="ExternalInput").ap()
    kT = dt("kT", [128, L], BF16, kind="ExternalInput").ap()
    vtm = dt("vtm", [L, 128], BF16, kind="ExternalInput").ap()
    lamv = dt("lamv", [128, 256], F32, kind="ExternalInput").ap()
    cst = dt("cst", [128, 3], F32, kind="ExternalInput").ap()
    tri = dt("tri", [128, 128], F32, kind="ExternalInput").ap()
    u4 = dt("u4", [32, 4, L], F32, kind="ExternalInput").ap()
    prow = dt("prow", [32, 5, 512], F32, kind="ExternalInput").ap()
    pcol = dt("pcol", [128, 3, 4], F32, kind="ExternalInput").ap()
    ccol = dt("ccol", [128, 2, 4, 32], F32, kind="ExternalInput").ap()
    dsk = dt("dsk", [32, 4], F32, kind="ExternalInput").ap()
    iota = dt("iota", [128, CH], F32, kind="ExternalInput").ap()
    aTo = dt("aTo", [128, L], BF16, kind="ExternalOutput").ap()
    zTo = dt("zTo", [128, L], BF16, kind="ExternalOutput").ap()
    ctx = Ctx(nc)
    ew = EW(ctx)
    with contextlib.ExitStack() as es:
        cur_es = [es]

        def T_(name, shape, dtype=F32, psum=False, nsub=1):
            return Tile(ctx, cur_es[0], name, shape, dtype, psum=psum, nsub=nsub)

        def A(t, idx=None):
            return (t[:, :] if idx is None else t[idx], [t.b[0]])
        ps = [T_(f"ps{i}", [128, 512], F32, psum=True) for i in range(8)]
        ones32 = T_("ones32", [128, 128])
        onesb = T_("onesb", [128, 128], BF16)
        ctx.op("pool", lambda e: e.memset(ones32[:, :], 1.0), writes=[ones32.b[0]])
        ctx.op("pool", lambda e: e.memset(onesb[:, :], 1.0), writes=[onesb.b[0]])
        tf = [T_(f"tf{i}", [128, 512]) for i in range(14)]
        tfi = [0]

        def tmpf():
            t = tf[tfi[0] % len(tf)]
            tfi[0] += 1
            return t
        tb = [T_(f"tb{i}", [128, 512], BF16) for i in range(8)]
        tbi = [0]

        def tmpb():
            t = tb[tbi[0] % len(tb)]
            tbi[0] += 1
            return t

        es_ssm = contextlib.ExitStack()
        cur_es[0] = es_ssm
        if do_ssm:
            pr = T_("prow", [32, 5, 512])
            pc = T_("pcol", [128, 3, 4])
            cc = T_("ccol", [128, 2, 4, 32])
            dk = T_("dsk", [32, 4])
            io = T_("iota", [128, CH])
            for t, src in ((pr, prow), (pc, pcol), (cc, ccol), (dk, dsk), (io, iota)):
                ctx.dma("sp", t[:], src, writes=[t.b[0]])
            rw = [T_(f"rw{i}", [32, 512]) for i in range(12)]
            R_ = lambda i: (rw[i][:, :], [rw[i].b[0]])
            P_ = lambda i: (pr[:, i, :], [pr.b[0]])
            bbre = T_("bbre", [32, 512], BF16)
            bbim = T_("bbim", [32, 512], BF16)
            ew.act(R_(0), P_(2), AF.Exp)
            ew.tt("dve", R_(1), P_(0), R_(0), ALU.mult)
            ew.act(R_(2), R_(1), AF.Exp)
            ew.tt("dve", R_(3), P_(1), R_(0), ALU.mult)
            ew.sincos(R_(3), R_(4), R_(5), R_(6), R_(7))
            ew.tt("dve", R_(8), R_(2), R_(7), ALU.mult)
            ew.tt("dve", R_(9), R_(2), R_(6), ALU.mult)
            ew.ts("dve", R_(8), R_(8), -1.0, ALU.add)
            ew.tt("dve", R_(0), P_(0), P_(0), ALU.mult)
            ew.tt("dve", R_(1), P_(1), P_(1), ALU.mult)
            ew.tt("dve", R_(0), R_(0), R_(1), ALU.add)
            ctx.op("dve", lambda e: e.reciprocal(out=rw[0][:, :], in_=rw[0][:, :]), reads=[rw[0].b[0]], writes=[rw[0].b[0]])
            ew.tt("dve", R_(1), R_(8), P_(0), ALU.mult)
            ew.tt("dve", R_(2), R_(9), P_(1), ALU.mult)
            ew.tt("dve", R_(1), R_(1), R_(2), ALU.add)
            ew.tt("dve", R_(10), R_(1), R_(0), ALU.mult)
            ew.tt("dve", R_(1), R_(9), P_(0), ALU.mult)
            ew.tt("dve", R_(2), R_(8), P_(1), ALU.mult)
            ew.tt("dve", R_(1), R_(1), R_(2), ALU.subtract)
            ew.tt("dve", R_(11), R_(1), R_(0), ALU.mult)
            ew.tt("dve", R_(1), R_(10), P_(3), ALU.mult)
            ew.tt("dve", R_(2), R_(11), P_(4), ALU.mult)
            ew.tt("dve", (bbre[:, :], [bbre.b[0]]), R_(1), R_(2), ALU.subtract)
            ew.tt("dve", R_(1), R_(10), P_(4), ALU.mult)
            ew.tt("dve", R_(2), R_(11), P_(3), ALU.mult)
            ew.tt("dve", (bbim[:, :], [bbim.b[0]]), R_(1), R_(2), ALU.add)
            cw = [T_(f"cw{i}", [128, 4]) for i in range(10)]
            Cw = lambda i: (cw[i][:, :], [cw[i].b[0]])
            Pc = lambda i: (pc[:, i, :], [pc.b[0]])
            ew.act(Cw(0), Pc(2), AF.Exp)
            ew.tt("dve", Cw(1), Pc(0), Cw(0), ALU.mult)
            ew.act(Cw(2), Cw(1), AF.Exp)
            ew.tt("dve", Cw(3), Pc(1), Cw(0), ALU.mult)
            ew.ts("dve", Cw(4), Cw(3), float(CH), ALU.mult)
            ew.sincos(Cw(4), Cw(5), Cw(6), Cw(7), Cw(8))
            mag, th, ci_, cr_ = cw[2], cw[3], cw[7], cw[8]
            dec = [T_(f"dec{p}", [128, CH]) for p in range(4)]
            cosT = [T_(f"cosT{p}", [128, CH]) for p in range(4)]
            sinT = [T_(f"sinT{p}", [128, CH]) for p in range(4)]
            a1, a2, a3 = T_("sa1", [128, CH]), T_("sa2", [128, CH]), T_("sa3", [128, CH])
            for p in range(4):
                ew.ts("dve", A(dec[p]), (io[:, :], [io.b[0]]), 0.0, ALU.mult, mag[:, p:p + 1], ALU.add, sreads=[mag.b[0]])
                ew.ts("dve", A(a1), (io[:, :], [io.b[0]]), th[:, p:p + 1], ALU.mult, sreads=[th.b[0]])
                ew.sincos(A(a1), A(a2), A(a3), A(sinT[p]), A(cosT[p]))
            crb = T_("crb", [128, 4, 32], BF16)
            ncib = T_("ncib", [128, 4, 32], BF16)
            ctx.op("act", lambda e: e.copy(out=crb[:, :, :], in_=cc[:, 0, :, :]), reads=[cc.b[0]], writes=[crb.b[0]])
            ctx.op("act", lambda e: e.mul(out=ncib[:, :, :], in_=cc[:, 1, :, :], mul=-1.0), reads=[cc.b[0]], writes=[ncib.b[0]])
            ir = [[T_(f"ir{p}_{k}", [128, 1]) for k in range(2)] for p in range(4)]
            ii = [[T_(f"ii{p}_{k}", [128, 1]) for k in range(2)] for p in range(4)]
            ctmp = [T_(f"ctmp{k}", [128, 1]) for k in range(4)]
            u32 = [T_(f"u32_{k}", [32, CH]) for k in range(3)]
            ubf = [T_(f"ubf_{k}", [32, CH], BF16) for k in range(3)]
            yst = [T_(f"yst{k}", [32, CH]) for k in range(2)]
            zst = [T_(f"zst{k}", [32, CH], BF16) for k in range(3)]
            step = 0
            for j in range(nchunk):
                t0 = j * CH
                for p in range(4):
                    uu, ub = u32[step % 3], ubf[step % 3]
                    ctx.dma("sp", uu[:, :], u4[:, p, t0:t0 + CH], writes=[uu.b[0]])
                    ctx.op("pool", lambda e, uu=uu, ub=ub: e.tensor_copy(out=ub[:, :], in_=uu[:, :]), reads=[uu.b[0]], writes=[ub.b[0]])
                    pxr, pxi, py = ps[(step % 2)], ps[2 + (step % 2)], ps[4 + (step % 2)]
                    ctx.op("pe", lambda e, pxr=pxr, ub=ub, p=p: e.matmul(pxr[:, :], bbre[:, p * 128:(p + 1) * 128], ub[:, :], start=True, stop=True),
                           reads=[bbre.b[0], ub.b[0]], writes=[pxr.b[0]])
                    ctx.op("pe", lambda e, pxi=pxi, ub=ub, p=p: e.matmul(pxi[:, :], bbim[:, p * 128:(p + 1) * 128], ub[:, :], start=True, stop=True),
                           reads=[bbim.b[0], ub.b[0]], writes=[pxi.b[0]])
                    xr, xi = tmpf(), tmpf()
                    ew.act(A(xr), A(pxr), AF.Copy)
                    ew.act(A(xi), A(pxi), AF.Copy)
                    m1, m2, m3, m4 = tmpf(), tmpf(), tmpf(), tmpf()
                    ew.tt("pool", A(m1), A(xr), A(cosT[p]), ALU.mult)
                    ew.tt("pool", A(m2), A(xi), A(sinT[p]), ALU.mult)
                    ew.tt("pool", A(m3), A(xi), A(cosT[p]), ALU.mult)
                    ew.tt("pool", A(m4), A(xr), A(sinT[p]), ALU.mult)
                    ew.tt("dve", A(m1), A(m1), A(m2), ALU.add)
                    ew.tt("dve", A(m3), A(m3), A(m4), ALU.subtract)
                    rr, ri = tmpf(), tmpf()
                    if j == 0:
                        ctx.op("dve", lambda e, rr=rr, m1=m1, p=p: e.tensor_tensor_scan(out=rr[:, :], data0=dec[p][:, :], data1=m1[:, :], initial=0.0, op0=ALU.mult, op1=ALU.add),
                               reads=[dec[p].b[0], m1.b[0]], writes=[rr.b[0]])
                        ctx.op("dve", lambda e, ri=ri, m3=m3, p=p: e.tensor_tensor_scan(out=ri[:, :], data0=dec[p][:, :], data1=m3[:, :], initial=0.0, op0=ALU.mult, op1=ALU.add),
                               reads=[dec[p].b[0], m3.b[0]], writes=[ri.b[0]])
                    else:
                        i_r, i_i = ir[p][j % 2], ii[p][j % 2]
                        ctx.op("dve", lambda e, rr=rr, m1=m1, p=p, i_r=i_r: e.tensor_tensor_scan(out=rr[:, :], data0=dec[p][:, :], data1=m1[:, :], initial=i_r[:, :], op0=ALU.mult, op1=ALU.add),
                               reads=[dec[p].b[0], m1.b[0], i_r.b[0]], writes=[rr.b[0]])
                        ctx.op("dve", lambda e, ri=ri, m3=m3, p=p, i_i=i_i: e.tensor_tensor_scan(out=ri[:, :], data0=dec[p][:, :], data1=m3[:, :], initial=i_i[:, :], op0=ALU.mult, op1=ALU.add),
                               reads=[dec[p].b[0], m3.b[0], i_i.b[0]], writes=[ri.b[0]])
                    if j + 1 < nchunk:
                        n_r, n_i = ir[p][(j + 1) % 2], ii[p][(j + 1) % 2]
                        rl = (rr[:, CH - 1:CH], [rr.b[0]])
                        il = (ri[:, CH - 1:CH], [ri.b[0]])
                        c0, c1 = ctmp[(2 * step) % 4], ctmp[(2 * step + 1) % 4]
                        ew.ts("dve", A(c0), il, ci_[:, p:p + 1], ALU.mult, sreads=[ci_.b[0]])
                        ew.stt(A(n_r), rl, cr_[:, p:p + 1], A(c0), ALU.mult, ALU.subtract, sreads=[cr_.b[0]])
                        ew.ts("dve", A(c1), il, cr_[:, p:p + 1], ALU.mult, sreads=[cr_.b[0]])
                        ew.stt(A(n_i), rl, ci_[:, p:p + 1], A(c1), ALU.mult, ALU.add, sreads=[ci_.b[0]])
                    d1, d2, d3, d4 = tmpf(), tmpf(), tmpf(), tmpf()
                    ew.tt("pool", A(d1), A(rr), A(cosT[p]), ALU.mult)
                    ew.tt("pool", A(d2), A(ri), A(sinT[p]), ALU.mult)
                    ew.tt("pool", A(d3), A(ri), A(cosT[p]), ALU.mult)
                    ew.tt("pool", A(d4), A(rr), A(sinT[p]), ALU.mult)
                    sr, si = tmpb(), tmpb()
                    ew.tt("dve", A(sr), A(d1), A(d2), ALU.subtract)
                    ew.tt("dve", A(si), A(d3), A(d4), ALU.add)
                    ctx.op_multi("pe", [
                        lambda e, py=py, sr=sr, p=p: e.matmul(py[0:32, :], crb[:, p, :], sr[:, :], start=True, stop=False),
                        lambda e, py=py, si=si, p=p: e.matmul(py[0:32, :], ncib[:, p, :], si[:, :], start=False, stop=True)],
                        reads=[crb.b[0], ncib.b[0], sr.b[0], si.b[0]], writes=[py.b[0]])
                    ys, zs = yst[step % 2], zst[step % 3]
                    ctx.op("dve", lambda e, ys=ys, uu=uu, py=py, p=p: e.scalar_tensor_tensor(
                        out=ys[:, :], in0=uu[:, :], scalar=dk[:, p:p + 1], in1=py[0:32, :], op0=ALU.mult, op1=ALU.add),
                        reads=[uu.b[0], py.b[0], dk.b[0]], writes=[ys.b[0]])
                    ctx.op("act", lambda e, ys=ys, zs=zs: e.activation(out=zs[:, :], in_=ys[:, :], func=AF.Gelu),
                           reads=[ys.b[0]], writes=[zs.b[0]])
                    ctx.dma("pool", zTo[p * 32:(p + 1) * 32, t0:t0 + CH], zs[:, :], reads=[zs.b[0]], is_output=True)
                    step += 1

        es_ssm.close()
        cur_es[0] = es
        ctx.fence("sp")
        if do_attn:
            qs = T_("qs", [128, L], BF16)
            ks = T_("ks", [128, L], BF16)
            vs = T_("vs", [128, L // 128, 128], BF16)
            ctx.dma("sp", qs[:, :], qT, writes=[qs.b[0]])
            ctx.dma("sp", ks[:, :], kT, writes=[ks.b[0]])
            ctx.dma("sp", vs[:, :, :], vtm.rearrange("(b p) d -> p b d", p=128), writes=[vs.b[0]])
            lv = T_("lamv", [128, 256])
            cs = T_("cst", [128, 3])
            tr32 = T_("tri32", [128, 128])
            trb = T_("trib", [128, 128], BF16)
            ctx.dma("sp", lv[:, :], lamv, writes=[lv.b[0]])
            ctx.dma("sp", cs[:, :], cst, writes=[cs.b[0]])
            ctx.dma("sp", tr32[:, :], tri, writes=[tr32.b[0]])
            ctx.op("dve", lambda e: e.tensor_copy(out=trb[:, :], in_=tr32[:, :]), reads=[tr32.b[0]], writes=[trb.b[0]])
            lt = [T_(f"lt{i}", [128, 64]) for i in range(2)]
            l1 = [T_(f"l1_{i}", [128, 1]) for i in range(6)]
            L1 = lambda i: (l1[i][:, :], [l1[i].b[0]])
            ew.tt("dve", A(lt[0]), (lv[:, 0:64], [lv.b[0]]), (lv[:, 64:128], [lv.b[0]]), ALU.mult)
            ew.tt("dve", A(lt[1]), (lv[:, 128:192], [lv.b[0]]), (lv[:, 192:256], [lv.b[0]]), ALU.mult)
            ctx.op("dve", lambda e: e.reduce_sum(out=l1[0][:, :], in_=lt[0][:, :], axis=mybir.AxisListType.X), reads=[lt[0].b[0]], writes=[l1[0].b[0]])
            ctx.op("dve", lambda e: e.reduce_sum(out=l1[1][:, :], in_=lt[1][:, :], axis=mybir.AxisListType.X), reads=[lt[1].b[0]], writes=[l1[1].b[0]])
            ew.act(L1(2), L1(0), AF.Exp)
            ew.act(L1(3), L1(1), AF.Exp)
            ew.tt("dve", L1(4), L1(3), L1(2), ALU.subtract)
            ew.tt("dve", L1(4), L1(4), (cs[:, 0:1], [cs.b[0]]), ALU.subtract)
            ew.tt("dve", L1(5), (cs[:, 1:2], [cs.b[0]]), (cs[:, 2:3], [cs.b[0]]), ALU.mult)
            nlam, gsc = l1[4], l1[5]
            pS = [[ps[0], ps[1]], [ps[2], ps[3]]]
            pO = [ps[4], ps[5]]
            pL = [ps[6], ps[7]]
            nq = L // 512
            sidx = 0
            for qt in range(nq):
                q0 = qt * 512
                nb = 4 * qt + 4
                pend = None

                def pv(item, first):
                    b, col0, P = item
                    for m in range(2):
                        ctx.op_multi("pe", [
                            lambda e, m=m, b=b, col0=col0, P=P, first=first, last=(b == nb - 1): e.matmul(pO[m][:, col0:512], vs[:, b, :], P[m][:, col0:512], start=first, stop=last),
                            lambda e, m=m, b=b, col0=col0, P=P, first=first, last=(b == nb - 1): e.matmul(pL[m][:, col0:512], onesb[:, :], P[m][:, col0:512], start=first, stop=last)],
                            reads=[vs.b[0], onesb.b[0], P[m].b[0]], writes=[pO[m].b[0], pL[m].b[0]])
                for b in range(nb):
                    d = b - 4 * qt
                    col0 = 128 * d if d > 0 else 0
                    P = [tmpb(), tmpb()]
                    for m in range(2):
                        S = pS[m][sidx % 2]
                        lo = 64 * m
                        ctx.op("pe", lambda e, S=S, lo=lo, b=b, col0=col0, q0=q0: e.matmul(
                            S[:, col0:512], ks[lo:lo + 64, b * 128:(b + 1) * 128], qs[lo:lo + 64, q0 + col0:q0 + 512], start=True, stop=True),
                            reads=[ks.b[0], qs.b[0]], writes=[S.b[0]])
                        ctx.op("act", lambda e, S=S, Pm=P[m], col0=col0: e.activation(out=Pm[:, col0:512], in_=S[:, col0:512], func=AF.Exp),
                               reads=[S.b[0]], writes=[P[m].b[0]])
                        if d >= 0:
                            ctx.op("pool", lambda e, Pm=P[m], col0=col0: e.tensor_tensor(
                                out=Pm[:, col0:col0 + 128], in0=Pm[:, col0:col0 + 128], in1=trb[:, :], op=ALU.mult),
                                reads=[P[m].b[0], trb.b[0]], writes=[P[m].b[0]])
                    sidx += 1
                    if pend is not None:
                        pv(pend, pend[0] == 0)
                    pend = (b, col0, P)
                pv(pend, pend[0] == 0)
                r1, r2, oa, ob = tmpf(), tmpf(), tmpf(), tmpf()
                ctx.op("dve", lambda e, r1=r1: e.reciprocal(out=r1[:, :], in_=pL[0][:, :]), reads=[pL[0].b[0]], writes=[r1.b[0]])
                ctx.op("dve", lambda e, r2=r2: e.reciprocal(out=r2[:, :], in_=pL[1][:, :]), reads=[pL[1].b[0]], writes=[r2.b[0]])
                ew.tt("dve", A(oa), A(pO[0]), A(r1), ALU.mult)
                ew.tt("dve", A(ob), A(pO[1]), A(r2), ALU.mult)
                o = tmpf()
                ew.stt(A(o), A(ob), nlam[:, 0:1], A(oa), ALU.mult, ALU.add, sreads=[nlam.b[0]])
                sq = tmpf()
                ew.act(A(sq), A(o), AF.Square)
                S = pS[0][sidx % 2]
                sidx += 1
                ctx.op("pe", lambda e, S=S, sq=sq: e.matmul(S[:, :], ones32[:, :], sq[:, :], start=True, stop=True),
                       reads=[ones32.b[0], sq.b[0]], writes=[S.b[0]])
                vr, sd = tmpf(), tmpf()
                ew.ts("dve", A(vr), A(S), 1.0 / 128.0, ALU.mult, RMS_EPS, ALU.add)
                ew.act(A(sd), A(vr), AF.Sqrt)
                ctx.op("dve", lambda e, vr=vr, sd=sd: e.reciprocal(out=vr[:, :], in_=sd[:, :]), reads=[sd.b[0]], writes=[vr.b[0]])
                yb = tmpb()
                ew.stt(A(yb), A(o), gsc[:, 0:1], A(vr), ALU.mult, ALU.mult, sreads=[gsc.b[0]])
                ctx.dma("pool", aTo[:, q0:q0 + 512], yb[:, :], reads=[yb.b[0]], is_output=True)
        ctx.emit()
    return nc


def ssm_params(core, a_re, a_im, b_re, b_im, c_re, c_im, log_dt, d):
    prow = np.zeros((32, 5, 512), np.float32)
    pcol = np.zeros((128, 3, 4), np.float32)
    ccol = np.zeros((128, 2, 4, 32), np.float32)
    dsk = np.zeros((32, 4), np.float32)
    for p in range(4):
        for s in range(2):
            g = 8 * core + 2 * p + s
            c0 = p * 128 + s * 64
            prow[:, 0, c0:c0 + 64] = a_re[g][None, :]
            prow[:, 1, c0:c0 + 64] = a_im[g][None, :]
            prow[:, 2, c0:c0 + 64] = log_dt[g]
            prow[s * 16:(s + 1) * 16, 3, c0:c0 + 64] = b_re[g].T
            prow[s * 16:(s + 1) * 16, 4, c0:c0 + 64] = b_im[g].T
            pcol[s * 64:(s + 1) * 64, 0, p] = a_re[g]
            pcol[s * 64:(s + 1) * 64, 1, p] = a_im[g]
            pcol[s * 64:(s + 1) * 64, 2, p] = log_dt[g]
            ccol[s * 64:(s + 1) * 64, 0, p, s * 16:(s + 1) * 16] = c_re[g].T
            ccol[s * 64:(s + 1) * 64, 1, p, s * 16:(s + 1) * 16] = c_im[g].T
            dsk[s * 16:(s + 1) * 16, p] = d[g * 16:(g + 1) * 16]
    return prow, pcol, ccol, dsk


def tri_const():
    k = np.arange(128)[:, None]
    q = np.arange(128)[None, :]
    return (k <= q).astype(np.float32)


def iota_const(n=512):
    return np.ascontiguousarray(np.broadcast_to(np.arange(n, dtype=np.float32)[None, :], (128, n)))


def _run(nc, in_maps):
    res = run_bass_kernel_spmd(nc, in_maps, core_ids=list(range(NCORES)))
    return res.results


def kernel(**inp):
    inp = {k: np.asarray(v) for k, v in inp.items()}
    L = SEQ
    T = L // NCORES
    f32 = np.float32
    units = []
    la_cols = ld_cols = None
    for i in range(DEPTH):
        ua = la_weights(inp["ffn1_w_gate"][i], inp["ffn1_w_up"][i], inp["ffn1_w_down"][i], inp["w_in"][i])
        ud = ld_weights(inp["ssm_w_glu"][i], inp["w_branch_ssm"][i], inp["w_branch_attn"][i], inp["w_out"][i],
                        inp["ffn2_w_gate"][i], inp["ffn2_w_up"][i], inp["ffn2_w_down"][i],
                        inp["ple_w_gate"][i], inp["ple_w_proj"][i])
        la_cols = sum(u.shape[1] for u in ua)
        ld_cols = sum(u.shape[1] for u in ud)
        units += ua + ud
    wall = np.concatenate(units, axis=1)
    del units
    tot = wall.shape[1]
    assert tot % NCORES == 0
    per = tot // NCORES
    nc0 = build_cast(per)
    r0 = _run(nc0, [{"x": np.ascontiguousarray(wall[:, c * per:(c + 1) * per])} for c in range(NCORES)])
    del wall
    wb = np.concatenate([r0[c]["y"] for c in range(NCORES)], axis=1)
    del r0
    lay = la_cols + ld_cols

    ncA = build_LA(T)
    ncC = build_LC(L)
    ncD = build_LD(T)
    perm = rope_perm()
    invf = rope_invf()[:, None]
    tri = tri_const()
    iota = iota_const()
    pos = inp["positions"][0].astype(np.int32)
    xT = [np.ascontiguousarray(inp["x"][0, c * T:(c + 1) * T, :].T) for c in range(NCORES)]
    posi = [np.ascontiguousarray(np.broadcast_to(pos[None, c * T:(c + 1) * T], (128, T))) for c in range(NCORES)]
    for i in range(DEPTH):
        lam_init = 0.8 - 0.6 * math.exp(-0.3 * i)
        wA = np.ascontiguousarray(wb[:, i * lay:i * lay + la_cols])
        vecA = np.concatenate([colvec(inp["ln1_g"][i]), colvec(inp["ln1_b"][i]), invf], axis=1).astype(f32)
        rA = _run(ncA, [{"xT": xT[c], "wts": wA, "vec": vecA, "perm": perm, "posi": posi[c]} for c in range(NCORES)])
        del wA
        lamv = np.concatenate([inp["lambda_q1"][i], inp["lambda_k1"][i], inp["lambda_q2"][i], inp["lambda_k2"][i]]).astype(f32)
        lamv = np.ascontiguousarray(np.broadcast_to(lamv[None, :], (128, 256)))
        cst = np.stack([np.full(128, lam_init, f32), np.full(128, 1.0 - lam_init, f32), inp["attn_subln_g"][i].astype(f32)], axis=1)
        mapsC = []
        for h in range(NCORES):
            sl = slice(h * 128, (h + 1) * 128)
            qh = np.concatenate([rA[c]["qT"][sl] for c in range(NCORES)], axis=1)
            kh = np.concatenate([rA[c]["kT"][sl] for c in range(NCORES)], axis=1)
            vh = np.ascontiguousarray(np.concatenate([rA[c]["vT"][sl] for c in range(NCORES)], axis=1).T)
            uh = np.concatenate([rA[c]["uT"][sl] for c in range(NCORES)], axis=1)
            u4 = np.ascontiguousarray(uh.reshape(4, 32, L).transpose(1, 0, 2))
            prow, pcol, ccol, dsk = ssm_params(h, inp["ssm_a_re"][i], inp["ssm_a_im"][i], inp["ssm_b_re"][i], inp["ssm_b_im"][i],
                                               inp["ssm_c_re"][i], inp["ssm_c_im"][i], inp["ssm_log_dt"][i], inp["ssm_d"][i])
            mapsC.append({"qT": qh, "kT": kh, "vtm": vh, "lamv": lamv, "cst": cst, "tri": tri, "u4": u4,
                          "prow": prow, "pcol": pcol, "ccol": ccol, "dsk": dsk, "iota": iota})
        rC = _run(ncC, mapsC)
        del mapsC
        wD = np.ascontiguousarray(wb[:, i * lay + la_cols:(i + 1) * lay])
        vecD = np.concatenate([colvec(inp["ln2_g"][i]), colvec(inp["ln2_b"][i]), colvec(inp["ln3_g"][i]), colvec(inp["ln3_b"][i]),
                               colvec(inp["ln4_g"][i]), colvec(inp["ln4_b"][i])], axis=1).astype(f32)
        mapsD = []
        for c in range(NCORES):
            ts_ = slice(c * T, (c + 1) * T)
            zT = np.ascontiguousarray(np.concatenate([rC[h]["zTo"][:, ts_] for h in range(NCORES)], axis=0))
            aT = np.ascontiguousarray(np.concatenate([rC[h]["aTo"][:, ts_] for h in range(NCORES)], axis=0))
            pT = np.ascontiguousarray(inp["p"][i, 0, ts_, :].T)
            mapsD.append({"x1T": rA[c]["x1T"], "zT": zT, "aT": aT, "sgT": rA[c]["sgT"], "pT": pT, "wts": wD, "vec": vecD})
        rD = _run(ncD, mapsD)
        del mapsD, rA, rC, wD
        xT = [rD[c]["x4T"] for c in range(NCORES)]
    out = np.concatenate([np.ascontiguousarray(xT[c].T) for c in range(NCORES)], axis=0)[None]
    return out.astype(np.float32)
```

```python
import math
import numpy as np
import ml_dtypes
import concourse.bass as bass
import concourse.mybir as mybir
from concourse.bass_utils import run_bass_kernel_spmd

F32 = mybir.dt.float32
BF16 = mybir.dt.bfloat16
I32 = mybir.dt.int32
AF = mybir.ActivationFunctionType
ALU = mybir.AluOpType
NPBF = ml_dtypes.bfloat16

D_MODEL = 2048
SEQ = 16384
DEPTH = 2
PLE_DIM = 256
D_FF = 5632
D_SSM = 1024
N_HEADS = 8
HEAD_DIM = 64
IN_WIDTH = 8192
ROT_DIM = 16
ROPE_THETA = 500000.0
LN_EPS = 1e-5
RMS_EPS = 1e-5
ALPHA = (2 * DEPTH) ** 0.25
NCORES = 8
TT = 512
KC = D_MODEL // 128
FC = D_FF // 128
MAGIC = 12582912.0
TWO_PI = 2.0 * math.pi
C1 = 6.28125
C2 = TWO_PI - 6.28125


class Buf:
    __slots__ = ("name", "w", "r")

    def __init__(self, name):
        self.name = name
        self.w = None
        self.r = []


class Ctx:
    ENG = ("pe", "act", "dve", "pool", "sp")

    def __init__(self, nc):
        self.nc = nc
        self.streams = {e: [] for e in self.ENG}
        self.cnt = {}
        self.waited = {e: {} for e in self.ENG}
        self.semkeys = []
        self.out_waits = []
        self.bufs = {}

    def buf(self, name):
        b = self.bufs.get(name)
        if b is None:
            b = self.bufs[name] = Buf(name)
        return b

    def _semkey(self, k):
        if not hasattr(self, "bind"):
            self.bind = {}
            self.free_phys = {}
        pid = self.bind.get(k)
        if pid is None:
            fl = self.free_phys.setdefault(k.split("_")[1], []) if k.startswith("d_") else None
            if fl:
                pid = fl.pop()
            else:
                pid = len(self.semkeys)
                self.semkeys.append(pid)
                self.cnt[pid] = 0
            self.bind[k] = pid
        return pid

    def release_dma_keys(self):
        for k in [k for k in getattr(self, "bind", {}) if k.startswith("d_")]:
            self.free_phys.setdefault(k.split("_")[1], []).append(self.bind.pop(k))

    def _deps(self, eng, reads, writes, own_key):
        deps = {}
        for b in reads:
            if b.w is not None:
                k, v = b.w
                deps[k] = max(deps.get(k, 0), v)
        for b in writes:
            if b.w is not None:
                k, v = b.w
                deps[k] = max(deps.get(k, 0), v)
            for k, v in b.r:
                deps[k] = max(deps.get(k, 0), v)
        out = []
        wt = self.waited[eng]
        for k, v in deps.items():
            if k == own_key and eng == "pe":
                continue
            if wt.get(k, 0) >= v:
                continue
            wt[k] = v
            out.append((k, v))
        return out

    def op(self, eng, fn, reads=(), writes=()):
        key = self._semkey("c_" + eng)
        waits = self._deps(eng, reads, writes, key)
        self.cnt[key] += 1
        val = self.cnt[key]
        self.streams[eng].append((waits, fn, key, 1))
        for b in writes:
            b.w = (key, val)
            b.r = []
        for b in reads:
            if b not in writes:
                b.r.append((key, val))
        return (key, val)

    def op_multi(self, eng, fns, reads=(), writes=()):
        key = self._semkey("c_" + eng)
        waits = self._deps(eng, reads, writes, key)
        self.cnt[key] += 1
        val = self.cnt[key]
        n = len(fns)
        for i, fn in enumerate(fns):
            self.streams[eng].append((waits if i == 0 else [], fn, key if i == n - 1 else None, 1))
        for b in writes:
            b.w = (key, val)
            b.r = []
        for b in reads:
            if b not in writes:
                b.r.append((key, val))
        return (key, val)

    def dma(self, eng, out, in_, reads=(), writes=(), semname=None, is_output=False):
        if semname is None:
            semname = (writes[0].name if writes else reads[0].name)
        key = self._semkey(f"d_{eng}_" + semname)
        waits = self._deps(eng, reads, writes, key)
        self.cnt[key] += 16
        val = self.cnt[key]

        def fn(e, out=out, in_=in_):
            return e.dma_start(out=out, in_=in_)
        self.streams[eng].append((waits, fn, key, 16))
        for b in writes:
            b.w = (key, val)
            b.r = []
        for b in reads:
            if b not in writes:
                b.r.append((key, val))
        if is_output:
            self.out_waits.append((key, val))
        return (key, val)

    def fence(self, eng="sp"):
        waits = []
        wt = self.waited[eng]
        skip = getattr(self, "cc_pids", ())
        for k in self.semkeys:
            if k in skip:
                continue
            v = self.cnt[k]
            if v > 0 and wt.get(k, 0) < v:
                wt[k] = v
                waits.append((k, v))
        self.streams[eng].append((waits, None, None, 0))

    def emit(self):
        nc = self.nc
        fin = {}
        for k, v in self.out_waits:
            fin[k] = max(fin.get(k, 0), v)
        import contextlib
        with contextlib.ExitStack() as es:
            sems = {}
            for k in self.semkeys:
                sems[k] = es.enter_context(nc.semaphore(f"sem{k}"))
            block = es.enter_context(nc.Block())
            streams = self.streams

            def run(e, name):
                for waits, fn, key, inc in streams[name]:
                    for k, v in waits:
                        e.wait_ge(sems[k], v)
                    if fn is None:
                        continue
                    ins = fn(e)
                    if key is not None:
                        if inc is None:
                            ins.then_inc(sems[key])
                        else:
                            ins.then_inc(sems[key], inc)
                if name == "sp":
                    for k, v in fin.items():
                        e.wait_ge(sems[k], v)

            @block.tensor
            def _(e):
                run(e, "pe")

            @block.scalar
            def _(e):
                run(e, "act")

            @block.vector
            def _(e):
                run(e, "dve")

            @block.gpsimd
            def _(e):
                run(e, "pool")

            @block.sync
            def _(e):
                run(e, "sp")


class Tile:
    def __init__(self, ctx, es, name, shape, dtype, psum=False, nsub=1):
        nc = ctx.nc
        name = name + getattr(ctx, "tag", "")
        if psum:
            self.t = es.enter_context(nc.psum_tensor("t_" + name, shape, dtype))
        else:
            self.t = es.enter_context(nc.sbuf_tensor("t_" + name, shape, dtype))
        self.b = [ctx.buf(f"{name}#{i}") for i in range(nsub)]
        self.name = name

    def __getitem__(self, idx):
        return self.t[idx]


def build_cast(ncols_total):
    import contextlib
    nc = bass.Bass("TRN2", target_bir_lowering=False)
    x = nc.dram_tensor("x", [128, ncols_total], F32, kind="ExternalInput").ap()
    y = nc.dram_tensor("y", [128, ncols_total], BF16, kind="ExternalOutput").ap()
    ctx = Ctx(nc)
    CB = 4096
    nblk = (ncols_total + CB - 1) // CB
    with contextlib.ExitStack() as es:
        NB = 3
        xin = [Tile(ctx, es, f"xin{i}", [128, CB], F32) for i in range(NB)]
        yo = [Tile(ctx, es, f"yo{i}", [128, CB], BF16) for i in range(NB)]
        engs = ["dve", "pool", "act"]
        def load(i):
            if i >= nblk:
                return
            c0 = i * CB
            cw = min(CB, ncols_total - c0)
            ctx.dma("sp", xin[i % NB][:, 0:cw], x[:, c0:c0 + cw], writes=[xin[i % NB].b[0]])
        for i in range(NB):
            load(i)
        for i in range(nblk):
            c0 = i * CB
            cw = min(CB, ncols_total - c0)
            xi, yi = xin[i % NB], yo[i % NB]
            eng = engs[i % 3]
            if eng == "act":
                ctx.op("act", lambda e, xi=xi, yi=yi, cw=cw: e.copy(out=yi[:, 0:cw], in_=xi[:, 0:cw]),
                       reads=[xi.b[0]], writes=[yi.b[0]])
            else:
                ctx.op(eng, lambda e, xi=xi, yi=yi, cw=cw: e.tensor_copy(out=yi[:, 0:cw], in_=xi[:, 0:cw]),
                       reads=[xi.b[0]], writes=[yi.b[0]])
            ctx.dma("sp", y[:, c0:c0 + cw], yi[:, 0:cw], reads=[yi.b[0]], is_output=True)
            load(i + NB)
        ctx.emit()
    return nc


def pretile(W):
    K, F = W.shape
    outs = []
    for fi in range(F // 128):
        for k0 in range(0, K, 2048):
            blk = W[k0:min(K, k0 + 2048), fi * 128:(fi + 1) * 128]
            n_c = blk.shape[0] // 128
            outs.append(blk.reshape(n_c, 128, 128).transpose(1, 0, 2).reshape(128, n_c * 128))
    return outs


def colvec(v):
    return np.ascontiguousarray(v.reshape(-1, 128).T)


class WStream:
    def __init__(self, ctx, es, wts_ap, nslab=8):
        self.ctx = ctx
        self.wts = wts_ap
        self.slabs = [Tile(ctx, es, f"slab{i}", [128, 2048], BF16) for i in range(nslab)]
        self.ns = nslab
        self.sched = []
        self.i = 0
        self.issued = 0

    def _issue(self, k):
        ent = self.sched[k]
        sl = self.slabs[k % self.ns]
        if len(ent) == 2:
            off, n = ent
            self.ctx.dma("sp", sl[:, 0:n], self.wts[:, off:off + n], writes=[sl.b[0]])
        else:
            ap, n, buf = ent
            self.ctx.dma("sp", sl[:, 0:n], ap, reads=[buf], writes=[sl.b[0]], semname=sl.b[0].name)

    def next(self, ncols):
        while self.issued < min(len(self.sched), self.i + self.ns):
            self._issue(self.issued)
            self.issued += 1
        n = self.sched[self.i][1]
        assert n == ncols, (self.i, n, ncols)
        sl = self.slabs[self.i % self.ns]
        self.i += 1
        return sl


class TokPhase:
    def __init__(self, nc, ctx, es, wts_ap, vec_ap, nvec, nslab=8, ntf=10):
        self.nc, self.ctx = nc, ctx
        self.ws = WStream(ctx, es, wts_ap, nslab=nslab)
        self.x32 = Tile(ctx, es, "x32", [128, KC, TT], F32, nsub=KC)
        self.xb = Tile(ctx, es, "xb", [128, KC, TT], BF16, nsub=KC)
        self.hT = Tile(ctx, es, "hT", [128, FC, TT], BF16, nsub=FC)
        self.ps = [Tile(ctx, es, f"ps{i}", [128, TT], F32, psum=True) for i in range(8)]
        self.psi = 0
        self.tf = [Tile(ctx, es, f"tf{i}", [128, TT], F32) for i in range(ntf)]
        self.tfi = 0
        self.lnt = [Tile(ctx, es, f"lnt{i}", [128, TT], F32) for i in range(5)]
        self.tb = [Tile(ctx, es, f"tb{i}", [128, TT], BF16) for i in range(6)]
        self.tbi = 0
        self.vec = Tile(ctx, es, "vec", [128, nvec], F32)
        self.ones32 = Tile(ctx, es, "ones32", [128, 128], F32)
        ctx.dma("sp", self.vec[:, :], vec_ap, writes=[self.vec.b[0]])
        ctx.op("pool", lambda e: e.memset(self.ones32[:, :], 1.0), writes=[self.ones32.b[0]])

    def next_ps(self):
        p = self.ps[self.psi % 8]
        self.psi += 1
        return p

    def tmpf(self):
        t = self.tf[self.tfi % len(self.tf)]
        self.tfi += 1
        return t

    def tmpb(self):
        t = self.tb[self.tbi % len(self.tb)]
        self.tbi += 1
        return t

    def acc(self, rhs):
        ctx = self.ctx
        ps = self.next_ps()
        n = len(rhs)
        k = 0
        while k < n:
            nk = min(16, n - k)
            sl = self.ws.next(nk * 128)
            fns = []
            for c in range(nk):
                ap = rhs[k + c][0]
                fns.append(lambda e, ps=ps, sl=sl, c=c, ap=ap, st=(k + c == 0), sp=(k + c == n - 1):
                           e.matmul(ps[:, :], sl[:, c * 128:(c + 1) * 128], ap, start=st, stop=sp))
            ctx.op_multi("pe", fns, reads=[sl.b[0]] + [rhs[k + c][1] for c in range(nk)], writes=[ps.b[0]])
            k += nk
        return ps

    @staticmethod
    def units_for(base, K, F):
        out = []
        for fi in range(F // 128):
            for k0 in range(0, K, 2048):
                n = (min(K, k0 + 2048) - k0)
                out.append((base, n))
                base += n
        return out, base

    def chunks(self, tile, lo, hi):
        return [(tile[:, c, :], tile.b[c]) for c in range(lo, hi)]

    def load_x(self, xT_ap, t0):
        ctx = self.ctx
        ctx.dma("sp", self.x32[:, :, :], xT_ap[:, t0:t0 + TT].rearrange("(c p) t -> p c t", p=128),
                writes=list(self.x32.b), semname="x32ld")
        for c in range(KC):
            eng = ("pool", "dve")[c % 2]
            ctx.op(eng, lambda e, c=c: e.tensor_copy(out=self.xb[:, c, :], in_=self.x32[:, c, :]),
                   reads=[self.x32.b[c]], writes=[self.xb.b[c]])

    def store_x(self, out_ap, t0, is_output=True):
        for c in range(KC):
            self.ctx.dma("pool", out_ap[c * 128:(c + 1) * 128, t0:t0 + TT], self.x32[:, c, :], reads=[self.x32.b[c]],
                         semname="x32st", is_output=is_output)

    def ffn(self):
        ctx = self.ctx
        xr = self.chunks(self.xb, 0, KC)
        for j in range(FC):
            pg = self.acc(xr)
            pu = self.acc(xr)
            sg = self.tmpf()
            ctx.op("act", lambda e, sg=sg, pg=pg: e.activation(out=sg[:, :], in_=pg[:, :], func=AF.Silu),
                   reads=[pg.b[0]], writes=[sg.b[0]])
            ctx.op("dve", lambda e, sg=sg, pu=pu, j=j: e.scalar_tensor_tensor(
                out=self.hT[:, j, :], in0=sg[:, :], scalar=0.5, in1=pu[:, :], op0=ALU.mult, op1=ALU.mult),
                reads=[sg.b[0], pu.b[0]], writes=[self.hT.b[j]])
        hr = self.chunks(self.hT, 0, FC)
        for m in range(KC):
            pd = self.acc(hr)
            ctx.op("dve", lambda e, pd=pd, m=m: e.scalar_tensor_tensor(
                out=self.x32[:, m, :], in0=self.x32[:, m, :], scalar=ALPHA, in1=pd[:, :], op0=ALU.mult, op1=ALU.add),
                reads=[pd.b[0], self.x32.b[m]], writes=[self.x32.b[m]])

    @staticmethod
    def ffn_units(base):
        out = []
        for j in range(FC):
            out.append((base, 2048)); base += 2048
            out.append((base, 2048)); base += 2048
        for m in range(KC):
            for n in (2048, 2048, 1536):
                out.append((base, n)); base += n
        return out, base

    def layernorm(self, gcol, bcol):
        ctx = self.ctx
        ps_sum = self.next_ps()
        ps_sq = self.next_ps()
        for c in range(KC):
            sq = self.tmpf()
            ctx.op("act", lambda e, sq=sq, c=c: e.activation(out=sq[:, :], in_=self.x32[:, c, :], func=AF.Square),
                   reads=[self.x32.b[c]], writes=[sq.b[0]])
            ctx.op("pe", lambda e, c=c: e.matmul(ps_sum[:, :], self.ones32[:, :], self.x32[:, c, :],
                                                 start=(c == 0), stop=(c == KC - 1)),
                   reads=[self.x32.b[c], self.ones32.b[0]], writes=[ps_sum.b[0]])
            ctx.op("pe", lambda e, c=c, sq=sq: e.matmul(ps_sq[:, :], self.ones32[:, :], sq[:, :],
                                                        start=(c == 0), stop=(c == KC - 1)),
                   reads=[sq.b[0], self.ones32.b[0]], writes=[ps_sq.b[0]])
        mean, msq, var, rstd, nmr = self.lnt
        inv = 1.0 / D_MODEL
        ctx.op("act", lambda e: e.mul(out=mean[:, :], in_=ps_sum[:, :], mul=inv), reads=[ps_sum.b[0]], writes=[mean.b[0]])
        ctx.op("dve", lambda e: e.tensor_tensor(out=msq[:, :], in0=mean[:, :], in1=mean[:, :], op=ALU.mult),
               reads=[mean.b[0]], writes=[msq.b[0]])
        ctx.op("dve", lambda e: e.scalar_tensor_tensor(out=var[:, :], in0=ps_sq[:, :], scalar=inv, in1=msq[:, :],
                                                       op0=ALU.mult, op1=ALU.subtract),
               reads=[ps_sq.b[0], msq.b[0]], writes=[var.b[0]])
        ctx.op("dve", lambda e: e.tensor_scalar(out=var[:, :], in0=var[:, :], scalar1=LN_EPS, scalar2=None, op0=ALU.add),
               reads=[var.b[0]], writes=[var.b[0]])
        ctx.op("act", lambda e: e.activation(out=msq[:, :], in_=var[:, :], func=AF.Sqrt), reads=[var.b[0]], writes=[msq.b[0]])
        ctx.op("dve", lambda e: e.reciprocal(out=rstd[:, :], in_=msq[:, :]), reads=[msq.b[0]], writes=[rstd.b[0]])
        ctx.op("dve", lambda e: e.scalar_tensor_tensor(out=nmr[:, :], in0=mean[:, :], scalar=-1.0, in1=rstd[:, :],
                                                       op0=ALU.mult, op1=ALU.mult),
               reads=[mean.b[0], rstd.b[0]], writes=[nmr.b[0]])
        for c in range(KC):
            t1 = self.tmpf()
            t2 = self.tmpf()
            ctx.op("dve", lambda e, c=c, t1=t1: e.tensor_tensor(out=t1[:, :], in0=self.x32[:, c, :], in1=rstd[:, :], op=ALU.mult),
                   reads=[self.x32.b[c], rstd.b[0]], writes=[t1.b[0]])
            ctx.op("pool", lambda e, t1=t1, t2=t2: e.tensor_tensor(out=t2[:, :], in0=t1[:, :], in1=nmr[:, :], op=ALU.add),
                   reads=[t1.b[0], nmr.b[0]], writes=[t2.b[0]])
            ctx.op("act", lambda e, c=c, t2=t2: e.activation(out=self.x32[:, c, :], in_=t2[:, :], func=AF.Identity,
                                                             scale=self.vec[:, gcol + c:gcol + c + 1],
                                                             bias=self.vec[:, bcol + c:bcol + c + 1]),
                   reads=[t2.b[0], self.vec.b[0]], writes=[self.x32.b[c]])
            ctx.op("pool", lambda e, c=c: e.tensor_copy(out=self.xb[:, c, :], in_=self.x32[:, c, :]),
                   reads=[self.x32.b[c]], writes=[self.xb.b[c]])
        return mean, var, rstd, nmr

    def rope_tables(self, es_tiles, posi_ap, t0, invf_col):
        ctx = self.ctx
        pi_t, posf, ang, kk, r, cosk, sink, cosq, sinq = es_tiles
        PI_SAFE = 3.1415925
        ctx.dma("sp", pi_t[:, :], posi_ap[:, t0:t0 + TT], writes=[pi_t.b[0]])
        ctx.op("dve", lambda e: e.tensor_copy(out=posf[:, :], in_=pi_t[:, :]), reads=[pi_t.b[0]], writes=[posf.b[0]])
        ctx.op("dve", lambda e: e.tensor_scalar(out=ang[:, :], in0=posf[:, :], scalar1=self.vec[:, invf_col:invf_col + 1],
                                                scalar2=None, op0=ALU.mult),
               reads=[posf.b[0], self.vec.b[0]], writes=[ang.b[0]])
        self.range_reduce(ang, kk, r)
        ctx.op("act", lambda e: e.activation(out=sink[:, :], in_=r[:, :], func=AF.Sin), reads=[r.b[0]], writes=[sink.b[0]])
        ctx.op("dve", lambda e: e.tensor_scalar(out=ang[:, :], in0=r[:, :], scalar1=math.pi / 2, scalar2=None, op0=ALU.add),
               reads=[r.b[0]], writes=[ang.b[0]])
        ctx.op("dve", lambda e: e.tensor_scalar(out=kk[:, :], in0=ang[:, :], scalar1=math.pi, scalar2=TWO_PI,
                                                op0=ALU.is_gt, op1=ALU.mult),
               reads=[ang.b[0]], writes=[kk.b[0]])
        ctx.op("dve", lambda e: e.tensor_tensor(out=ang[:, :], in0=ang[:, :], in1=kk[:, :], op=ALU.subtract),
               reads=[ang.b[0], kk.b[0]], writes=[ang.b[0]])
        ctx.op("dve", lambda e: e.tensor_scalar(out=ang[:, :], in0=ang[:, :], scalar1=-PI_SAFE, scalar2=PI_SAFE,
                                                op0=ALU.max, op1=ALU.min),
               reads=[ang.b[0]], writes=[ang.b[0]])
        ctx.op("act", lambda e: e.activation(out=cosk[:, :], in_=ang[:, :], func=AF.Sin), reads=[ang.b[0]], writes=[cosk.b[0]])
        sc = HEAD_DIM ** -0.5
        ctx.op("pool", lambda e: e.tensor_scalar(out=cosq[:, :], in0=cosk[:, :], scalar1=sc, scalar2=None, op0=ALU.mult),
               reads=[cosk.b[0]], writes=[cosq.b[0]])
        ctx.op("pool", lambda e: e.tensor_scalar(out=sinq[:, :], in0=sink[:, :], scalar1=sc, scalar2=None, op0=ALU.mult),
               reads=[sink.b[0]], writes=[sinq.b[0]])

    def range_reduce(self, ang, kk, r, eng="dve"):
        ctx = self.ctx
        PI_SAFE = 3.1415925
        ctx.op(eng, lambda e: e.tensor_scalar(out=kk[:, :], in0=ang[:, :], scalar1=1.0 / TWO_PI, scalar2=MAGIC,
                                              op0=ALU.mult, op1=ALU.add), reads=[ang.b[0]], writes=[kk.b[0]])
        ctx.op(eng, lambda e: e.tensor_scalar(out=kk[:, :], in0=kk[:, :], scalar1=MAGIC, scalar2=None, op0=ALU.subtract),
               reads=[kk.b[0]], writes=[kk.b[0]])
        ctx.op(eng, lambda e: e.scalar_tensor_tensor(out=r[:, :], in0=kk[:, :], scalar=-C1, in1=ang[:, :],
                                                     op0=ALU.mult, op1=ALU.add),
               reads=[kk.b[0], ang.b[0]], writes=[r.b[0]])
        ctx.op(eng, lambda e: e.scalar_tensor_tensor(out=r[:, :], in0=kk[:, :], scalar=-C2, in1=r[:, :],
                                                     op0=ALU.mult, op1=ALU.add),
               reads=[kk.b[0], r.b[0]], writes=[r.b[0]])
        ctx.op(eng, lambda e: e.tensor_scalar(out=r[:, :], in0=r[:, :], scalar1=-PI_SAFE, scalar2=PI_SAFE,
                                              op0=ALU.max, op1=ALU.min), reads=[r.b[0]], writes=[r.b[0]])


def rope_perm():
    P = np.zeros((128, 128), np.float32)
    for blk in (0, 64):
        for i in range(8):
            P[blk + i + 8, blk + i] = -1.0
            P[blk + i, blk + i + 8] = 1.0
    return P


def rope_invf():
    v = np.zeros((128,), np.float32)
    inv = ROPE_THETA ** (-np.arange(0, ROT_DIM, 2, dtype=np.float32) / ROT_DIM)
    for blk in (0, 64):
        for j in range(16):
            v[blk + j] = inv[j % 8]
    return v.astype(np.float32)


def build_LA(T, stages=(1, 1, 1)):
    import contextlib
    nc = bass.Bass("TRN2", target_bir_lowering=False)
    n_w = (3 * D_MODEL * D_FF + D_MODEL * IN_WIDTH) // 128
    xT = nc.dram_tensor("xT", [D_MODEL, T], F32, kind="ExternalInput").ap()
    wts = nc.dram_tensor("wts", [128, n_w], BF16, kind="ExternalInput").ap()
    vec = nc.dram_tensor("vec", [128, 33], F32, kind="ExternalInput").ap()
    perm = nc.dram_tensor("perm", [128, 128], F32, kind="ExternalInput").ap()
    posi = nc.dram_tensor("posi", [128, T], I32, kind="ExternalInput").ap()
    x1T = nc.dram_tensor("x1T", [D_MODEL, T], F32, kind="ExternalOutput").ap()
    uT = nc.dram_tensor("uT", [D_SSM, T], F32, kind="ExternalOutput").ap()
    qT = nc.dram_tensor("qT", [1024, T], BF16, kind="ExternalOutput").ap()
    kT = nc.dram_tensor("kT", [1024, T], BF16, kind="ExternalOutput").ap()
    vT = nc.dram_tensor("vT", [1024, T], BF16, kind="ExternalOutput").ap()
    sgT = nc.dram_tensor("sgT", [4096, T], BF16, kind="ExternalOutput").ap()
    ctx = Ctx(nc)
    with contextlib.ExitStack() as es:
        ph = TokPhase(nc, ctx, es, wts, vec, 33)
        permt = Tile(ctx, es, "permt", [128, 128], F32)
        ctx.dma("sp", permt[:, :], perm, writes=[permt.b[0]])
        rt_i = Tile(ctx, es, "rt_pi", [128, TT], I32)
        rt = [rt_i] + [Tile(ctx, es, f"rt{i}", [128, TT], F32) for i in range(8)]
        cosk, sink, cosq, sinq = rt[5], rt[6], rt[7], rt[8]
        sched = []
        ntile = T // TT
        for _ in range(ntile):
            base = 0
            u, base = TokPhase.ffn_units(base)
            sched += u
            u, base = TokPhase.units_for(base, D_MODEL, IN_WIDTH)
            sched += u
        ph.ws.sched = sched
        for ti in range(ntile):
            t0 = ti * TT
            ph.load_x(xT, t0)
            ph.rope_tables(rt, posi, t0, 32)
            if stages[0]:
                ph.ffn()
            else:
                ph.ws.i += 2 * FC + 3 * KC
                ph.ws.issued = ph.ws.i
            if stages[1]:
                dbg = ph.layernorm(0, 16)
                if len(stages) > 3:
                    for i, tl in enumerate(dbg):
                        ctx.dma("pool", uT[i * 128:(i + 1) * 128, t0:t0 + TT], tl[:, :], reads=[tl.b[0]], is_output=True)
            ph.store_x(x1T, t0)
            xr = ph.chunks(ph.xb, 0, KC)
            for o in range(IN_WIDTH // 128 if stages[2] else 0):
                ps = ph.acc(xr)
                if o < 8:
                    st = ph.tmpf()
                    ctx.op("act", lambda e, st=st, ps=ps: e.copy(out=st[:, :], in_=ps[:, :]), reads=[ps.b[0]], writes=[st.b[0]])
                    ctx.dma("pool", uT[o * 128:(o + 1) * 128, t0:t0 + TT], st[:, :], reads=[st.b[0]], is_output=True)
                elif o < 24:
                    isq = o < 16
                    ct, sn = (cosq, sinq) if isq else (cosk, sink)
                    dst = qT if isq else kT
                    h = o - 8 if isq else o - 16
                    qf = ph.tmpf()
                    ctx.op("act", lambda e, qf=qf, ps=ps: e.copy(out=qf[:, :], in_=ps[:, :]), reads=[ps.b[0]], writes=[qf.b[0]])
                    ps2 = ph.next_ps()
                    ctx.op("pe", lambda e, ps2=ps2, qf=qf: e.matmul(ps2[:, :], permt[:, :], qf[:, :], start=True, stop=True),
                           reads=[qf.b[0], permt.b[0]], writes=[ps2.b[0]])
                    t1 = ph.tmpf()
                    t2 = ph.tmpf()
                    ctx.op("dve", lambda e, t1=t1, qf=qf, ct=ct: e.tensor_tensor(out=t1[:, :], in0=qf[:, :], in1=ct[:, :], op=ALU.mult),
                           reads=[qf.b[0], ct.b[0]], writes=[t1.b[0]])
                    ctx.op("dve", lambda e, t2=t2, ps2=ps2, sn=sn: e.tensor_tensor(out=t2[:, :], in0=ps2[:, :], in1=sn[:, :], op=ALU.mult),
                           reads=[ps2.b[0], sn.b[0]], writes=[t2.b[0]])
                    sb = ph.tmpb()
                    ctx.op("pool", lambda e, sb=sb, t1=t1, t2=t2: e.tensor_tensor(out=sb[:, :], in0=t1[:, :], in1=t2[:, :], op=ALU.add),
                           reads=[t1.b[0], t2.b[0]], writes=[sb.b[0]])
                    ctx.dma("pool", dst[h * 128:(h + 1) * 128, t0:t0 + TT], sb[:, :], reads=[sb.b[0]], is_output=True)
                elif o < 32:
                    h = o - 24
                    sb = ph.tmpb()
                    ctx.op("act", lambda e, sb=sb, ps=ps: e.copy(out=sb[:, :], in_=ps[:, :]), reads=[ps.b[0]], writes=[sb.b[0]])
                    ctx.dma("pool", vT[h * 128:(h + 1) * 128, t0:t0 + TT], sb[:, :], reads=[sb.b[0]], is_output=True)
                else:
                    g = o - 32
                    sb = ph.tmpb()
                    ctx.op("act", lambda e, sb=sb, ps=ps: e.activation(out=sb[:, :], in_=ps[:, :], func=AF.Sigmoid),
                           reads=[ps.b[0]], writes=[sb.b[0]])
                    ctx.dma("pool", sgT[g * 128:(g + 1) * 128, t0:t0 + TT], sb[:, :], reads=[sb.b[0]], is_output=True)
        ctx.emit()
    return nc


def la_weights(ffn_g, ffn_u, ffn_d, w_in):
    g = pretile(ffn_g)
    u = pretile(ffn_u)
    d = pretile(ffn_d)
    out = []
    for j in range(FC):
        out.append(g[j]); out.append(u[j])
    out += d
    out += pretile(w_in)
    return out


def ld_weights(w_glu, w_a, w_b, w_out, ffn_g, ffn_u, ffn_d, ple_g, ple_p):
    out = []
    out += pretile(w_glu)
    a = pretile(w_a)
    b = pretile(w_b)
    for o in range(KC):
        out.append(a[o]); out.append(b[o])
    out += pretile(w_out)
    g = pretile(ffn_g); u = pretile(ffn_u); d = pretile(ffn_d)
    for j in range(FC):
        out.append(g[j]); out.append(u[j])
    out += d
    pg = pretile(ple_g); pp = pretile(ple_p)
    for o in range(KC):
        out.append(pg[o]); out.append(pp[o])
    return out


def ld_sched():
    s = []
    base = 0
    for o in range(8):
        s.append((base, 1024)); base += 1024
    for o in range(KC):
        s.append((base, 1024)); base += 1024
        s.append((base, 1024)); base += 1024
    for o in range(KC):
        s.append((base, 2048)); base += 2048
    u, base = TokPhase.ffn_units(base)
    s += u
    for o in range(KC):
        s.append((base, 2048)); base += 2048
        s.append((base, 256)); base += 256
    return s, base


def build_LD(T):
    import contextlib
    nc = bass.Bass("TRN2", target_bir_lowering=False)
    sched1, n_w = ld_sched()
    x1T = nc.dram_tensor("x1T", [D_MODEL, T], F32, kind="ExternalInput").ap()
    zT = nc.dram_tensor("zT", [D_SSM, T], BF16, kind="ExternalInput").ap()
    aT = nc.dram_tensor("aT", [1024, T], BF16, kind="ExternalInput").ap()
    sgT = nc.dram_tensor("sgT", [4096, T], BF16, kind="ExternalInput").ap()
    pT = nc.dram_tensor("pT", [PLE_DIM, T], F32, kind="ExternalInput").ap()
    wts = nc.dram_tensor("wts", [128, n_w], BF16, kind="ExternalInput").ap()
    vec = nc.dram_tensor("vec", [128, 96], F32, kind="ExternalInput").ap()
    x4T = nc.dram_tensor("x4T", [D_MODEL, T], F32, kind="ExternalOutput").ap()
    ctx = Ctx(nc)
    with contextlib.ExitStack() as es:
        ph = TokPhase(nc, ctx, es, wts, vec, 96)
        zb = Tile(ctx, es, "zb", [128, 8, TT], BF16, nsub=8)
        ab = Tile(ctx, es, "ab", [128, 8, TT], BF16, nsub=8)
        zz = Tile(ctx, es, "zz", [128, 8, TT], BF16, nsub=8)
        p32 = Tile(ctx, es, "p32", [128, 2, TT], F32, nsub=2)
        pb = Tile(ctx, es, "pb", [128, 2, TT], BF16, nsub=2)
        ntile = T // TT
        ph.ws.sched = sched1 * ntile
        hT, x32, xb = ph.hT, ph.x32, ph.xb
        for ti in range(ntile):
            t0 = ti * TT
            ctx.dma("sp", x32[:, :, :], x1T[:, t0:t0 + TT].rearrange("(c p) t -> p c t", p=128), writes=list(x32.b), semname="x32ld")
            ctx.dma("sp", zb[:, :, :], zT[:, t0:t0 + TT].rearrange("(c p) t -> p c t", p=128), writes=list(zb.b), semname="zbld")
            ctx.dma("sp", ab[:, :, :], aT[:, t0:t0 + TT].rearrange("(c p) t -> p c t", p=128), writes=list(ab.b), semname="abld")
            ctx.dma("sp", hT[:, 0:32, :], sgT[:, t0:t0 + TT].rearrange("(c p) t -> p c t", p=128), writes=list(hT.b[0:32]), semname="sgld")
            ctx.dma("sp", p32[:, :, :], pT[:, t0:t0 + TT].rearrange("(c p) t -> p c t", p=128), writes=list(p32.b), semname="pld")
            for c in range(2):
                ctx.op("pool", lambda e, c=c: e.tensor_copy(out=pb[:, c, :], in_=p32[:, c, :]), reads=[p32.b[c]], writes=[pb.b[c]])
            zr = ph.chunks(zb, 0, 8)
            for o in range(8):
                ps = ph.acc(zr)
                sg = ph.tmpf()
                ctx.op("act", lambda e, sg=sg, ps=ps: e.activation(out=sg[:, :], in_=ps[:, :], func=AF.Sigmoid),
                       reads=[ps.b[0]], writes=[sg.b[0]])
                ctx.op("dve", lambda e, sg=sg, o=o: e.tensor_tensor(out=zz[:, o, :], in0=zb[:, o, :], in1=sg[:, :], op=ALU.mult),
                       reads=[sg.b[0], zb.b[o]], writes=[zz.b[o]])
            zzr = ph.chunks(zz, 0, 8)
            ar = ph.chunks(ab, 0, 8)
            for o in range(KC):
                pa = ph.acc(zzr)
                pbb = ph.acc(ar)
                t1 = ph.tmpf()
                t2 = ph.tmpf()
                ctx.op("dve", lambda e, t1=t1, pa=pa, o=o: e.tensor_tensor(out=t1[:, :], in0=pa[:, :], in1=hT[:, o, :], op=ALU.mult),
                       reads=[pa.b[0], hT.b[o]], writes=[t1.b[0]])
                ctx.op("dve", lambda e, t2=t2, pbb=pbb, o=o: e.tensor_tensor(out=t2[:, :], in0=pbb[:, :], in1=hT[:, 16 + o, :], op=ALU.mult),
                       reads=[pbb.b[0], hT.b[16 + o]], writes=[t2.b[0]])
                ctx.op("pool", lambda e, t1=t1, t2=t2, o=o: e.tensor_tensor(out=xb[:, o, :], in0=t1[:, :], in1=t2[:, :], op=ALU.add),
                       reads=[t1.b[0], t2.b[0]], writes=[xb.b[o]])
            mr = ph.chunks(xb, 0, KC)
            for o in range(KC):
                ps = ph.acc(mr)
                ctx.op("dve", lambda e, ps=ps, o=o: e.scalar_tensor_tensor(
                    out=x32[:, o, :], in0=x32[:, o, :], scalar=ALPHA, in1=ps[:, :], op0=ALU.mult, op1=ALU.add),
                    reads=[ps.b[0], x32.b[o]], writes=[x32.b[o]])
            ph.layernorm(0, 16)
            ph.ffn()
            ph.layernorm(32, 48)
            xr = ph.chunks(xb, 0, KC)
            pr = ph.chunks(pb, 0, 2)
            for o in range(KC):
                pg = ph.acc(xr)
                pp = ph.acc(pr)
                sg = ph.tmpf()
                t1 = ph.tmpf()
                ctx.op("act", lambda e, sg=sg, pg=pg: e.activation(out=sg[:, :], in_=pg[:, :], func=AF.Sigmoid),
                       reads=[pg.b[0]], writes=[sg.b[0]])
                ctx.op("dve", lambda e, sg=sg, pp=pp, t1=t1: e.tensor_tensor(out=t1[:, :], in0=pp[:, :], in1=sg[:, :], op=ALU.mult),
                       reads=[sg.b[0], pp.b[0]], writes=[t1.b[0]])
                ctx.op("dve", lambda e, t1=t1, o=o: e.scalar_tensor_tensor(
                    out=x32[:, o, :], in0=x32[:, o, :], scalar=ALPHA, in1=t1[:, :], op0=ALU.mult, op1=ALU.add),
                    reads=[t1.b[0], x32.b[o]], writes=[x32.b[o]])
            ph.layernorm(64, 80)
            ph.store_x(x4T, t0)
        ctx.emit()
    return nc


class EW:
    def __init__(self, ctx):
        self.ctx = ctx

    def tt(self, eng, out, a, b, op):
        self.ctx.op(eng, lambda e: e.tensor_tensor(out=out[0], in0=a[0], in1=b[0], op=op),
                    reads=a[1] + b[1], writes=out[1])

    def ts(self, eng, out, a, s1, op0, s2=None, op1=None, sreads=()):
        if op1 is None:
            self.ctx.op(eng, lambda e: e.tensor_scalar(out=out[0], in0=a[0], scalar1=s1, scalar2=None, op0=op0),
                        reads=a[1] + list(sreads), writes=out[1])
        else:
            self.ctx.op(eng, lambda e: e.tensor_scalar(out=out[0], in0=a[0], scalar1=s1, scalar2=s2, op0=op0, op1=op1),
                        reads=a[1] + list(sreads), writes=out[1])

    def stt(self, out, a, s, b, op0, op1, sreads=()):
        self.ctx.op("dve", lambda e: e.scalar_tensor_tensor(out=out[0], in0=a[0], scalar=s, in1=b[0], op0=op0, op1=op1),
                    reads=a[1] + b[1] + list(sreads), writes=out[1])

    def act(self, out, a, func, scale=None):
        if scale is None:
            self.ctx.op("act", lambda e: e.activation(out=out[0], in_=a[0], func=func), reads=a[1], writes=out[1])
        else:
            self.ctx.op("act", lambda e: e.activation(out=out[0], in_=a[0], func=func, scale=scale), reads=a[1], writes=out[1])

    def reduce(self, ang, kk, r, eng="dve"):
        PI_SAFE = 3.1415925
        self.ts(eng, kk, ang, 1.0 / TWO_PI, ALU.mult, MAGIC, ALU.add)
        self.ts(eng, kk, kk, MAGIC, ALU.subtract)
        self.stt(r, kk, -C1, ang, ALU.mult, ALU.add)
        self.stt(r, kk, -C2, r, ALU.mult, ALU.add)
        self.ts(eng, r, r, -PI_SAFE, ALU.max, PI_SAFE, ALU.min)

    def sincos(self, ang, kk, r, s_out, c_out):
        PI_SAFE = 3.1415925
        self.reduce(ang, kk, r)
        self.act(s_out, r, AF.Sin)
        self.ts("dve", ang, r, math.pi / 2, ALU.add)
        self.ts("dve", kk, ang, math.pi, ALU.is_gt, TWO_PI, ALU.mult)
        self.tt("dve", ang, ang, kk, ALU.subtract)
        self.ts("dve", ang, ang, -PI_SAFE, ALU.max, PI_SAFE, ALU.min)
        self.act(c_out, ang, AF.Sin)


def build_LC(L, do_attn=True, do_ssm=True):
    import contextlib
    nc = bass.Bass("TRN2", target_bir_lowering=False)
    CH = 512
    nchunk = L // CH
    dt = nc.dram_tensor
    qT = dt("qT", [128, L], BF16, kind="ExternalInput").ap()
    kT = dt("kT", [128, L], BF16, kind="ExternalInput").ap()
    vtm = dt("vtm", [L, 128], BF16, kind="ExternalInput").ap()
    lamv = dt("lamv", [128, 256], F32, kind="ExternalInput").ap()
    cst = dt("cst", [128, 3], F32, kind="ExternalInput").ap()
    tri = dt("tri", [128, 128], F32, kind="ExternalInput").ap()
    u4 = dt("u4", [32, 4, L], F32, kind="ExternalInput").ap()
    prow = dt("prow", [32, 5, 512], F32, kind="ExternalInput").ap()
    pcol = dt("pcol", [128, 3, 4], F32, kind="ExternalInput").ap()
    ccol = dt("ccol", [128, 2, 4, 32], F32, kind="ExternalInput").ap()
    dsk = dt("dsk", [32, 4], F32, kind="ExternalInput").ap()
    iota = dt("iota", [128, CH], F32, kind="ExternalInput").ap()
    aTo = dt("aTo", [128, L], BF16, kind="ExternalOutput").ap()
    zTo = dt("zTo", [128, L], BF16, kind="ExternalOutput").ap()
    ctx = Ctx(nc)
    ew = EW(ctx)
    with contextlib.ExitStack() as es:
        cur_es = [es]

        def T_(name, shape, dtype=F32, psum=False, nsub=1):
            return Tile(ctx, cur_es[0], name, shape, dtype, psum=psum, nsub=nsub)

        def A(t, idx=None):
            return (t[:, :] if idx is None else t[idx], [t.b[0]])
        ps = [T_(f"ps{i}", [128, 512], F32, psum=True) for i in range(8)]
        ones32 = T_("ones32", [128, 128])
        onesb = T_("onesb", [128, 128], BF16)
        ctx.op("pool", lambda e: e.memset(ones32[:, :], 1.0), writes=[ones32.b[0]])
        ctx.op("pool", lambda e: e.memset(onesb[:, :], 1.0), writes=[onesb.b[0]])
        tf = [T_(f"tf{i}", [128, 512]) for i in range(14)]
        tfi = [0]

        def tmpf():
            t = tf[tfi[0] % len(tf)]
            tfi[0] += 1
            return t
        tb = [T_(f"tb{i}", [128, 512], BF16) for i in range(8)]
        tbi = [0]

        def tmpb():
            t = tb[tbi[0] % len(tb)]
            tbi[0] += 1
            return t

        es_ssm = contextlib.ExitStack()
        cur_es[0] = es_ssm
        if do_ssm:
            pr = T_("prow", [32, 5, 512])
            pc = T_("pcol", [128, 3, 4])
            cc = T_("ccol", [128, 2, 4, 32])
            dk = T_("dsk", [32, 4])
            io = T_("iota", [128, CH])
            for t, src in ((pr, prow), (pc, pcol), (cc, ccol), (dk, dsk), (io, iota)):
                ctx.dma("sp", t[:], src, writes=[t.b[0]])
            rw = [T_(f"rw{i}", [32, 512]) for i in range(12)]
            R_ = lambda i: (rw[i][:, :], [rw[i].b[0]])
            P_ = lambda i: (pr[:, i, :], [pr.b[0]])
            bbre = T_("bbre", [32, 512], BF16)
            bbim = T_("bbim", [32, 512], BF16)
            ew.act(R_(0), P_(2), AF.Exp)
            ew.tt("dve", R_(1), P_(0), R_(0), ALU.mult)
            ew.act(R_(2), R_(1), AF.Exp)
            ew.tt("dve", R_(3), P_(1), R_(0), ALU.mult)
            ew.sincos(R_(3), R_(4), R_(5), R_(6), R_(7))
            ew.tt("dve", R_(8), R_(2), R_(7), ALU.mult)
            ew.tt("dve", R_(9), R_(2), R_(6), ALU.mult)
            ew.ts("dve", R_(8), R_(8), -1.0, ALU.add)
            ew.tt("dve", R_(0), P_(0), P_(0), ALU.mult)
            ew.tt("dve", R_(1), P_(1), P_(1), ALU.mult)
            ew.tt("dve", R_(0), R_(0), R_(1), ALU.add)
            ctx.op("dve", lambda e: e.reciprocal(out=rw[0][:, :], in_=rw[0][:, :]), reads=[rw[0].b[0]], writes=[rw[0].b[0]])
            ew.tt("dve", R_(1), R_(8), P_(0), ALU.mult)
            ew.tt("dve", R_(2), R_(9), P_(1), ALU.mult)
            ew.tt("dve", R_(1), R_(1), R_(2), ALU.add)
            ew.tt("dve", R_(10), R_(1), R_(0), ALU.mult)
            ew.tt("dve", R_(1), R_(9), P_(0), ALU.mult)
            ew.tt("dve", R_(2), R_(8), P_(1), ALU.mult)
            ew.tt("dve", R_(1), R_(1), R_(2), ALU.subtract)
            ew.tt("dve", R_(11), R_(1), R_(0), ALU.mult)
            ew.tt("dve", R_(1), R_(10), P_(3), ALU.mult)
            ew.tt("dve", R_(2), R_(11), P_(4), ALU.mult)
            ew.tt("dve", (bbre[:, :], [bbre.b[0]]), R_(1), R_(2), ALU.subtract)
            ew.tt("dve", R_(1), R_(10), P_(4), ALU.mult)
            ew.tt("dve", R_(2), R_(11), P_(3), ALU.mult)
            ew.tt("dve", (bbim[:, :], [bbim.b[0]]), R_(1), R_(2), ALU.add)
            cw = [T_(f"cw{i}", [128, 4]) for i in range(10)]
            Cw = lambda i: (cw[i][:, :], [cw[i].b[0]])
            Pc = lambda i: (pc[:, i, :], [pc.b[0]])
            ew.act(Cw(0), Pc(2), AF.Exp)
            ew.tt("dve", Cw(1), Pc(0), Cw(0), ALU.mult)
            ew.act(Cw(2), Cw(1), AF.Exp)
            ew.tt("dve", Cw(3), Pc(1), Cw(0), ALU.mult)
            ew.ts("dve", Cw(4), Cw(3), float(CH), ALU.mult)
            ew.sincos(Cw(4), Cw(5), Cw(6), Cw(7), Cw(8))
            mag, th, ci_, cr_ = cw[2], cw[3], cw[7], cw[8]
            dec = [T_(f"dec{p}", [128, CH]) for p in range(4)]
            cosT = [T_(f"cosT{p}", [128, CH]) for p in range(4)]
            sinT = [T_(f"sinT{p}", [128, CH]) for p in range(4)]
            a1, a2, a3 = T_("sa1", [128, CH]), T_("sa2", [128, CH]), T_("sa3", [128, CH])
            for p in range(4):
                ew.ts("dve", A(dec[p]), (io[:, :], [io.b[0]]), 0.0, ALU.mult, mag[:, p:p + 1], ALU.add, sreads=[mag.b[0]])
                ew.ts("dve", A(a1), (io[:, :], [io.b[0]]), th[:, p:p + 1], ALU.mult, sreads=[th.b[0]])
                ew.sincos(A(a1), A(a2), A(a3), A(sinT[p]), A(cosT[p]))
            crb = T_("crb", [128, 4, 32], BF16)
            ncib = T_("ncib", [128, 4, 32], BF16)
            ctx.op("act", lambda e: e.copy(out=crb[:, :, :], in_=cc[:, 0, :, :]), reads=[cc.b[0]], writes=[crb.b[0]])
            ctx.op("act", lambda e: e.mul(out=ncib[:, :, :], in_=cc[:, 1, :, :], mul=-1.0), reads=[cc.b[0]], writes=[ncib.b[0]])
            ir = [[T_(f"ir{p}_{k}", [128, 1]) for k in range(2)] for p in range(4)]
            ii = [[T_(f"ii{p}_{k}", [128, 1]) for k in range(2)] for p in range(4)]
            ctmp = [T_(f"ctmp{k}", [128, 1]) for k in range(4)]
            u32 = [T_(f"u32_{k}", [32, CH]) for k in range(3)]
            ubf = [T_(f"ubf_{k}", [32, CH], BF16) for k in range(3)]
            yst = [T_(f"yst{k}", [32, CH]) for k in range(2)]
            zst = [T_(f"zst{k}", [32, CH], BF16) for k in range(3)]
            step = 0
            for j in range(nchunk):
                t0 = j * CH
                for p in range(4):
                    uu, ub = u32[step % 3], ubf[step % 3]
                    ctx.dma("sp", uu[:, :], u4[:, p, t0:t0 + CH], writes=[uu.b[0]])
                    ctx.op("pool", lambda e, uu=uu, ub=ub: e.tensor_copy(out=ub[:, :], in_=uu[:, :]), reads=[uu.b[0]], writes=[ub.b[0]])
                    pxr, pxi, py = ps[(step % 2)], ps[2 + (step % 2)], ps[4 + (step % 2)]
                    ctx.op("pe", lambda e, pxr=pxr, ub=ub, p=p: e.matmul(pxr[:, :], bbre[:, p * 128:(p + 1) * 128], ub[:, :], start=True, stop=True),
                           reads=[bbre.b[0], ub.b[0]], writes=[pxr.b[0]])
                    ctx.op("pe", lambda e, pxi=pxi, ub=ub, p=p: e.matmul(pxi[:, :], bbim[:, p * 128:(p + 1) * 128], ub[:, :], start=True, stop=True),
                           reads=[bbim.b[0], ub.b[0]], writes=[pxi.b[0]])
                    xr, xi = tmpf(), tmpf()
                    ew.act(A(xr), A(pxr), AF.Copy)
                    ew.act(A(xi), A(pxi), AF.Copy)
                    m1, m2, m3, m4 = tmpf(), tmpf(), tmpf(), tmpf()
                    ew.tt("pool", A(m1), A(xr), A(cosT[p]), ALU.mult)
                    ew.tt("pool", A(m2), A(xi), A(sinT[p]), ALU.mult)
                    ew.tt("pool", A(m3), A(xi), A(cosT[p]), ALU.mult)
                    ew.tt("pool", A(m4), A(xr), A(sinT[p]), ALU.mult)
                    ew.tt("dve", A(m1), A(m1), A(m2), ALU.add)
                    ew.tt("dve", A(m3), A(m3), A(m4), ALU.subtract)
                    rr, ri = tmpf(), tmpf()
                    if j == 0:
                        ctx.op("dve", lambda e, rr=rr, m1=m1, p=p: e.tensor_tensor_scan(out=rr[:, :], data0=dec[p][:, :], data1=m1[:, :], initial=0.0, op0=ALU.mult, op1=ALU.add),
                               reads=[dec[p].b[0], m1.b[0]], writes=[rr.b[0]])
                        ctx.op("dve", lambda e, ri=ri, m3=m3, p=p: e.tensor_tensor_scan(out=ri[:, :], data0=dec[p][:, :], data1=m3[:, :], initial=0.0, op0=ALU.mult, op1=ALU.add),
                               reads=[dec[p].b[0], m3.b[0]], writes=[ri.b[0]])
                    else:
                        i_r, i_i = ir[p][j % 2], ii[p][j % 2]
                        ctx.op("dve", lambda e, rr=rr, m1=m1, p=p, i_r=i_r: e.tensor_tensor_scan(out=rr[:, :], data0=dec[p][:, :], data1=m1[:, :], initial=i_r[:, :], op0=ALU.mult, op1=ALU.add),
                               reads=[dec[p].b[0], m1.b[0], i_r.b[0]], writes=[rr.b[0]])
                        ctx.op("dve", lambda e, ri=ri, m3=m3, p=p, i_i=i_i: e.tensor_tensor_scan(out=ri[:, :], data0=dec[p][:, :], data1=m3[:, :], initial=i_i[:, :], op0=ALU.mult, op1=ALU.add),
                               reads=[dec[p].b[0], m3.b[0], i_i.b[0]], writes=[ri.b[0]])
                    if j + 1 < nchunk:
                        n_r, n_i = ir[p][(j + 1) % 2], ii[p][(j + 1) % 2]
                        rl = (rr[:, CH - 1:CH], [rr.b[0]])
                        il = (ri[:, CH - 1:CH], [ri.b[0]])
                        c0, c1 = ctmp[(2 * step) % 4], ctmp[(2 * step + 1) % 4]
                        ew.ts("dve", A(c0), il, ci_[:, p:p + 1], ALU.mult, sreads=[ci_.b[0]])
                        ew.stt(A(n_r), rl, cr_[:, p:p + 1], A(c0), ALU.mult, ALU.subtract, sreads=[cr_.b[0]])
                        ew.ts("dve", A(c1), il, cr_[:, p:p + 1], ALU.mult, sreads=[cr_.b[0]])
                        ew.stt(A(n_i), rl, ci_[:, p:p + 1], A(c1), ALU.mult, ALU.add, sreads=[ci_.b[0]])
                    d1, d2, d3, d4 = tmpf(), tmpf(), tmpf(), tmpf()
                    ew.tt("pool", A(d1), A(rr), A(cosT[p]), ALU.mult)
                    ew.tt("pool", A(d2), A(ri), A(sinT[p]), ALU.mult)
                    ew.tt("pool", A(d3), A(ri), A(cosT[p]), ALU.mult)
                    ew.tt("pool", A(d4), A(rr), A(sinT[p]), ALU.mult)
                    sr, si = tmpb(), tmpb()
                    ew.tt("dve", A(sr), A(d1), A(d2), ALU.subtract)
                    ew.tt("dve", A(si), A(d3), A(d4), ALU.add)
                    ctx.op_multi("pe", [
                        lambda e, py=py, sr=sr, p=p: e.matmul(py[0:32, :], crb[:, p, :], sr[:, :], start=True, stop=False),
                        lambda e, py=py, si=si, p=p: e.matmul(py[0:32, :], ncib[:, p, :], si[:, :], start=False, stop=True)],
                        reads=[crb.b[0], ncib.b[0], sr.b[0], si.b[0]], writes=[py.b[0]])
                    ys, zs = yst[step % 2], zst[step % 3]
                    ctx.op("dve", lambda e, ys=ys, uu=uu, py=py, p=p: e.scalar_tensor_tensor(
                        out=ys[:, :], in0=uu[:, :], scalar=dk[:, p:p + 1], in1=py[0:32, :], op0=ALU.mult, op1=ALU.add),
                        reads=[uu.b[0], py.b[0], dk.b[0]], writes=[ys.b[0]])
                    ctx.op("act", lambda e, ys=ys, zs=zs: e.activation(out=zs[:, :], in_=ys[:, :], func=AF.Gelu),
                           reads=[ys.b[0]], writes=[zs.b[0]])
                    ctx.dma("pool", zTo[p * 32:(p + 1) * 32, t0:t0 + CH], zs[:, :], reads=[zs.b[0]], is_output=True)
                    step += 1

        es_ssm.close()
        cur_es[0] = es
        ctx.fence("sp")
        if do_attn:
            qs = T_("qs", [128, L], BF16)
            ks = T_("ks", [128, L], BF16)
            vs = T_("vs", [128, L // 128, 128], BF16)
            ctx.dma("sp", qs[:, :], qT, writes=[qs.b[0]])
            ctx.dma("sp", ks[:, :], kT, writes=[ks.b[0]])
            ctx.dma("sp", vs[:, :, :], vtm.rearrange("(b p) d -> p b d", p=128), writes=[vs.b[0]])
            lv = T_("lamv", [128, 256])
            cs = T_("cst", [128, 3])
            tr32 = T_("tri32", [128, 128])
            trb = T_("trib", [128, 128], BF16)
            ctx.dma("sp", lv[:, :], lamv, writes=[lv.b[0]])
            ctx.dma("sp", cs[:, :], cst, writes=[cs.b[0]])
            ctx.dma("sp", tr32[:, :], tri, writes=[tr32.b[0]])
            ctx.op("dve", lambda e: e.tensor_copy(out=trb[:, :], in_=tr32[:, :]), reads=[tr32.b[0]], writes=[trb.b[0]])
            lt = [T_(f"lt{i}", [128, 64]) for i in range(2)]
            l1 = [T_(f"l1_{i}", [128, 1]) for i in range(6)]
            L1 = lambda i: (l1[i][:, :], [l1[i].b[0]])
            ew.tt("dve", A(lt[0]), (lv[:, 0:64], [lv.b[0]]), (lv[:, 64:128], [lv.b[0]]), ALU.mult)
            ew.tt("dve", A(lt[1]), (lv[:, 128:192], [lv.b[0]]), (lv[:, 192:256], [lv.b[0]]), ALU.mult)
            ctx.op("dve", lambda e: e.reduce_sum(out=l1[0][:, :], in_=lt[0][:, :], axis=mybir.AxisListType.X), reads=[lt[0].b[0]], writes=[l1[0].b[0]])
            ctx.op("dve", lambda e: e.reduce_sum(out=l1[1][:, :], in_=lt[1][:, :], axis=mybir.AxisListType.X), reads=[lt[1].b[0]], writes=[l1[1].b[0]])
            ew.act(L1(2), L1(0), AF.Exp)
            ew.act(L1(3), L1(1), AF.Exp)
            ew.tt("dve", L1(4), L1(3), L1(2), ALU.subtract)
            ew.tt("dve", L1(4), L1(4), (cs[:, 0:1], [cs.b[0]]), ALU.subtract)
            ew.tt("dve", L1(5), (cs[:, 1:2], [cs.b[0]]), (cs[:, 2:3], [cs.b[0]]), ALU.mult)
            nlam, gsc = l1[4], l1[5]
            pS = [[ps[0], ps[1]], [ps[2], ps[3]]]
            pO = [ps[4], ps[5]]
            pL = [ps[6], ps[7]]
            nq = L // 512
            sidx = 0
            for qt in range(nq):
                q0 = qt * 512
                nb = 4 * qt + 4
                pend = None

                def pv(item, first):
                    b, col0, P = item
                    for m in range(2):
                        ctx.op_multi("pe", [
                            lambda e, m=m, b=b, col0=col0, P=P, first=first, last=(b == nb - 1): e.matmul(pO[m][:, col0:512], vs[:, b, :], P[m][:, col0:512], start=first, stop=last),
                            lambda e, m=m, b=b, col0=col0, P=P, first=first, last=(b == nb - 1): e.matmul(pL[m][:, col0:512], onesb[:, :], P[m][:, col0:512], start=first, stop=last)],
                            reads=[vs.b[0], onesb.b[0], P[m].b[0]], writes=[pO[m].b[0], pL[m].b[0]])
                for b in range(nb):
                    d = b - 4 * qt
                    col0 = 128 * d if d > 0 else 0
                    P = [tmpb(), tmpb()]
                    for m in range(2):
                        S = pS[m][sidx % 2]
                        lo = 64 * m
                        ctx.op("pe", lambda e, S=S, lo=lo, b=b, col0=col0, q0=q0: e.matmul(
                            S[:, col0:512], ks[lo:lo + 64, b * 128:(b + 1) * 128], qs[lo:lo + 64, q0 + col0:q0 + 512], start=True, stop=True),
                            reads=[ks.b[0], qs.b[0]], writes=[S.b[0]])
                        ctx.op("act", lambda e, S=S, Pm=P[m], col0=col0: e.activation(out=Pm[:, col0:512], in_=S[:, col0:512], func=AF.Exp),
                               reads=[S.b[0]], writes=[P[m].b[0]])
                        if d >= 0:
                            ctx.op("pool", lambda e, Pm=P[m], col0=col0: e.tensor_tensor(
                                out=Pm[:, col0:col0 + 128], in0=Pm[:, col0:col0 + 128], in1=trb[:, :], op=ALU.mult),
                                reads=[P[m].b[0], trb.b[0]], writes=[P[m].b[0]])
                    sidx += 1
                    if pend is not None:
                        pv(pend, pend[0] == 0)
                    pend = (b, col0, P)
                pv(pend, pend[0] == 0)
                r1, r2, oa, ob = tmpf(), tmpf(), tmpf(), tmpf()
                ctx.op("dve", lambda e, r1=r1: e.reciprocal(out=r1[:, :], in_=pL[0][:, :]), reads=[pL[0].b[0]], writes=[r1.b[0]])
                ctx.op("dve", lambda e, r2=r2: e.reciprocal(out=r2[:, :], in_=pL[1][:, :]), reads=[pL[1].b[0]], writes=[r2.b[0]])
                ew.tt("dve", A(oa), A(pO[0]), A(r1), ALU.mult)
                ew.tt("dve", A(ob), A(pO[1]), A(r2), ALU.mult)
                o = tmpf()
                ew.stt(A(o), A(ob), nlam[:, 0:1], A(oa), ALU.mult, ALU.add, sreads=[nlam.b[0]])
                sq = tmpf()
                ew.act(A(sq), A(o), AF.Square)
                S = pS[0][sidx % 2]
                sidx += 1
                ctx.op("pe", lambda e, S=S, sq=sq: e.matmul(S[:, :], ones32[:, :], sq[:, :], start=True, stop=True),
                       reads=[ones32.b[0], sq.b[0]], writes=[S.b[0]])
                vr, sd = tmpf(), tmpf()
                ew.ts("dve", A(vr), A(S), 1.0 / 128.0, ALU.mult, RMS_EPS, ALU.add)
                ew.act(A(sd), A(vr), AF.Sqrt)
                ctx.op("dve", lambda e, vr=vr, sd=sd: e.reciprocal(out=vr[:, :], in_=sd[:, :]), reads=[sd.b[0]], writes=[vr.b[0]])
                yb = tmpb()
                ew.stt(A(yb), A(o), gsc[:, 0:1], A(vr), ALU.mult, ALU.mult, sreads=[gsc.b[0]])
                ctx.dma("pool", aTo[:, q0:q0 + 512], yb[:, :], reads=[yb.b[0]], is_output=True)
        ctx.emit()
    return nc


def ssm_params(core, a_re, a_im, b_re, b_im, c_re, c_im, log_dt, d):
    prow = np.zeros((32, 5, 512), np.float32)
    pcol = np.zeros((128, 3, 4), np.float32)
    ccol = np.zeros((128, 2, 4, 32), np.float32)
    dsk = np.zeros((32, 4), np.float32)
    for p in range(4):
        for s in range(2):
            g = 8 * core + 2 * p + s
            c0 = p * 128 + s * 64
            prow[:, 0, c0:c0 + 64] = a_re[g][None, :]
            prow[:, 1, c0:c0 + 64] = a_im[g][None, :]
            prow[:, 2, c0:c0 + 64] = log_dt[g]
            prow[s * 16:(s + 1) * 16, 3, c0:c0 + 64] = b_re[g].T
            prow[s * 16:(s + 1) * 16, 4, c0:c0 + 64] = b_im[g].T
            pcol[s * 64:(s + 1) * 64, 0, p] = a_re[g]
            pcol[s * 64:(s + 1) * 64, 1, p] = a_im[g]
            pcol[s * 64:(s + 1) * 64, 2, p] = log_dt[g]
            ccol[s * 64:(s + 1) * 64, 0, p, s * 16:(s + 1) * 16] = c_re[g].T
            ccol[s * 64:(s + 1) * 64, 1, p, s * 16:(s + 1) * 16] = c_im[g].T
            dsk[s * 16:(s + 1) * 16, p] = d[g * 16:(g + 1) * 16]
    return prow, pcol, ccol, dsk


def tri_const():
    k = np.arange(128)[:, None]
    q = np.arange(128)[None, :]
    return (k <= q).astype(np.float32)


def iota_const(n=512):
    return np.ascontiguousarray(np.broadcast_to(np.arange(n, dtype=np.float32)[None, :], (128, n)))


def _run(nc, in_maps):
    res = run_bass_kernel_spmd(nc, in_maps, core_ids=list(range(NCORES)))
    return res.results


def kernel(**inp):
    inp = {k: np.asarray(v) for k, v in inp.items()}
    L = SEQ
    T = L // NCORES
    f32 = np.float32
    units = []
    la_cols = ld_cols = None
    for i in range(DEPTH):
        ua = la_weights(inp["ffn1_w_gate"][i], inp["ffn1_w_up"][i], inp["ffn1_w_down"][i], inp["w_in"][i])
        ud = ld_weights(inp["ssm_w_glu"][i], inp["w_branch_ssm"][i], inp["w_branch_attn"][i], inp["w_out"][i],
                        inp["ffn2_w_gate"][i], inp["ffn2_w_up"][i], inp["ffn2_w_down"][i],
                        inp["ple_w_gate"][i], inp["ple_w_proj"][i])
        la_cols = sum(u.shape[1] for u in ua)
        ld_cols = sum(u.shape[1] for u in ud)
        units += ua + ud
    wall = np.concatenate(units, axis=1)
    del units
    tot = wall.shape[1]
    assert tot % NCORES == 0
    per = tot // NCORES
    nc0 = build_cast(per)
    r0 = _run(nc0, [{"x": np.ascontiguousarray(wall[:, c * per:(c + 1) * per])} for c in range(NCORES)])
    del wall
    wb = np.concatenate([r0[c]["y"] for c in range(NCORES)], axis=1)
    del r0
    lay = la_cols + ld_cols

    ncA = build_LA(T)
    ncC = build_LC(L)
    ncD = build_LD(T)
    perm = rope_perm()
    invf = rope_invf()[:, None]
    tri = tri_const()
    iota = iota_const()
    pos = inp["positions"][0].astype(np.int32)
    xT = [np.ascontiguousarray(inp["x"][0, c * T:(c + 1) * T, :].T) for c in range(NCORES)]
    posi = [np.ascontiguousarray(np.broadcast_to(pos[None, c * T:(c + 1) * T], (128, T))) for c in range(NCORES)]
    for i in range(DEPTH):
        lam_init = 0.8 - 0.6 * math.exp(-0.3 * i)
        wA = np.ascontiguousarray(wb[:, i * lay:i * lay + la_cols])
        vecA = np.concatenate([colvec(inp["ln1_g"][i]), colvec(inp["ln1_b"][i]), invf], axis=1).astype(f32)
        rA = _run(ncA, [{"xT": xT[c], "wts": wA, "vec": vecA, "perm": perm, "posi": posi[c]} for c in range(NCORES)])
        del wA
        lamv = np.concatenate([inp["lambda_q1"][i], inp["lambda_k1"][i], inp["lambda_q2"][i], inp["lambda_k2"][i]]).astype(f32)
        lamv = np.ascontiguousarray(np.broadcast_to(lamv[None, :], (128, 256)))
        cst = np.stack([np.full(128, lam_init, f32), np.full(128, 1.0 - lam_init, f32), inp["attn_subln_g"][i].astype(f32)], axis=1)
        mapsC = []
        for h in range(NCORES):
            sl = slice(h * 128, (h + 1) * 128)
            qh = np.concatenate([rA[c]["qT"][sl] for c in range(NCORES)], axis=1)
            kh = np.concatenate([rA[c]["kT"][sl] for c in range(NCORES)], axis=1)
            vh = np.ascontiguousarray(np.concatenate([rA[c]["vT"][sl] for c in range(NCORES)], axis=1).T)
            uh = np.concatenate([rA[c]["uT"][sl] for c in range(NCORES)], axis=1)
            u4 = np.ascontiguousarray(uh.reshape(4, 32, L).transpose(1, 0, 2))
            prow, pcol, ccol, dsk = ssm_params(h, inp["ssm_a_re"][i], inp["ssm_a_im"][i], inp["ssm_b_re"][i], inp["ssm_b_im"][i],
                                               inp["ssm_c_re"][i], inp["ssm_c_im"][i], inp["ssm_log_dt"][i], inp["ssm_d"][i])
            mapsC.append({"qT": qh, "kT": kh, "vtm": vh, "lamv": lamv, "cst": cst, "tri": tri, "u4": u4,
                          "prow": prow, "pcol": pcol, "ccol": ccol, "dsk": dsk, "iota": iota})
        rC = _run(ncC, mapsC)
        del mapsC
        wD = np.ascontiguousarray(wb[:, i * lay + la_cols:(i + 1) * lay])
        vecD = np.concatenate([colvec(inp["ln2_g"][i]), colvec(inp["ln2_b"][i]), colvec(inp["ln3_g"][i]), colvec(inp["ln3_b"][i]),
                               colvec(inp["ln4_g"][i]), colvec(inp["ln4_b"][i])], axis=1).astype(f32)
        mapsD = []
        for c in range(NCORES):
            ts_ = slice(c * T, (c + 1) * T)
            zT = np.ascontiguousarray(np.concatenate([rC[h]["zTo"][:, ts_] for h in range(NCORES)], axis=0))
            aT = np.ascontiguousarray(np.concatenate([rC[h]["aTo"][:, ts_] for h in range(NCORES)], axis=0))
            pT = np.ascontiguousarray(inp["p"][i, 0, ts_, :].T)
            mapsD.append({"x1T": rA[c]["x1T"], "zT": zT, "aT": aT, "sgT": rA[c]["sgT"], "pT": pT, "wts": wD, "vec": vecD})
        rD = _run(ncD, mapsD)
        del mapsD, rA, rC, wD
        xT = [rD[c]["x4T"] for c in range(NCORES)]
    out = np.concatenate([np.ascontiguousarray(xT[c].T) for c in range(NCORES)], axis=0)[None]
    return out.astype(np.float32)


def split_parts(sizes, nparts):
    tot = float(sum(sizes))
    target = tot / nparts
    part_of = []
    cum = 0.0
    for n in sizes:
        part_of.append(min(nparts - 1, int((cum + n / 2.0) / target)))
        cum += n
    cols = [0] * nparts
    locs = []
    for i, n in enumerate(sizes):
        p = part_of[i]
        locs.append((p, cols[p], n))
        cols[p] += n
    per = max(cols)
    per = (per + 511) // 512 * 512
    return locs, per


def la_unit_sizes():
    return [2048] * (2 * FC) + [2048, 2048, 1536] * KC + [2048] * 24 + [2048] * 8 + [2048] * 32


def la_units_fused(ffn_g, ffn_u, ffn_d, w_in):
    g = pretile(ffn_g); u = pretile(ffn_u); d = pretile(ffn_d)
    out = []
    for j in range(FC):
        out.append(g[j]); out.append(u[j])
    out += d
    wi = pretile(w_in)
    out += wi[0:24]
    for hf in range(2):
        for kg in range(4):
            blk = w_in[kg * 512:(kg + 1) * 512, 3072 + hf * 512:3072 + (hf + 1) * 512]
            out.append(blk.reshape(4, 128, 512).transpose(1, 0, 2).reshape(128, 2048))
    out += wi[32:64]
    return out


def ld_unit_sizes():
    s, _ = ld_sched()
    return [n for _, n in s]


NPART = 16


class Fused:
    def __init__(self):
        import contextlib
        self.nc = nc = bass.Bass("TRN2", target_bir_lowering=False)
        self.ctx = ctx = Ctx(nc)
        self.ew = EW(ctx)
        L, T = SEQ, SEQ // NCORES
        self.L, self.T = L, T
        dt = nc.dram_tensor
        self.locA, self.perA = split_parts(la_unit_sizes(), NPART)
        self.locD, self.perD = split_parts(ld_unit_sizes(), NPART)
        self.x_in = dt("xT", [D_MODEL, T], F32, kind="ExternalInput").ap()
        self.posi = dt("posi", [128, T], I32, kind="ExternalInput").ap()
        self.perm = dt("perm", [128, 128], F32, kind="ExternalInput").ap()
        self.sel = dt("sel", [128, 8, 128], BF16, kind="ExternalInput").ap()
        self.tri = dt("tri", [128, 128], F32, kind="ExternalInput").ap()
        self.iota = dt("iota", [128, 512], F32, kind="ExternalInput").ap()
        self.wsl = {}
        for i in range(DEPTH):
            for g, per in (("A", self.perA), ("D", self.perD)):
                for hf in range(2):
                    self.wsl[(i, g, hf)] = dt(f"w{g}{i}_{hf}", [128, per], F32, kind="ExternalInput").ap()
        self.vecA = [dt(f"vecA{i}", [128, 33], F32, kind="ExternalInput").ap() for i in range(DEPTH)]
        self.vecD = [dt(f"vecD{i}", [128, 96 + 8], F32, kind="ExternalInput").ap() for i in range(DEPTH)]
        self.pT = [dt(f"pT{i}", [PLE_DIM, T], F32, kind="ExternalInput").ap() for i in range(DEPTH)]
        self.lamv = [dt(f"lamv{i}", [128, 256], F32, kind="ExternalInput").ap() for i in range(DEPTH)]
        self.cst = [dt(f"cst{i}", [128, 3], F32, kind="ExternalInput").ap() for i in range(DEPTH)]
        self.prow = [dt(f"prow{i}", [32, 5, 512], F32, kind="ExternalInput").ap() for i in range(DEPTH)]
        self.pcol = [dt(f"pcol{i}", [128, 3, 4], F32, kind="ExternalInput").ap() for i in range(DEPTH)]
        self.ccol = [dt(f"ccol{i}", [128, 2, 4, 32], F32, kind="ExternalInput").ap() for i in range(DEPTH)]
        self.out = dt("outT", [D_MODEL, T], F32, kind="ExternalOutput").ap()
        BOUND = 64 * 1024 * 1024
        state = {"cur": None, "npad": 0}

        def dti(name, shape, dtype, nocross=False):
            esz = 2 if dtype == BF16 else 4
            size = int(np.prod(shape)) * esz
            if state["cur"] is None:
                t0 = nc.dram_tensor("dram_anchor", [1, 64], F32)
                m0 = nc.lookup_mloc(t0)
                state["cur"] = m0.addr + 256
            cur = state["cur"]
            if nocross and (cur // BOUND) != ((cur + size - 1) // BOUND):
                padb = (cur // BOUND + 1) * BOUND - cur
                nc.dram_tensor(f"dram_pad{state['npad']}", [1, padb // 4], F32)
                state["npad"] += 1
            t = nc.dram_tensor(name, shape, dtype)
            m = nc.lookup_mloc(t)
            if nocross:
                assert (m.addr // BOUND) == ((m.addr + size - 1) // BOUND), (name, m.addr, size)
            state["cur"] = m.addr + size
            return t
        self.wl = {}
        self.wg = {}
        for k, ap in self.wsl.items():
            per = ap.shape[1]
            nm = f"{k[1]}{k[0]}_{k[2]}"
            self.wl[k] = dti("wl" + nm, [128, per], BF16, True).ap()
            self.wg[k] = dti("wg" + nm, [NCORES * 128, per], BF16, True).ap()
        self.ex = []
        for i in range(DEPTH):
            e = {}
            e["u_loc"] = dti(f"u_loc{i}", [1024, T], BF16, True).ap()
            e["u_all"] = dti(f"u_all{i}", [8 * 1024, T], BF16, True).ap()
            e["q_loc"] = dti(f"q_loc{i}", [1024, T], BF16, True).ap()
            e["q_all"] = dti(f"q_all{i}", [8 * 1024, T], BF16, True).ap()
            e["k_loc"] = dti(f"k_loc{i}", [1024, T], BF16, True).ap()
            e["k_all"] = dti(f"k_all{i}", [8 * 1024, T], BF16, True).ap()
            e["v_loc"] = dti(f"v_loc{i}", [8 * T, 128], BF16, True).ap()
            e["v_all"] = dti(f"v_all{i}", [8 * 8 * T, 128], BF16, True).ap()
            e["y_loc"] = dti(f"y_loc{i}", [1024, T], BF16, True).ap()
            e["y_all"] = dti(f"y_all{i}", [8 * 1024, T], BF16, True).ap()
            e["a_loc"] = dti(f"a_loc{i}", [1024, T], BF16, True).ap()
            e["a_all"] = dti(f"a_all{i}", [8 * 1024, T], BF16, True).ap()
            e["x1T"] = dti(f"x1T{i}", [D_MODEL, T], F32).ap()
            e["u32T"] = dti(f"u32T{i}", [1024, T], F32).ap()
            e["sgT"] = dti(f"sgT{i}", [4096, T], BF16).ap()
            e["x4T"] = dti(f"x4T{i}", [D_MODEL, T], F32).ap() if i + 1 < DEPTH else self.out
            self.ex.append(e)
        self.cc_n = 0

    def allgather(self, src, dst, dst_buf):
        ctx = self.ctx
        ctx.fence("pool")
        key = ctx._semkey(f"cc{self.cc_n}")
        if not hasattr(ctx, "cc_pids"):
            ctx.cc_pids = set()
        ctx.cc_pids.add(key)
        self.cc_n += 1
        ctx.cnt[key] += 1
        rg = [list(range(NCORES))]
        ctx.streams["pool"].append(([], lambda e: e.collective_compute("AllGather", ALU.bypass, rg, [src.opt()], [dst.opt()]), key, None))
        dst_buf.w = (key, 1)
        dst_buf.r = []

    def fence_all(self):
        for e in Ctx.ENG:
            self.ctx.fence(e)
        self.ctx.release_dma_keys()

    def phase_weights(self, keys, ag_now=None):
        import contextlib
        ctx = self.ctx
        CB = 4096
        with contextlib.ExitStack() as es:
            NB = 3
            xin = [Tile(ctx, es, f"wxin{i}", [128, CB], F32) for i in range(NB)]
            yo = [Tile(ctx, es, f"wyo{i}", [128, CB], BF16) for i in range(NB)]
            engs = ["dve", "pool", "act"]
            blocks = []
            for k in keys:
                per = self.wsl[k].shape[1]
                for c0 in range(0, per, CB):
                    blocks.append((k, c0, min(CB, per - c0), c0 + CB >= per))
            nblk = len(blocks)

            def load(i):
                if i >= nblk:
                    return
                k, c0, cw, _ = blocks[i]
                ctx.dma("sp", xin[i % NB][:, 0:cw], self.wsl[k][:, c0:c0 + cw], writes=[xin[i % NB].b[0]])
            for i in range(NB):
                load(i)
            for i, (k, c0, cw, last) in enumerate(blocks):
                xi, yi = xin[i % NB], yo[i % NB]
                eng = engs[i % 3]
                if eng == "act":
                    ctx.op("act", lambda e, xi=xi, yi=yi, cw=cw: e.copy(out=yi[:, 0:cw], in_=xi[:, 0:cw]), reads=[xi.b[0]], writes=[yi.b[0]])
                else:
                    ctx.op(eng, lambda e, xi=xi, yi=yi, cw=cw: e.tensor_copy(out=yi[:, 0:cw], in_=xi[:, 0:cw]), reads=[xi.b[0]], writes=[yi.b[0]])
                ctx.dma("sp", self.wl[k][:, c0:c0 + cw], yi[:, 0:cw], reads=[yi.b[0]])
                load(i + NB)
                if last and (ag_now is None or k in ag_now):
                    self.allgather(self.wl[k], self.wg[k], ctx.buf("wg%s%d_%d" % (k[1], k[0], k[2])))
        self.fence_all()

    def wsched(self, i, g, ntile):
        locs = self.locA if g == "A" else self.locD
        out = []
        for (p, off, n) in locs:
            hf, r = divmod(p, 8)
            k = (i, g, hf)
            out.append((self.wg[k][r * 128:(r + 1) * 128, off:off + n], n, self.ctx.buf("wg%s%d_%d" % (g, i, hf))))
        return out * ntile

    def select8(self, stage, lhs_fn, out_ap, ps_buf, ncol, selt):
        fns = []
        for j in range(8):
            fns.append(lambda e, j=j: e.matmul(out_ap, lhs_fn(j), stage[:, j, 0:ncol], start=(j == 0), stop=(j == 7)))
        self.ctx.op_multi("pe", fns, reads=[stage.b[0], selt.b[0]], writes=[ps_buf])

    def phase_A(self, i):
        import contextlib
        nc, ctx, T = self.nc, self.ctx, self.T
        ex = self.ex[i]
        x_src = self.x_in if i == 0 else self.ex[i - 1]["x4T"]
        ctx.tag = f"@A{i}"
        with contextlib.ExitStack() as es:
            ph = TokPhase(nc, ctx, es, None, self.vecA[i], 33)
            permt = Tile(ctx, es, "permt", [128, 128], F32)
            ctx.dma("sp", permt[:, :], self.perm, writes=[permt.b[0]])
            rt_i = Tile(ctx, es, "rt_pi", [128, TT], I32)
            rt = [rt_i] + [Tile(ctx, es, f"rt{k}", [128, TT], F32) for k in range(8)]
            cosk, sink, cosq, sinq = rt[5], rt[6], rt[7], rt[8]
            ntile = T // TT
            ph.ws.sched = self.wsched(i, "A", ntile)
            v3 = ex["v_loc"].rearrange("(h t) d -> t h d", h=8)
            for ti in range(ntile):
                t0 = ti * TT
                ph.load_x(x_src, t0)
                ph.rope_tables(rt, self.posi, t0, 32)
                ph.ffn()
                ph.layernorm(0, 16)
                ph.store_x(ex["x1T"], t0, is_output=False)
                xr = ph.chunks(ph.xb, 0, KC)
                for o in range(24):
                    ps = ph.acc(xr)
                    if o < 8:
                        st = ph.tmpf()
                        sb = ph.tmpb()
                        ctx.op("act", lambda e, st=st, ps=ps: e.copy(out=st[:, :], in_=ps[:, :]), reads=[ps.b[0]], writes=[st.b[0]])
                        ctx.op("pool", lambda e, st=st, sb=sb: e.tensor_copy(out=sb[:, :], in_=st[:, :]), reads=[st.b[0]], writes=[sb.b[0]])
                        ctx.dma("pool", ex["u32T"][o * 128:(o + 1) * 128, t0:t0 + TT], st[:, :], reads=[st.b[0]])
                        ctx.dma("pool", ex["u_loc"][o * 128:(o + 1) * 128, t0:t0 + TT], sb[:, :], reads=[sb.b[0]])
                    else:
                        isq = o < 16
                        ct, sn = (cosq, sinq) if isq else (cosk, sink)
                        row0 = (o - 8) * 128 if isq else (o - 16) * 128
                        qk_dst = ex["q_loc"] if isq else ex["k_loc"]
                        qf = ph.tmpf()
                        ctx.op("act", lambda e, qf=qf, ps=ps: e.copy(out=qf[:, :], in_=ps[:, :]), reads=[ps.b[0]], writes=[qf.b[0]])
                        ps2 = ph.next_ps()
                        ctx.op("pe", lambda e, ps2=ps2, qf=qf: e.matmul(ps2[:, :], permt[:, :], qf[:, :], start=True, stop=True),
                               reads=[qf.b[0], permt.b[0]], writes=[ps2.b[0]])
                        t1 = ph.tmpf()
                        t2 = ph.tmpf()
                        ctx.op("dve", lambda e, t1=t1, qf=qf, ct=ct: e.tensor_tensor(out=t1[:, :], in0=qf[:, :], in1=ct[:, :], op=ALU.mult),
                               reads=[qf.b[0], ct.b[0]], writes=[t1.b[0]])
                        ctx.op("dve", lambda e, t2=t2, ps2=ps2, sn=sn: e.tensor_tensor(out=t2[:, :], in0=ps2[:, :], in1=sn[:, :], op=ALU.mult),
                               reads=[ps2.b[0], sn.b[0]], writes=[t2.b[0]])
                        sb = ph.tmpb()
                        ctx.op("pool", lambda e, sb=sb, t1=t1, t2=t2: e.tensor_tensor(out=sb[:, :], in0=t1[:, :], in1=t2[:, :], op=ALU.add),
                               reads=[t1.b[0], t2.b[0]], writes=[sb.b[0]])
                        ctx.dma("pool", qk_dst[row0:row0 + 128, t0:t0 + TT], sb[:, :], reads=[sb.b[0]])
                for hf in range(2):
                    pss = [ph.next_ps() for _ in range(4)]
                    for kg in range(4):
                        sl = ph.ws.next(2048)
                        for s_ in range(4):
                            fns = []
                            for c in range(4):
                                fns.append(lambda e, p_=pss[s_], sl=sl, c=c, kc=4 * kg + c, s_=s_, st=(kg == 0 and c == 0), sp=(kg == 3 and c == 3):
                                           e.matmul(p_[:, :], ph.xb[:, kc, s_ * 128:(s_ + 1) * 128], sl[:, c * 512:(c + 1) * 512], start=st, stop=sp))
                            ctx.op_multi("pe", fns, reads=[sl.b[0]] + [ph.xb.b[4 * kg + c] for c in range(4)], writes=[pss[s_].b[0]])
                    for s_ in range(4):
                        sb = ph.tmpb()
                        ctx.op("act", lambda e, sb=sb, p_=pss[s_]: e.copy(out=sb[:, :], in_=p_[:, :]), reads=[pss[s_].b[0]], writes=[sb.b[0]])
                        ctx.dma("pool", v3[t0 + s_ * 128:t0 + (s_ + 1) * 128, hf * 4:(hf + 1) * 4, :],
                                sb[:, :].rearrange("t (h d) -> t h d", h=4), reads=[sb.b[0]])
                for g in range(32):
                    ps = ph.acc(xr)
                    sb = ph.tmpb()
                    ctx.op("act", lambda e, sb=sb, ps=ps: e.activation(out=sb[:, :], in_=ps[:, :], func=AF.Sigmoid),
                           reads=[ps.b[0]], writes=[sb.b[0]])
                    ctx.dma("pool", ex["sgT"][g * 128:(g + 1) * 128, t0:t0 + TT], sb[:, :], reads=[sb.b[0]])
        self.allgather(ex["u_loc"], ex["u_all"], ctx.buf(f"u_all{i}"))
        self.allgather(ex["q_loc"], ex["q_all"], ctx.buf(f"q_all{i}"))
        self.allgather(ex["k_loc"], ex["k_all"], ctx.buf(f"k_all{i}"))
        self.allgather(ex["v_loc"], ex["v_all"], ctx.buf(f"v_all{i}"))
        for k in self.deferred_ag.pop(i, []):
            self.allgather(self.wl[k], self.wg[k], ctx.buf("wg%s%d_%d" % (k[1], k[0], k[2])))
        self.fence_all()

    def phase_C(self, i):
        import contextlib
        nc, ctx, ew, L, T = self.nc, self.ctx, self.ew, self.L, self.T
        ex = self.ex[i]
        CH = 512
        nchunk = L // CH
        ctx.tag = f"@C{i}"
        b_uall, b_vall = ctx.buf(f"u_all{i}"), ctx.buf(f"v_all{i}")
        b_qkall = [ctx.buf(f"q_all{i}"), ctx.buf(f"k_all{i}")]
        with contextlib.ExitStack() as es:
            cur_es = [es]

            def T_(name, shape, dtype=F32, psum=False, nsub=1):
                return Tile(ctx, cur_es[0], name, shape, dtype, psum=psum, nsub=nsub)

            def A(t, idx=None):
                return (t[:, :] if idx is None else t[idx], [t.b[0]])
            ps = [T_(f"ps{k}", [128, 512], F32, psum=True) for k in range(8)]
            ones32 = T_("ones32", [128, 128])
            onesb = T_("onesb", [128, 128], BF16)
            ctx.op("pool", lambda e: e.memset(ones32[:, :], 1.0), writes=[ones32.b[0]])
            ctx.op("pool", lambda e: e.memset(onesb[:, :], 1.0), writes=[onesb.b[0]])
            selt = T_("selt", [128, 8, 128], BF16)
            ctx.dma("sp", selt[:, :, :], self.sel, writes=[selt.b[0]])
            stage = [T_(f"stage{k}", [128, 8, 512], BF16) for k in range(2)]
            stg_i = [0]

            def next_stage():
                s_ = stage[stg_i[0] % 2]
                stg_i[0] += 1
                return s_
            tf = [T_(f"tf{k}", [128, 512]) for k in range(14)]
            tfi = [0]

            def tmpf():
                t = tf[tfi[0] % len(tf)]
                tfi[0] += 1
                return t
            tb = [T_(f"tb{k}", [128, 512], BF16) for k in range(8)]
            tbi = [0]

            def tmpb():
                t = tb[tbi[0] % len(tb)]
                tbi[0] += 1
                return t

            es_ssm = contextlib.ExitStack()
            cur_es[0] = es_ssm
            pr = T_("prow", [32, 5, 512])
            pc = T_("pcol", [128, 3, 4])
            cc = T_("ccol", [128, 2, 4, 32])
            io = T_("iota", [128, CH])
            for t, src in ((pr, self.prow[i]), (pc, self.pcol[i]), (cc, self.ccol[i]), (io, self.iota)):
                ctx.dma("sp", t[:], src, writes=[t.b[0]])
            rw = [T_(f"rw{k}", [32, 512]) for k in range(12)]
            R_ = lambda k: (rw[k][:, :], [rw[k].b[0]])
            P_ = lambda k: (pr[:, k, :], [pr.b[0]])
            bbre = T_("bbre", [32, 512], BF16)
            bbim = T_("bbim", [32, 512], BF16)
            ew.act(R_(0), P_(2), AF.Exp)
            ew.tt("dve", R_(1), P_(0), R_(0), ALU.mult)
            ew.act(R_(2), R_(1), AF.Exp)
            ew.tt("dve", R_(3), P_(1), R_(0), ALU.mult)
            ew.sincos(R_(3), R_(4), R_(5), R_(6), R_(7))
            ew.tt("dve", R_(8), R_(2), R_(7), ALU.mult)
            ew.tt("dve", R_(9), R_(2), R_(6), ALU.mult)
            ew.ts("dve", R_(8), R_(8), -1.0, ALU.add)
            ew.tt("dve", R_(0), P_(0), P_(0), ALU.mult)
            ew.tt("dve", R_(1), P_(1), P_(1), ALU.mult)
            ew.tt("dve", R_(0), R_(0), R_(1), ALU.add)
            ctx.op("dve", lambda e: e.reciprocal(out=rw[0][:, :], in_=rw[0][:, :]), reads=[rw[0].b[0]], writes=[rw[0].b[0]])
            ew.tt("dve", R_(1), R_(8), P_(0), ALU.mult)
            ew.tt("dve", R_(2), R_(9), P_(1), ALU.mult)
            ew.tt("dve", R_(1), R_(1), R_(2), ALU.add)
            ew.tt("dve", R_(10), R_(1), R_(0), ALU.mult)
            ew.tt("dve", R_(1), R_(9), P_(0), ALU.mult)
            ew.tt("dve", R_(2), R_(8), P_(1), ALU.mult)
            ew.tt("dve", R_(1), R_(1), R_(2), ALU.subtract)
            ew.tt("dve", R_(11), R_(1), R_(0), ALU.mult)
            ew.tt("dve", R_(1), R_(10), P_(3), ALU.mult)
            ew.tt("dve", R_(2), R_(11), P_(4), ALU.mult)
            ew.tt("dve", (bbre[:, :], [bbre.b[0]]), R_(1), R_(2), ALU.subtract)
            ew.tt("dve", R_(1), R_(10), P_(4), ALU.mult)
            ew.tt("dve", R_(2), R_(11), P_(3), ALU.mult)
            ew.tt("dve", (bbim[:, :], [bbim.b[0]]), R_(1), R_(2), ALU.add)
            cw = [T_(f"cw{k}", [128, 4]) for k in range(10)]
            Cw = lambda k: (cw[k][:, :], [cw[k].b[0]])
            Pc = lambda k: (pc[:, k, :], [pc.b[0]])
            ew.act(Cw(0), Pc(2), AF.Exp)
            ew.tt("dve", Cw(1), Pc(0), Cw(0), ALU.mult)
            ew.act(Cw(2), Cw(1), AF.Exp)
            ew.tt("dve", Cw(3), Pc(1), Cw(0), ALU.mult)
            ew.ts("dve", Cw(4), Cw(3), float(CH), ALU.mult)
            ew.sincos(Cw(4), Cw(5), Cw(6), Cw(7), Cw(8))
            mag, th, ci_, cr_ = cw[2], cw[3], cw[7], cw[8]
            dec = [T_(f"dec{p}", [128, CH]) for p in range(4)]
            cosT = [T_(f"cosT{p}", [128, CH]) for p in range(4)]
            sinT = [T_(f"sinT{p}", [128, CH]) for p in range(4)]
            a1, a2, a3 = T_("sa1", [128, CH]), T_("sa2", [128, CH]), T_("sa3", [128, CH])
            for p in range(4):
                ew.ts("dve", A(dec[p]), (io[:, :], [io.b[0]]), 0.0, ALU.mult, mag[:, p:p + 1], ALU.add, sreads=[mag.b[0]])
                ew.ts("dve", A(a1), (io[:, :], [io.b[0]]), th[:, p:p + 1], ALU.mult, sreads=[th.b[0]])
                ew.sincos(A(a1), A(a2), A(a3), A(sinT[p]), A(cosT[p]))
            crb = T_("crb", [128, 4, 32], BF16)
            ncib = T_("ncib", [128, 4, 32], BF16)
            ctx.op("act", lambda e: e.copy(out=crb[:, :, :], in_=cc[:, 0, :, :]), reads=[cc.b[0]], writes=[crb.b[0]])
            ctx.op("act", lambda e: e.mul(out=ncib[:, :, :], in_=cc[:, 1, :, :], mul=-1.0), reads=[cc.b[0]], writes=[ncib.b[0]])
            ir = [[T_(f"ir{p}_{k}", [128, 1]) for k in range(2)] for p in range(4)]
            ii = [[T_(f"ii{p}_{k}", [128, 1]) for k in range(2)] for p in range(4)]
            ctmp = [T_(f"ctmp{k}", [128, 1]) for k in range(4)]
            ubf = [T_(f"ubf_{k}", [32, CH], BF16) for k in range(3)]
            zst = [T_(f"zst{k}", [32, CH], BF16) for k in range(3)]
            step = 0
            import os as _os
            DBG = _os.environ.get("KDBG", "")
            for j in range(0 if "nossm" in DBG else nchunk):
                t0 = j * CH
                c_blk, tl = divmod(t0, T)
                stg = next_stage()
                ctx.dma("sp", stg[:, :, :], ex["u_all"][c_blk * 1024:(c_blk + 1) * 1024, tl:tl + CH].rearrange("(h d) t -> d h t", d=128),
                        reads=[b_uall], writes=[stg.b[0]], semname=stg.b[0].name)
                for p in range(4):
                    ub = ubf[step % 3]
                    psel = ps[6 + (step % 2)]
                    self.select8(stg, lambda jj, p=p: selt[:, jj, 32 * p:32 * p + 32], psel[0:32, :], psel.b[0], CH, selt)
                    ctx.op("act", lambda e, ub=ub, psel=psel: e.copy(out=ub[:, :], in_=psel[0:32, :]), reads=[psel.b[0]], writes=[ub.b[0]])
                    pxr, pxi, py = ps[(step % 2)], ps[2 + (step % 2)], ps[4 + (step % 2)]
                    ctx.op("pe", lambda e, pxr=pxr, ub=ub, p=p: e.matmul(pxr[:, :], bbre[:, p * 128:(p + 1) * 128], ub[:, :], start=True, stop=True),
                           reads=[bbre.b[0], ub.b[0]], writes=[pxr.b[0]])
                    ctx.op("pe", lambda e, pxi=pxi, ub=ub, p=p: e.matmul(pxi[:, :], bbim[:, p * 128:(p + 1) * 128], ub[:, :], start=True, stop=True),
                           reads=[bbim.b[0], ub.b[0]], writes=[pxi.b[0]])
                    xr, xi = tmpf(), tmpf()
                    ew.act(A(xr), A(pxr), AF.Copy)
                    ew.act(A(xi), A(pxi), AF.Copy)
                    m1, m2, m3, m4 = tmpf(), tmpf(), tmpf(), tmpf()
                    ew.tt("pool", A(m1), A(xr), A(cosT[p]), ALU.mult)
                    ew.tt("pool", A(m2), A(xi), A(sinT[p]), ALU.mult)
                    ew.tt("pool", A(m3), A(xi), A(cosT[p]), ALU.mult)
                    ew.tt("dve", A(m4), A(xr), A(sinT[p]), ALU.mult)
                    ew.tt("dve", A(m1), A(m1), A(m2), ALU.add)
                    ew.tt("dve", A(m3), A(m3), A(m4), ALU.subtract)
                    rr, ri = tmpf(), tmpf()
                    if j == 0:
                        ctx.op("dve", lambda e, rr=rr, m1=m1, p=p: e.tensor_tensor_scan(out=rr[:, :], data0=dec[p][:, :], data1=m1[:, :], initial=0.0, op0=ALU.mult, op1=ALU.add),
                               reads=[dec[p].b[0], m1.b[0]], writes=[rr.b[0]])
                        ctx.op("dve", lambda e, ri=ri, m3=m3, p=p: e.tensor_tensor_scan(out=ri[:, :], data0=dec[p][:, :], data1=m3[:, :], initial=0.0, op0=ALU.mult, op1=ALU.add),
                               reads=[dec[p].b[0], m3.b[0]], writes=[ri.b[0]])
                    else:
                        i_r, i_i = ir[p][j % 2], ii[p][j % 2]
                        ctx.op("dve", lambda e, rr=rr, m1=m1, p=p, i_r=i_r: e.tensor_tensor_scan(out=rr[:, :], data0=dec[p][:, :], data1=m1[:, :], initial=i_r[:, :], op0=ALU.mult, op1=ALU.add),
                               reads=[dec[p].b[0], m1.b[0], i_r.b[0]], writes=[rr.b[0]])
                        ctx.op("dve", lambda e, ri=ri, m3=m3, p=p, i_i=i_i: e.tensor_tensor_scan(out=ri[:, :], data0=dec[p][:, :], data1=m3[:, :], initial=i_i[:, :], op0=ALU.mult, op1=ALU.add),
                               reads=[dec[p].b[0], m3.b[0], i_i.b[0]], writes=[ri.b[0]])
                    if j + 1 < nchunk:
                        n_r, n_i = ir[p][(j + 1) % 2], ii[p][(j + 1) % 2]
                        rl = (rr[:, CH - 1:CH], [rr.b[0]])
                        il = (ri[:, CH - 1:CH], [ri.b[0]])
                        c0, c1 = ctmp[(2 * step) % 4], ctmp[(2 * step + 1) % 4]
                        ew.ts("dve", A(c0), il, ci_[:, p:p + 1], ALU.mult, sreads=[ci_.b[0]])
                        ew.stt(A(n_r), rl, cr_[:, p:p + 1], A(c0), ALU.mult, ALU.subtract, sreads=[cr_.b[0]])
                        ew.ts("dve", A(c1), il, cr_[:, p:p + 1], ALU.mult, sreads=[cr_.b[0]])
                        ew.stt(A(n_i), rl, ci_[:, p:p + 1], A(c1), ALU.mult, ALU.add, sreads=[ci_.b[0]])
                    d1, d2, d3, d4 = tmpf(), tmpf(), tmpf(), tmpf()
                    ew.tt("pool", A(d1), A(rr), A(cosT[p]), ALU.mult)
                    ew.tt("pool", A(d2), A(ri), A(sinT[p]), ALU.mult)
                    ew.tt("dve", A(d3), A(ri), A(cosT[p]), ALU.mult)
                    ew.tt("dve", A(d4), A(rr), A(sinT[p]), ALU.mult)
                    sr, si = tmpb(), tmpb()
                    ew.tt("dve", A(sr), A(d1), A(d2), ALU.subtract)
                    ew.tt("dve", A(si), A(d3), A(d4), ALU.add)
                    ctx.op_multi("pe", [
                        lambda e, py=py, sr=sr, p=p: e.matmul(py[0:32, :], crb[:, p, :], sr[:, :], start=True, stop=False),
                        lambda e, py=py, si=si, p=p: e.matmul(py[0:32, :], ncib[:, p, :], si[:, :], start=False, stop=True)],
                        reads=[crb.b[0], ncib.b[0], sr.b[0], si.b[0]], writes=[py.b[0]])
                    zs = zst[step % 3]
                    ctx.op("act", lambda e, zs=zs, py=py: e.copy(out=zs[:, :], in_=py[0:32, :]), reads=[py.b[0]], writes=[zs.b[0]])
                    ctx.dma("pool", ex["y_loc"][c_blk * 128 + p * 32:c_blk * 128 + (p + 1) * 32, tl:tl + CH], zs[:, :], reads=[zs.b[0]])
                    step += 1
            self.allgather(ex["y_loc"], ex["y_all"], ctx.buf(f"y_all{i}"))
            es_ssm.close()
            cur_es[0] = es
            self.fence_all()

            NQT = L // 512
            qs = T_("qs", [128, L], BF16, nsub=NQT)
            ks = T_("ks", [128, L], BF16, nsub=NQT)
            vs = T_("vs", [128, L // 128, 128], BF16, nsub=NQT)
            lv = T_("lamv", [128, 256])
            cs = T_("cst", [128, 3])
            tr32 = T_("tri32", [128, 128])
            trb = T_("trib", [128, 128], BF16)
            ctx.dma("sp", lv[:, :], self.lamv[i], writes=[lv.b[0]])
            ctx.dma("sp", cs[:, :], self.cst[i], writes=[cs.b[0]])
            ctx.dma("sp", tr32[:, :], self.tri, writes=[tr32.b[0]])
            ctx.op("dve", lambda e: e.tensor_copy(out=trb[:, :], in_=tr32[:, :]), reads=[tr32.b[0]], writes=[trb.b[0]])
            pidx = [0]

            def gather(tile_i):
                c_blk, tl = divmod(tile_i * 512, T)
                for which, dst in ((0, qs), (1, ks)):
                    stg = next_stage()
                    r0 = c_blk * 1024
                    ctx.dma("sp", stg[:, :, :], ex[("q_all", "k_all")[which]][r0:r0 + 1024, tl:tl + 512].rearrange("(h d) t -> d h t", d=128),
                            reads=[b_qkall[which]], writes=[stg.b[0]], semname=stg.b[0].name)
                    pp = ps[pidx[0] % 4]
                    pidx[0] += 1
                    self.select8(stg, lambda jj: selt[:, jj, :], pp[:, :], pp.b[0], 512, selt)
                    ctx.op("dve", lambda e, pp=pp, dst=dst, tile_i=tile_i: e.tensor_copy(out=dst[:, tile_i * 512:(tile_i + 1) * 512], in_=pp[:, :]),
                           reads=[pp.b[0]], writes=[dst.b[tile_i]])
                stg = next_stage()
                for bb in range(4):
                    tb0 = tl + bb * 128
                    ctx.dma("sp", stg[:, :, bb * 128:(bb + 1) * 128],
                            ex["v_all"][c_blk * 8 * T:(c_blk + 1) * 8 * T, :].rearrange("(h t) d -> t h d", h=8)[tb0:tb0 + 128, :, :],
                            reads=[b_vall], writes=[stg.b[0]], semname=stg.b[0].name)
                pp = ps[pidx[0] % 4]
                pidx[0] += 1
                for bb in range(4):
                    fns = []
                    for jj in range(8):
                        fns.append(lambda e, pp=pp, stg=stg, bb=bb, jj=jj: e.matmul(pp[:, bb * 128:(bb + 1) * 128], selt[:, jj, :], stg[:, jj, bb * 128:(bb + 1) * 128],
                                                                             start=(jj == 0), stop=(jj == 7)))
                    ctx.op_multi("pe", fns, reads=[stg.b[0], selt.b[0]], writes=[pp.b[0]])
                ctx.op("dve", lambda e, pp=pp, tile_i=tile_i: e.tensor_copy(out=vs[:, tile_i * 4:(tile_i + 1) * 4, :], in_=pp[:, :].rearrange("t (b d) -> t b d", b=4)),
                       reads=[pp.b[0]], writes=[vs.b[tile_i]])
            gather(0)
            gather(1)
            lt = [T_(f"lt{k}", [128, 64]) for k in range(2)]
            l1 = [T_(f"l1_{k}", [128, 1]) for k in range(6)]
            L1 = lambda k: (l1[k][:, :], [l1[k].b[0]])
            ew.tt("dve", A(lt[0]), (lv[:, 0:64], [lv.b[0]]), (lv[:, 64:128], [lv.b[0]]), ALU.mult)
            ew.tt("dve", A(lt[1]), (lv[:, 128:192], [lv.b[0]]), (lv[:, 192:256], [lv.b[0]]), ALU.mult)
            ctx.op("dve", lambda e: e.reduce_sum(out=l1[0][:, :], in_=lt[0][:, :], axis=mybir.AxisListType.X), reads=[lt[0].b[0]], writes=[l1[0].b[0]])
            ctx.op("dve", lambda e: e.reduce_sum(out=l1[1][:, :], in_=lt[1][:, :], axis=mybir.AxisListType.X), reads=[lt[1].b[0]], writes=[l1[1].b[0]])
            ew.act(L1(2), L1(0), AF.Exp)
            ew.act(L1(3), L1(1), AF.Exp)
            ew.tt("dve", L1(4), L1(3), L1(2), ALU.subtract)
            ew.tt("dve", L1(4), L1(4), (cs[:, 0:1], [cs.b[0]]), ALU.subtract)
            ew.tt("dve", L1(5), (cs[:, 1:2], [cs.b[0]]), (cs[:, 2:3], [cs.b[0]]), ALU.mult)
            nlam, gsc = l1[4], l1[5]
            pS = [[ps[0], ps[1]], [ps[2], ps[3]]]
            pO = [ps[4], ps[5]]
            pL = [ps[6], ps[7]]
            nq = L // 512
            sidx = 0
            for qt in range(0 if "noattn" in DBG else nq):
                q0 = qt * 512
                c_blk, tl = divmod(q0, T)
                nb = 4 * qt + 4
                pend = None

                def pv(item, first, nb=nb):
                    b, col0, P = item
                    for m in range(2):
                        ctx.op_multi("pe", [
                            lambda e, m=m, b=b, col0=col0, P=P, first=first, last=(b == nb - 1): e.matmul(pO[m][:, col0:512], vs[:, b, :], P[m][:, col0:512], start=first, stop=last),
                            lambda e, m=m, b=b, col0=col0, P=P, first=first, last=(b == nb - 1): e.matmul(pL[m][:, col0:512], onesb[:, :], P[m][:, col0:512], start=first, stop=last)],
                            reads=[vs.b[b // 4], onesb.b[0], P[m].b[0]], writes=[pO[m].b[0], pL[m].b[0]])
                for b in range(nb):
                    d = b - 4 * qt
                    col0 = 128 * d if d > 0 else 0
                    P = [tmpb(), tmpb()]
                    for m in range(2):
                        S = pS[m][sidx % 2]
                        lo = 64 * m
                        ctx.op("pe", lambda e, S=S, lo=lo, b=b, col0=col0, q0=q0: e.matmul(
                            S[:, col0:512], ks[lo:lo + 64, b * 128:(b + 1) * 128], qs[lo:lo + 64, q0 + col0:q0 + 512], start=True, stop=True),
                            reads=[ks.b[b // 4], qs.b[qt]], writes=[S.b[0]])
                        ctx.op("act", lambda e, S=S, Pm=P[m], col0=col0: e.activation(out=Pm[:, col0:512], in_=S[:, col0:512], func=AF.Exp),
                               reads=[S.b[0]], writes=[P[m].b[0]])
                        if d >= 0:
                            ctx.op("pool", lambda e, Pm=P[m], col0=col0: e.tensor_tensor(
                                out=Pm[:, col0:col0 + 128], in0=Pm[:, col0:col0 + 128], in1=trb[:, :], op=ALU.mult),
                                reads=[P[m].b[0], trb.b[0]], writes=[P[m].b[0]])
                    sidx += 1
                    if pend is not None:
                        pv(pend, pend[0] == 0)
                    pend = (b, col0, P)
                    if b == nb // 2 and qt + 2 < nq:
                        gather(qt + 2)
                pv(pend, pend[0] == 0)
                r1, r2, oa, ob = tmpf(), tmpf(), tmpf(), tmpf()
                ctx.op("dve", lambda e, r1=r1: e.reciprocal(out=r1[:, :], in_=pL[0][:, :]), reads=[pL[0].b[0]], writes=[r1.b[0]])
                ctx.op("dve", lambda e, r2=r2: e.reciprocal(out=r2[:, :], in_=pL[1][:, :]), reads=[pL[1].b[0]], writes=[r2.b[0]])
                ew.tt("dve", A(oa), A(pO[0]), A(r1), ALU.mult)
                ew.tt("dve", A(ob), A(pO[1]), A(r2), ALU.mult)
                o = tmpf()
                ew.stt(A(o), A(ob), nlam[:, 0:1], A(oa), ALU.mult, ALU.add, sreads=[nlam.b[0]])
                sq = tmpf()
                ew.act(A(sq), A(o), AF.Square)
                S = pS[0][sidx % 2]
                sidx += 1
                ctx.op("pe", lambda e, S=S, sq=sq: e.matmul(S[:, :], ones32[:, :], sq[:, :], start=True, stop=True),
                       reads=[ones32.b[0], sq.b[0]], writes=[S.b[0]])
                vr, sd = tmpf(), tmpf()
                ew.ts("dve", A(vr), A(S), 1.0 / 128.0, ALU.mult, RMS_EPS, ALU.add)
                ew.act(A(sd), A(vr), AF.Sqrt)
                ctx.op("dve", lambda e, vr=vr, sd=sd: e.reciprocal(out=vr[:, :], in_=sd[:, :]), reads=[sd.b[0]], writes=[vr.b[0]])
                yb = tmpb()
                ew.stt(A(yb), A(o), gsc[:, 0:1], A(vr), ALU.mult, ALU.mult, sreads=[gsc.b[0]])
                ctx.dma("pool", ex["a_loc"][c_blk * 128:(c_blk + 1) * 128, tl:tl + 512], yb[:, :], reads=[yb.b[0]])
        self.allgather(ex["a_loc"], ex["a_all"], ctx.buf(f"a_all{i}"))
        self.fence_all()

    def phase_D(self, i):
        import contextlib
        nc, ctx, T = self.nc, self.ctx, self.T
        ex = self.ex[i]
        ctx.tag = f"@D{i}"
        b_yall, b_aall = ctx.buf(f"y_all{i}"), ctx.buf(f"a_all{i}")
        with contextlib.ExitStack() as es:
            ph = TokPhase(nc, ctx, es, None, self.vecD[i], 104, nslab=6, ntf=8)
            zb = Tile(ctx, es, "zb", [128, 8, TT], BF16, nsub=8)
            ab = Tile(ctx, es, "ab", [128, 8, TT], BF16, nsub=8)
            zz = Tile(ctx, es, "zz", [128, 8, TT], BF16, nsub=8)
            p32 = Tile(ctx, es, "p32", [128, 2, TT], F32, nsub=2)
            pb = Tile(ctx, es, "pb", [128, 2, TT], BF16, nsub=2)
            selt = Tile(ctx, es, "selt", [128, 8, 128], BF16)
            stage = Tile(ctx, es, "stage", [128, 8, TT], BF16)
            ctx.dma("sp", selt[:, :, :], self.sel, writes=[selt.b[0]])
            ntile = T // TT
            ph.ws.sched = self.wsched(i, "D", ntile)
            hT, x32, xb = ph.hT, ph.x32, ph.xb
            for ti in range(ntile):
                t0 = ti * TT
                ctx.dma("sp", x32[:, :, :], ex["x1T"][:, t0:t0 + TT].rearrange("(c p) t -> p c t", p=128), writes=list(x32.b), semname="x32ld" + ctx.tag)
                ctx.dma("sp", hT[:, 0:32, :], ex["sgT"][:, t0:t0 + TT].rearrange("(c p) t -> p c t", p=128), writes=list(hT.b[0:32]), semname="sgld" + ctx.tag)
                ctx.dma("sp", p32[:, :, :], self.pT[i][:, t0:t0 + TT].rearrange("(c p) t -> p c t", p=128), writes=list(p32.b), semname="pld" + ctx.tag)
                for c in range(2):
                    ctx.op("pool", lambda e, c=c: e.tensor_copy(out=pb[:, c, :], in_=p32[:, c, :]), reads=[p32.b[c]], writes=[pb.b[c]])
                for o in range(8):
                    ctx.dma("sp", stage[:, :, :], ex["y_all"][o * 1024:(o + 1) * 1024, t0:t0 + TT].rearrange("(c d) t -> d c t", d=128),
                            reads=[b_yall], writes=[stage.b[0]], semname=stage.b[0].name)
                    pp = ph.next_ps()
                    self.select8(stage, lambda jj: selt[:, jj, :], pp[:, :], pp.b[0], TT, selt)
                    u32 = ph.tmpf()
                    ctx.dma("sp", u32[:, :], ex["u32T"][o * 128:(o + 1) * 128, t0:t0 + TT], writes=[u32.b[0]])
                    yv = ph.tmpf()
                    ctx.op("dve", lambda e, yv=yv, u32=u32, pp=pp, o=o: e.scalar_tensor_tensor(
                        out=yv[:, :], in0=u32[:, :], scalar=ph.vec[:, 96 + o:97 + o], in1=pp[:, :], op0=ALU.mult, op1=ALU.add),
                        reads=[u32.b[0], pp.b[0], ph.vec.b[0]], writes=[yv.b[0]])
                    ctx.op("act", lambda e, yv=yv, o=o: e.activation(out=zb[:, o, :], in_=yv[:, :], func=AF.Gelu),
                           reads=[yv.b[0]], writes=[zb.b[o]])
                for o in range(8):
                    ctx.dma("sp", stage[:, :, :], ex["a_all"][o * 1024:(o + 1) * 1024, t0:t0 + TT].rearrange("(c d) t -> d c t", d=128),
                            reads=[b_aall], writes=[stage.b[0]], semname=stage.b[0].name)
                    pp = ph.next_ps()
                    self.select8(stage, lambda jj: selt[:, jj, :], pp[:, :], pp.b[0], TT, selt)
                    ctx.op("act", lambda e, pp=pp, o=o: e.copy(out=ab[:, o, :], in_=pp[:, :]), reads=[pp.b[0]], writes=[ab.b[o]])
                zr = ph.chunks(zb, 0, 8)
                for o in range(8):
                    ps = ph.acc(zr)
                    sg = ph.tmpf()
                    ctx.op("act", lambda e, sg=sg, ps=ps: e.activation(out=sg[:, :], in_=ps[:, :], func=AF.Sigmoid),
                           reads=[ps.b[0]], writes=[sg.b[0]])
                    ctx.op("dve", lambda e, sg=sg, o=o: e.tensor_tensor(out=zz[:, o, :], in0=zb[:, o, :], in1=sg[:, :], op=ALU.mult),
                           reads=[sg.b[0], zb.b[o]], writes=[zz.b[o]])
                zzr = ph.chunks(zz, 0, 8)
                ar = ph.chunks(ab, 0, 8)
                for o in range(KC):
                    pa = ph.acc(zzr)
                    pbb = ph.acc(ar)
                    t1 = ph.tmpf()
                    t2 = ph.tmpf()
                    ctx.op("dve", lambda e, t1=t1, pa=pa, o=o: e.tensor_tensor(out=t1[:, :], in0=pa[:, :], in1=hT[:, o, :], op=ALU.mult),
                           reads=[pa.b[0], hT.b[o]], writes=[t1.b[0]])
                    ctx.op("dve", lambda e, t2=t2, pbb=pbb, o=o: e.tensor_tensor(out=t2[:, :], in0=pbb[:, :], in1=hT[:, 16 + o, :], op=ALU.mult),
                           reads=[pbb.b[0], hT.b[16 + o]], writes=[t2.b[0]])
                    ctx.op("pool", lambda e, t1=t1, t2=t2, o=o: e.tensor_tensor(out=xb[:, o, :], in0=t1[:, :], in1=t2[:, :], op=ALU.add),
                           reads=[t1.b[0], t2.b[0]], writes=[xb.b[o]])
                mr = ph.chunks(xb, 0, KC)
                for o in range(KC):
                    ps = ph.acc(mr)
                    ctx.op("dve", lambda e, ps=ps, o=o: e.scalar_tensor_tensor(
                        out=x32[:, o, :], in0=x32[:, o, :], scalar=ALPHA, in1=ps[:, :], op0=ALU.mult, op1=ALU.add),
                        reads=[ps.b[0], x32.b[o]], writes=[x32.b[o]])
                ph.layernorm(0, 16)
                ph.ffn()
                ph.layernorm(32, 48)
                xr = ph.chunks(xb, 0, KC)
                pr = ph.chunks(pb, 0, 2)
                for o in range(KC):
                    pg = ph.acc(xr)
                    pp = ph.acc(pr)
                    sg = ph.tmpf()
                    t1 = ph.tmpf()
                    ctx.op("act", lambda e, sg=sg, pg=pg: e.activation(out=sg[:, :], in_=pg[:, :], func=AF.Sigmoid),
                           reads=[pg.b[0]], writes=[sg.b[0]])
                    ctx.op("dve", lambda e, sg=sg, pp=pp, t1=t1: e.tensor_tensor(out=t1[:, :], in0=pp[:, :], in1=sg[:, :], op=ALU.mult),
                           reads=[sg.b[0], pp.b[0]], writes=[t1.b[0]])
                    ctx.op("dve", lambda e, t1=t1, o=o: e.scalar_tensor_tensor(
                        out=x32[:, o, :], in0=x32[:, o, :], scalar=ALPHA, in1=t1[:, :], op0=ALU.mult, op1=ALU.add),
                        reads=[t1.b[0], x32.b[o]], writes=[x32.b[o]])
                ph.layernorm(64, 80)
                ph.store_x(ex["x4T"], t0, is_output=(i + 1 == DEPTH))
        self.fence_all()

    def build(self, upto=99):
        keys = [(0, "A", 0), (0, "A", 1), (0, "D", 0), (0, "D", 1), (1, "A", 0), (1, "A", 1), (1, "D", 0), (1, "D", 1)]
        now = [(0, "A", 0), (0, "A", 1)]
        self.deferred_ag = {0: [(0, "D", 0), (0, "D", 1), (1, "A", 0), (1, "A", 1)], 1: [(1, "D", 0), (1, "D", 1)]}
        self.phase_weights(keys, ag_now=now)
        n = 0
        for i in range(DEPTH):
            for ph in (self.phase_A, self.phase_C, self.phase_D):
                n += 1
                if n <= upto:
                    ph(i)
        if upto < 6:
            self.ctx.dma("sp", self.out[0:128, 0:512], self.x_in[0:128, 0:512], is_output=True, semname="dbgout",
                         reads=[self.ctx.buf("dbg_in")], writes=[self.ctx.buf("dbg_out")])
        self.ctx.emit()
        return self.nc


def kernel_fused(**inp):
    fz = Fused()
    nc = fz.build()
    maps = fused_maps(fz, inp)
    res = _run(nc, maps)
    out = np.concatenate([np.ascontiguousarray(res[c]["outT"].T) for c in range(NCORES)], axis=0)[None]
    return out.astype(np.float32)


def fused_maps(fz, inp):
    inp = {k: np.asarray(v) for k, v in inp.items()}
    L, T = SEQ, SEQ // NCORES
    f32 = np.float32
    perm = rope_perm(); invf = rope_invf()[:, None]; tri = tri_const(); iota = iota_const()
    pos = inp["positions"][0].astype(np.int32)
    maps = [dict() for _ in range(NCORES)]
    for i in range(DEPTH):
        ua = la_units_fused(inp["ffn1_w_gate"][i], inp["ffn1_w_up"][i], inp["ffn1_w_down"][i], inp["w_in"][i])
        ud = ld_weights(inp["ssm_w_glu"][i], inp["w_branch_ssm"][i], inp["w_branch_attn"][i], inp["w_out"][i],
                        inp["ffn2_w_gate"][i], inp["ffn2_w_up"][i], inp["ffn2_w_down"][i],
                        inp["ple_w_gate"][i], inp["ple_w_proj"][i])
        for g, units, locs, per in (("A", ua, fz.locA, fz.perA), ("D", ud, fz.locD, fz.perD)):
            arrs = [np.zeros((128, per), f32) for _ in range(NPART)]
            for u_, (p, off, n) in zip(units, locs):
                arrs[p][:, off:off + n] = u_
            for p in range(NPART):
                hf, r = divmod(p, 8)
                maps[r][f"w{g}{i}_{hf}"] = arrs[p]
        del ua, ud
    for c in range(NCORES):
        m = maps[c]
        m["xT"] = np.ascontiguousarray(inp["x"][0, c * T:(c + 1) * T, :].T)
        m["posi"] = np.ascontiguousarray(np.broadcast_to(pos[None, c * T:(c + 1) * T], (128, T)))
        m["perm"] = perm
        sel = np.zeros((128, 8, 128), f32)
        sel[:, c, :] = np.eye(128, dtype=f32)
        m["sel"] = sel.astype(NPBF)
        m["tri"] = tri
        m["iota"] = iota
        for i in range(DEPTH):
            lam_init = 0.8 - 0.6 * math.exp(-0.3 * i)
            m[f"vecA{i}"] = np.concatenate([colvec(inp["ln1_g"][i]), colvec(inp["ln1_b"][i]), invf], axis=1).astype(f32)
            m[f"vecD{i}"] = np.concatenate([colvec(inp["ln2_g"][i]), colvec(inp["ln2_b"][i]), colvec(inp["ln3_g"][i]), colvec(inp["ln3_b"][i]),
                                            colvec(inp["ln4_g"][i]), colvec(inp["ln4_b"][i]), colvec(inp["ssm_d"][i])], axis=1).astype(f32)
            m[f"pT{i}"] = np.ascontiguousarray(inp["p"][i, 0, c * T:(c + 1) * T, :].T)
            lamv = np.concatenate([inp["lambda_q1"][i], inp["lambda_k1"][i], inp["lambda_q2"][i], inp["lambda_k2"][i]]).astype(f32)
            m[f"lamv{i}"] = np.ascontiguousarray(np.broadcast_to(lamv[None, :], (128, 256)))
            m[f"cst{i}"] = np.stack([np.full(128, lam_init, f32), np.full(128, 1.0 - lam_init, f32), inp["attn_subln_g"][i].astype(f32)], axis=1)
            prow, pcol, ccol, dsk = ssm_params(c, inp["ssm_a_re"][i], inp["ssm_a_im"][i], inp["ssm_b_re"][i], inp["ssm_b_im"][i],
                                               inp["ssm_c_re"][i], inp["ssm_c_im"][i], inp["ssm_log_dt"][i], inp["ssm_d"][i])
            m[f"prow{i}"] = prow; m[f"pcol{i}"] = pcol; m[f"ccol{i}"] = ccol
    return maps


kernel_unfused = kernel
kernel = kernel_fused
```
